# Optimizing a Trainium2 kernel written in Bass

```python
import math
import jax, jax.numpy as jnp
from jax import lax
import numpy as np

D_MODEL = 1024
BATCH = 32
SEQ = 2048
DEPTH = 1
DEC_BATCH = 8
DEC_SEQ = 2048
PAST_LEN = 128

ATTN_WIDTH = D_MODEL // 2
HYENA_WIDTH = D_MODEL - ATTN_WIDTH
HEAD_DIM = 64
N_HEADS = ATTN_WIDTH // HEAD_DIM
DILATED_BRANCHES = ((128, 1), (512, 4), (2048, 16))
ATTN_BLOCK = 64
ROPE_THETA = 10000.0
HYENA_ORDER = 2
SHORT_CONV = 3
FILTER_EMB = 33
FILTER_HIDDEN = 64
FILTER_OUT = HYENA_ORDER * 2 * HYENA_WIDTH
DECAY_TARGET = 1e-2
FAST_DECAY_PCT = 0.3
SLOW_DECAY_PCT = 1.5
D_FF = 4 * D_MODEL
IN_WIDTH = 3 * ATTN_WIDTH + 3 * HYENA_WIDTH
EPS = 1e-6
NEG_INF = -1e30

kernel_name = "hybrid_dilated_attn_hyena_encoder"


def rmsnorm(x, g):
    xf = x.astype(jnp.float32)
    y = xf * lax.rsqrt(jnp.mean(xf * xf, axis=-1, keepdims=True) + EPS) * g.astype(jnp.float32)
    return y.astype(x.dtype)


def rmsnorm_f32(x, g):
    return x * lax.rsqrt(jnp.mean(x * x, axis=-1, keepdims=True) + EPS) * g.astype(jnp.float32)


def rope(x):
    S, E = x.shape[1], x.shape[3]
    half = E // 2
    inv_freq = ROPE_THETA ** (-jnp.arange(half, dtype=jnp.float32) / half)
    ang = jnp.arange(S, dtype=jnp.float32)[:, None] * inv_freq[None, :]
    c = jnp.cos(ang)[None, :, None, :]
    s = jnp.sin(ang)[None, :, None, :]
    x1, x2 = x[..., :half], x[..., half:]
    return jnp.concatenate([x1 * c - x2 * s, x2 * c + x1 * s], axis=-1)


def window_attention(q, k, v, half):
    N, L, H, E = q.shape
    blk = ATTN_BLOCK
    nb = -(-L // blk)
    Lp = nb * blk
    W = blk + 2 * half
    qb = jnp.pad(q, ((0, 0), (0, Lp - L), (0, 0), (0, 0))).reshape(N, nb, blk, H, E)
    pad_kv = ((0, 0), (half, Lp - L + half), (0, 0), (0, 0))
    kp = jnp.pad(k, pad_kv)
    vp = jnp.pad(v, pad_kv)
    idx = (jnp.arange(nb) * blk)[:, None] + jnp.arange(W)[None, :]
    kb = kp[:, idx]
    vb = vp[:, idx]
    s = jnp.einsum('nbqhe,nbkhe->nbhqk', qb, kb) * (1.0 / math.sqrt(E))
    qpos = (jnp.arange(nb) * blk)[:, None] + jnp.arange(blk)[None, :]
    kpos = idx - half
    rel = kpos[:, None, :] - qpos[:, :, None]
    valid = (jnp.abs(rel) <= half) & (kpos >= 0)[:, None, :] & (kpos < L)[:, None, :]
    s = jnp.where(valid[None, :, None, :, :], s, NEG_INF)
    lse = jax.nn.logsumexp(s, axis=-1)
    p = jnp.exp(s - lse[..., None])
    out = jnp.einsum('nbhqk,nbkhe->nbqhe', p, vb).reshape(N, Lp, H, E)[:, :L]
    lse = lse.transpose(0, 1, 3, 2).reshape(N, Lp, H)[:, :L]
    return out, lse


def dilated_branch(q, k, v, window, dilation):
    B, S, H, E = q.shape
    L = S // dilation
    half = window // (2 * dilation)

    def to_res(t):
        return t.reshape(B, L, dilation, H, E).transpose(0, 2, 1, 3, 4).reshape(B * dilation, L, H, E)

    out, lse = window_attention(to_res(q), to_res(k), to_res(v), half)
    out = out.reshape(B, dilation, L, H, E).transpose(0, 2, 1, 3, 4).reshape(B, S, H, E)
    lse = lse.reshape(B, dilation, L, H).transpose(0, 2, 1, 3).reshape(B, S, H)
    return out, lse


def dilated_attention(q, k, v):
    outs, lses = [], []
    for window, dilation in DILATED_BRANCHES:
        o, l = dilated_branch(q, k, v, window, dilation)
        outs.append(o)
        lses.append(l)
    wts = jax.nn.softmax(jnp.stack(lses, axis=0), axis=0)
    return jnp.sum(wts[..., None] * jnp.stack(outs, axis=0), axis=0)


def short_conv(u, w, b):
    pad = SHORT_CONV // 2
    S = u.shape[1]
    up = jnp.pad(u, ((0, 0), (pad, SHORT_CONV - 1 - pad), (0, 0)))
    y = b
    for t in range(SHORT_CONV):
        y = y + up[:, t:t + S] * w[t]
    return y


def implicit_filter_spectra(L, w1, b1, f1, w2, b2, f2, w3, b3):
    pos = jnp.arange(L, dtype=jnp.float32)
    t = pos / max(L - 1, 1)
    bands = (FILTER_EMB - 1) // 2
    fr = jnp.linspace(1e-4, bands - 1, bands, dtype=jnp.float32)
    ang = 2.0 * math.pi * pos[:, None] * fr[None, :] / L
    feat = jnp.concatenate([t[:, None], jnp.cos(ang), -jnp.sin(ang)], axis=-1)
    h = jnp.sin(f1 * (feat @ w1 + b1))
    h = jnp.sin(f2 * (h @ w2 + b2))
    h = (h @ w3 + b3).reshape(L, HYENA_ORDER, 2, HYENA_WIDTH)
    deltas = jnp.linspace(math.log(DECAY_TARGET) / FAST_DECAY_PCT,
                          math.log(DECAY_TARGET) / SLOW_DECAY_PCT, HYENA_WIDTH, dtype=jnp.float32)
    decay = jnp.exp(-t[:, None] * jnp.abs(deltas)[None, :])
    h = h * decay[:, None, None, :]
    spec = jnp.fft.rfft(h, n=2 * L, axis=0)
    return spec[:, :, 0] + jnp.conj(spec[:, :, 1])


def long_conv(z, Hn):
    L = z.shape[1]
    Z = jnp.fft.rfft(z, n=2 * L, axis=1)
    return jnp.fft.irfft(Z * Hn[None], n=2 * L, axis=1)[:, :L]


def hyena_mixer(u, conv_w, conv_b, skip, spectra):
    u = short_conv(u, conv_w, conv_b)
    z, x1, x2 = jnp.split(u, 3, axis=-1)
    gates = (x1, x2)
    for n in range(HYENA_ORDER):
        z = gates[n] * (long_conv(z, spectra[:, n]) + skip[n] * z)
    return z


def encoder_layer(x, mix_norm, w_in, q_norm, k_norm, hy_conv_w, hy_conv_b,
                  flt_w1, flt_b1, flt_freq1, flt_w2, flt_b2, flt_freq2, flt_w3, flt_b3,
                  hy_skip, attn_out_norm, hy_out_norm, w_out, ffn_norm, w_up, w_down):
    B, S, _ = x.shape
    f32 = jnp.float32
    xn = rmsnorm(x, mix_norm)
    proj = xn @ w_in
    a = ATTN_WIDTH
    q = proj[..., 0:a].astype(f32).reshape(B, S, N_HEADS, HEAD_DIM)
    k = proj[..., a:2 * a].astype(f32).reshape(B, S, N_HEADS, HEAD_DIM)
    v = proj[..., 2 * a:3 * a].astype(f32).reshape(B, S, N_HEADS, HEAD_DIM)
    u = proj[..., 3 * a:].astype(f32)
    q = rope(rmsnorm_f32(q, q_norm))
    k = rope(rmsnorm_f32(k, k_norm))
    attn = dilated_attention(q, k, v).reshape(B, S, ATTN_WIDTH)

    spectra = implicit_filter_spectra(S, flt_w1.astype(f32), flt_b1.astype(f32), flt_freq1.astype(f32),
                                      flt_w2.astype(f32), flt_b2.astype(f32), flt_freq2.astype(f32),
                                      flt_w3.astype(f32), flt_b3.astype(f32))
    hy = hyena_mixer(u, hy_conv_w.astype(f32), hy_conv_b.astype(f32), hy_skip.astype(f32), spectra)

    mixed = jnp.concatenate([rmsnorm_f32(attn, attn_out_norm), rmsnorm_f32(hy, hy_out_norm)], axis=-1)
    h = x + mixed.astype(x.dtype) @ w_out

    hn = rmsnorm(h, ffn_norm)
    ff = jnp.square(jax.nn.relu(hn @ w_up))
    return h + ff @ w_down


def trunk(x, mix_norm, w_in, q_norm, k_norm, hy_conv_w, hy_conv_b,
          flt_w1, flt_b1, flt_freq1, flt_w2, flt_b2, flt_freq2, flt_w3, flt_b3,
          hy_skip, attn_out_norm, hy_out_norm, w_out, ffn_norm, w_up, w_down):
    for i in range(DEPTH):
        x = encoder_layer(x, mix_norm[i], w_in[i], q_norm[i], k_norm[i], hy_conv_w[i], hy_conv_b[i],
                          flt_w1[i], flt_b1[i], flt_freq1[i], flt_w2[i], flt_b2[i], flt_freq2[i],
                          flt_w3[i], flt_b3[i], hy_skip[i], attn_out_norm[i], hy_out_norm[i],
                          w_out[i], ffn_norm[i], w_up[i], w_down[i])
    return x


def setup_inputs(seed: int = 0) -> dict:
    key = jax.random.key(seed)
    ks = jax.random.split(key, 24)
    f32 = jnp.float32

    def nrm(k, shape, scale):
        return jax.random.normal(k, shape, f32) * scale

    def gain(k, shape):
        return 1.0 + 0.02 * jax.random.normal(k, shape, f32)

    C = HYENA_WIDTH
    return {
        "x_prompt": nrm(ks[0], (BATCH, SEQ, D_MODEL), 1.0),
        "x_sample": nrm(ks[1], (DEC_BATCH, DEC_SEQ, D_MODEL), 1.0),
        "mix_norm": gain(ks[2], (DEPTH, D_MODEL)),
        "w_in": nrm(ks[3], (DEPTH, D_MODEL, IN_WIDTH), D_MODEL ** -0.5),
        "q_norm": gain(ks[4], (DEPTH, HEAD_DIM)),
        "k_norm": gain(ks[5], (DEPTH, HEAD_DIM)),
        "hy_conv_w": nrm(ks[6], (DEPTH, SHORT_CONV, 3 * C), SHORT_CONV ** -0.5),
        "hy_conv_b": nrm(ks[7], (DEPTH, 3 * C), 0.02),
        "flt_w1": nrm(ks[8], (DEPTH, FILTER_EMB, FILTER_HIDDEN), FILTER_EMB ** -0.5),
        "flt_b1": nrm(ks[9], (DEPTH, FILTER_HIDDEN), 0.02),
        "flt_freq1": gain(ks[10], (DEPTH, FILTER_HIDDEN)),
        "flt_w2": nrm(ks[11], (DEPTH, FILTER_HIDDEN, FILTER_HIDDEN), FILTER_HIDDEN ** -0.5),
        "flt_b2": nrm(ks[12], (DEPTH, FILTER_HIDDEN), 0.02),
        "flt_freq2": gain(ks[13], (DEPTH, FILTER_HIDDEN)),
        "flt_w3": nrm(ks[14], (DEPTH, FILTER_HIDDEN, FILTER_OUT), FILTER_HIDDEN ** -0.5),
        "flt_b3": nrm(ks[15], (DEPTH, FILTER_OUT), 0.02),
        "hy_skip": nrm(ks[16], (DEPTH, HYENA_ORDER, C), 0.5),
        "attn_out_norm": gain(ks[17], (DEPTH, ATTN_WIDTH)),
        "hy_out_norm": gain(ks[18], (DEPTH, HYENA_WIDTH)),
        "w_out": nrm(ks[19], (DEPTH, D_MODEL, D_MODEL), D_MODEL ** -0.5),
        "ffn_norm": gain(ks[20], (DEPTH, D_MODEL)),
        "w_up": nrm(ks[21], (DEPTH, D_MODEL, D_FF), D_MODEL ** -0.5),
        "w_down": nrm(ks[22], (DEPTH, D_FF, D_MODEL), D_FF ** -0.5),
    }


def reference(x_prompt, x_sample, mix_norm, w_in, q_norm, k_norm, hy_conv_w, hy_conv_b,
              flt_w1, flt_b1, flt_freq1, flt_w2, flt_b2, flt_freq2, flt_w3, flt_b3,
              hy_skip, attn_out_norm, hy_out_norm, w_out, ffn_norm, w_up, w_down):
    y_prompt = trunk(x_prompt, mix_norm, w_in, q_norm, k_norm, hy_conv_w, hy_conv_b,
                     flt_w1, flt_b1, flt_freq1, flt_w2, flt_b2, flt_freq2, flt_w3, flt_b3,
                     hy_skip, attn_out_norm, hy_out_norm, w_out, ffn_norm, w_up, w_down)
    y_sample = trunk(x_sample, mix_norm, w_in, q_norm, k_norm, hy_conv_w, hy_conv_b,
                     flt_w1, flt_b1, flt_freq1, flt_w2, flt_b2, flt_freq2, flt_w3, flt_b3,
                     hy_skip, attn_out_norm, hy_out_norm, w_out, ffn_norm, w_up, w_down)
    return (y_prompt, y_sample)
```

```python
import contextlib
import numpy as np
import ml_dtypes
import concourse.bass as bass
import concourse.mybir as mybir
from concourse.bass_utils import run_bass_kernel_spmd

F32 = mybir.dt.float32
BF16 = mybir.dt.bfloat16
AF = mybir.ActivationFunctionType
ALU = mybir.AluOpType
NCORES = 8
S = 2048
D = 1024
EPS = 1e-6

ENGS = ("pe", "act", "dve", "pool", "sp")


def _interval(ap):
    pat = ap.ap
    name = ap.tensor.name
    esz = mybir.dt.size(ap.dtype)
    sp = str(ap.space).upper()
    off = ap.offset
    if "SB" in sp or "PSUM" in sp:
        row = pat[0][0]
        p0 = off // row
        lo = off - p0 * row
        ext = 0
        for st, n in pat[1:]:
            ext += abs(st) * (n - 1)
        if "PSUM" in sp:
            return name, 0, 2048, (p0 // 32) * 32, ((p0 + pat[0][1] + 31) // 32) * 32
        return name, lo * esz, (lo + ext + 1) * esz, p0, p0 + pat[0][1]
    ext = 0
    for st, n in pat:
        ext += abs(st) * (n - 1)
    return name, off * esz, (off + ext + 1) * esz, 0, 1


class Sched:
    def __init__(self, nc, n_dma_sems=8):
        self.nc = nc
        self.ops = []
        self.streams = {e: [] for e in ENGS}
        self.wr = {}
        self.rd = {}
        self.K = n_dma_sems
        self.notrack = set()
        self.tag = "setup"

    def _deps_for(self, opid, reads, writes):
        deps = set()
        ri = [_interval(a) for a in reads]
        wi = [_interval(a) for a in writes]
        for key, lo, hi, plo, phi in ri:
            if key in self.notrack:
                continue
            for (l, h, pl, ph, w) in self.wr.get(key, ()):
                if l < hi and lo < h and pl < phi and plo < ph:
                    deps.add(w)
        for key, lo, hi, plo, phi in wi:
            if key in self.notrack:
                continue
            for (l, h, pl, ph, w) in self.wr.get(key, ()):
                if l < hi and lo < h and pl < phi and plo < ph:
                    deps.add(w)
            for (l, h, pl, ph, r) in self.rd.get(key, ()):
                if l < hi and lo < h and pl < phi and plo < ph:
                    deps.add(r)
        deps.discard(opid)
        for key, lo, hi, plo, phi in wi:
            if key in self.notrack:
                continue
            wl = self.wr.setdefault(key, [])
            wl[:] = [x for x in wl if not (lo <= x[0] and x[1] <= hi and plo <= x[2] and x[3] <= phi)]
            wl.append((lo, hi, plo, phi, opid))
            rl = self.rd.setdefault(key, [])
            rl[:] = [x for x in rl if not (lo <= x[0] and x[1] <= hi and plo <= x[2] and x[3] <= phi)]
        eng = self.ops[opid]["eng"]
        isdma = self.ops[opid]["dma"]
        for key, lo, hi, plo, phi in ri:
            if key in self.notrack:
                continue
            rl = self.rd.setdefault(key, [])
            if not isdma:
                rl[:] = [x for x in rl if not (x[0] == lo and x[1] == hi and x[2] == plo and x[3] == phi
                                               and self.ops[x[4]]["eng"] == eng and not self.ops[x[4]]["dma"])]
            rl.append((lo, hi, plo, phi, opid))
        return deps

    def op(self, eng, fn, reads=(), writes=(), dma=False):
        opid = len(self.ops)
        rec = dict(eng=eng, fn=fn, deps=None, dma=dma, tag=self.tag)
        self.ops.append(rec)
        rec["deps"] = self._deps_for(opid, list(reads), list(writes))
        self.streams[eng].append(opid)
        return opid

    def dma(self, out, in_, q="sp"):
        return self.op(q, lambda e: e.dma_start(out=out, in_=in_), reads=[in_], writes=[out], dma=True)

    def mm(self, out, lhsT, rhs, start=True, stop=True):
        return self.op("pe", lambda e: e.matmul(out, lhsT, rhs, start=start, stop=stop, skip_group_check=True),
                       reads=[lhsT, rhs], writes=[out])

    def tr(self, out, in_, ident):
        return self.op("pe", lambda e: e.transpose(out, in_, ident), reads=[in_, ident], writes=[out])

    def emit(self):
        nc = self.nc
        ops = self.ops
        needed = set()
        for o in ops:
            for d in o["deps"]:
                if not (o["eng"] == "pe" and ops[d]["eng"] == "pe"):
                    needed.add(d)
        cnt = {e: 0 for e in ENGS}
        dcnt = {e: 0 for e in ENGS}
        for e in ENGS:
            for i in self.streams[e]:
                o = ops[i]
                o["thr"] = None
                if o["dma"]:
                    k = dcnt[e]
                    dcnt[e] += 1
                    o["sem"] = ("d", e, k % self.K)
                    o["val"] = 16 * (k // self.K + 1)
                    if k >= self.K:
                        o["thr"] = (("d", e, k % self.K), 16 * (k // self.K))
                elif i in needed:
                    cnt[e] += 1
                    o["sem"] = ("c", e, 0)
                    o["val"] = cnt[e]
                else:
                    o["sem"] = None
        with contextlib.ExitStack() as st:
            sems = {}
            for e in ENGS:
                sems[("c", e, 0)] = st.enter_context(nc.semaphore(f"c_{e}"))
                if dcnt[e]:
                    for k in range(self.K):
                        sems[("d", e, k)] = st.enter_context(nc.semaphore(f"d_{e}{k}"))
            block = st.enter_context(nc.Block())

            def run(e, engobj):
                waited = {}
                for i in self.streams[e]:
                    o = ops[i]
                    req = {}
                    for d in o["deps"]:
                        od = ops[d]
                        if od["eng"] == "pe" and e == "pe":
                            continue
                        s = od["sem"]
                        if s is not None and req.get(s, 0) < od["val"]:
                            req[s] = od["val"]
                    if o["thr"] is not None:
                        s, v = o["thr"]
                        if req.get(s, 0) < v:
                            req[s] = v
                    for s, v in req.items():
                        if waited.get(s, 0) < v:
                            engobj.wait_ge(sems[s], v)
                            waited[s] = v
                    ins = o["fn"](engobj)
                    if o["sem"] is not None:
                        ins.then_inc(sems[o["sem"]], 16 if o["dma"] else 1)
                if dcnt[e]:
                    for k in range(self.K):
                        n = len(range(k, dcnt[e], self.K))
                        if n:
                            engobj.wait_ge(sems[("d", e, k)], 16 * n)

            @block.tensor
            def _(eng):
                run("pe", eng)

            @block.scalar
            def _(eng):
                run("act", eng)

            @block.vector
            def _(eng):
                run("dve", eng)

            @block.gpsimd
            def _(eng):
                run("pool", eng)

            @block.sync
            def _(eng):
                run("sp", eng)


_CONST = None


def make_consts():
    global _CONST
    if _CONST is not None:
        return _CONST
    bf = ml_dtypes.bfloat16
    c = {}
    c["ident"] = np.eye(128, dtype=np.float32).astype(bf)
    bm = np.zeros((128, 128), np.float32)
    bm[:64, :64] = 1.0 / 64
    bm[64:, 64:] = 1.0 / 64
    c["bm"] = bm.astype(bf)
    r0 = np.zeros((128, 128), np.float32)
    for m in range(128):
        hb, e = (m // 64) * 64, m % 64
        if e < 32:
            r0[hb + e + 32, m] = -1.0
        else:
            r0[hb + e - 32, m] = 1.0
    c["r0"] = r0
    half = 32
    inv_freq = (np.float32(10000.0) ** (-(np.arange(half, dtype=np.float32) / np.float32(half)))).astype(np.float32)
    ang = (np.arange(S, dtype=np.float32)[:, None] * inv_freq[None, :]).astype(np.float32)
    cs = np.cos(ang.astype(np.float64)).astype(np.float32).T
    sn = np.sin(ang.astype(np.float64)).astype(np.float32).T
    c["cosT"] = np.tile(cs, (4, 1))
    c["sinT"] = np.tile(sn, (4, 1)).astype(bf)
    kk = np.arange(128)[:, None]
    qq = np.arange(128)[None, :]
    m1 = (qq <= kk).astype(np.float32)
    m2 = (qq >= kk).astype(np.float32)
    m1e = np.zeros((128, 128), np.float32)
    m1e[:64] = m1[64:]
    c["maskG"] = np.concatenate([m1, m2, m1, m2], 1).astype(bf)
    c["maskA"] = np.concatenate([m1e, m2, m1, m2], 1).astype(bf)
    c["maskB"] = np.concatenate([m1e, m2, m1e, m2], 1).astype(bf)
    mb = (np.abs(qq - kk) <= 64).astype(np.float32)
    mb3 = np.zeros((128, 4, 16, 32), np.float32)
    for cc in range(4):
        mb3[:, cc, :, :] = mb[:, None, 32 * cc:32 * cc + 32]
    c["mb3"] = mb3.reshape(128, 4 * 512).astype(bf)
    par = np.arange(2, dtype=np.int64)[:, None, None]
    jj = np.arange(8, dtype=np.int64)[None, :, None]
    pp = np.arange(128, dtype=np.int64)[None, None, :]
    tok = par + 256 * jj + 2 * pp
    c["tok"] = tok
    k = np.arange(1024, dtype=np.int64)
    m = ((2 * k[None, None, None, :] + 1) * tok[..., None]) % 8192
    th = m.astype(np.float64) * (2.0 * np.pi / 8192.0)
    C = np.cos(th).reshape(2, 8, 128, 8, 128)
    Sn = np.sin(th).reshape(2, 8, 128, 8, 128)
    cf = C.transpose(3, 2, 0, 1, 4)
    sf = (-Sn).transpose(3, 2, 0, 1, 4)
    c["fwd"] = np.ascontiguousarray(np.stack([cf, sf], 3)).astype(bf)
    sc = 2.0 / 4096.0
    ci = (sc * C).transpose(0, 1, 4, 3, 2).reshape(16, 128, 8, 128)
    si = (-sc * Sn).transpose(0, 1, 4, 3, 2).reshape(16, 128, 8, 128)
    c["inv"] = np.ascontiguousarray(np.stack([ci, si], 2)).astype(bf)
    L = S
    pos = np.arange(L, dtype=np.float32)
    tt = (pos / np.float32(L - 1)).astype(np.float32)
    bands = 16
    fr = np.linspace(1e-4, bands - 1, bands, dtype=np.float32)
    angf = (np.float32(2.0 * np.pi) * pos[:, None] * fr[None, :] / np.float32(L)).astype(np.float32)
    feat = np.concatenate([tt[:, None], np.cos(angf.astype(np.float64)).astype(np.float32),
                           -np.sin(angf.astype(np.float64)).astype(np.float32)], -1)
    c["featT"] = np.ascontiguousarray(feat.T).astype(np.float32)
    deltas = np.linspace(np.log(1e-2) / 0.3, np.log(1e-2) / 1.5, 512, dtype=np.float32)
    decay = np.exp(-(tt[:, None] * np.abs(deltas)[None, :]).astype(np.float32).astype(np.float64)).astype(np.float32)
    c["decay"] = np.ascontiguousarray(decay[tok.reshape(16, 128)].transpose(1, 0, 2))
    del c["tok"]
    _CONST = c
    return c


CONST_SPECS = [("ident", [128, 128], BF16), ("bm", [128, 128], BF16), ("r0", [128, 128], F32),
               ("cosT", [128, S], F32), ("sinT", [128, S], BF16),
               ("maskG", [128, 512], BF16), ("maskA", [128, 512], BF16), ("maskB", [128, 512], BF16),
               ("mb3", [128, 2048], BF16), ("fwd", [8, 128, 2, 2, 8, 128], BF16),
               ("inv", [16, 128, 2, 8, 128], BF16), ("featT", [33, S], F32), ("decay", [128, 16, 512], F32)]

PARAM_SPECS = [("w_in", [D, 3072]), ("w_out", [D, D]), ("w_up", [D, 4096]), ("w_down", [4096, D]),
               ("g_mix", [128, 8]), ("g_ffn", [128, 8]), ("g_out", [128, 8]), ("g_qk", [128, 2]),
               ("convw", [128, 48]), ("flt_w1", [33, 64]), ("flt_w2", [64, 64]), ("flt_w3", [64, 2048]),
               ("flt_v", [64, 4]), ("flt_b3", [1, 2048]), ("hy_skip", [1, 1024])]


def layout_params(p):
    f = lambda a: np.ascontiguousarray(np.asarray(a, dtype=np.float32))
    out = {}
    out["w_in"] = f(p["w_in"][0])
    out["w_out"] = f(p["w_out"][0])
    out["w_up"] = f(p["w_up"][0])
    out["w_down"] = f(p["w_down"][0])
    out["g_mix"] = f(np.asarray(p["mix_norm"][0]).reshape(8, 128).T)
    out["g_ffn"] = f(np.asarray(p["ffn_norm"][0]).reshape(8, 128).T)
    gcat = np.concatenate([np.asarray(p["attn_out_norm"][0]), np.asarray(p["hy_out_norm"][0])])
    out["g_out"] = f(gcat.reshape(8, 128).T)
    gq = np.tile(np.asarray(p["q_norm"][0]), 2)
    gk = np.tile(np.asarray(p["k_norm"][0]), 2)
    out["g_qk"] = f(np.stack([gq, gk], 1))
    cw = np.asarray(p["hy_conv_w"][0])
    cb = np.asarray(p["hy_conv_b"][0])
    cwb = np.concatenate([cw, cb[None, :]], 0)
    out["convw"] = f(cwb.reshape(4, 12, 128).transpose(2, 1, 0).reshape(128, 48))
    out["flt_w1"] = f(p["flt_w1"][0])
    out["flt_w2"] = f(p["flt_w2"][0])
    out["flt_w3"] = f(p["flt_w3"][0])
    out["flt_v"] = f(np.stack([np.asarray(p["flt_b1"][0]), np.asarray(p["flt_freq1"][0]),
                               np.asarray(p["flt_b2"][0]), np.asarray(p["flt_freq2"][0])], 1))
    out["flt_b3"] = f(np.asarray(p["flt_b3"][0])[None, :])
    out["hy_skip"] = f(np.asarray(p["hy_skip"][0]).reshape(1, 1024))
    return out


ARENA_BYTES = 180 * 1024


def build_program(NS, dbg=None):
    nc = bass.Bass("TRN2", target_bir_lowering=False)
    x_d = nc.dram_tensor("x", [NS, S, D], F32, kind="ExternalInput").ap()
    y_d = nc.dram_tensor("y", [NS, S, D], F32, kind="ExternalOutput").ap()
    cd = {n: nc.dram_tensor(n, shp, dt, kind="ExternalInput").ap() for n, shp, dt in CONST_SPECS}
    pd = {n: nc.dram_tensor(n, shp, F32, kind="ExternalInput").ap() for n, shp in PARAM_SPECS}
    win_s = nc.dram_tensor("win_s", [24, 128, 8, 128], BF16).ap()
    wup_s = nc.dram_tensor("wup_s", [32, 128, 8, 128], BF16).ap()
    wdn_s = nc.dram_tensor("wdn_s", [32, 128, 1024], BF16).ap()
    ksp_s = nc.dram_tensor("ksp_s", [2, 8, 128, 4, 512], BF16).ap()
    rope_s = nc.dram_tensor("rope_s", [2, 128, S], BF16).ap()
    dbg_d = {}
    if dbg:
        for n, shp in dbg.items():
            dbg_d[n] = nc.dram_tensor(n, shp, F32, kind="ExternalOutput").ap()

    with contextlib.ExitStack() as st:
        SB = lambda n, s, d: st.enter_context(nc.sbuf_tensor(n + "_sb", s, d))
        arena = SB("arena", [128, ARENA_BYTES // 2], BF16)
        banks = [st.enter_context(nc.psum_tensor(f"bank{i}", [128, 512], F32)) for i in range(8)]
        Sx = Sched(nc)
        Sx.notrack.update(["x"] + [n for n, _, _ in CONST_SPECS] + [n for n, _ in PARAM_SPECS])

        def A16(off, n):
            assert off % 4 == 0 and off + 2 * n <= ARENA_BYTES, (off, n)
            return arena[:, off // 2: off // 2 + n]

        def A32(off, n):
            assert off % 4 == 0 and off + 4 * n <= ARENA_BYTES, (off, n)
            return arena[:, off // 2: off // 2 + 2 * n].bitcast(F32)

        def pb16(b):
            return banks[b][:, :].bitcast(BF16)

        def act(out, in_, func, extra=(), **kw):
            w = [out] + ([kw["accum_out"]] if "accum_out" in kw else [])
            Sx.op("act", lambda e: e.activation(out, in_, func, **kw), reads=[in_] + list(extra), writes=w)

        NOPOOL = True

        def tt(eng, out, in0, in1, op):
            if NOPOOL and eng == "pool":
                eng = "dve"
            Sx.op(eng, lambda e: e.tensor_tensor(out, in0, in1, op), reads=[in0, in1], writes=[out])

        def ts(eng, out, in0, s1, s2, op0, op1=None, extra=()):
            if NOPOOL and eng == "pool":
                eng = "dve"
                if op1 == ALU.mult and s2 == 1.0:
                    op1, s2 = None, None
            if op1 is None:
                Sx.op(eng, lambda e: e.tensor_scalar(out, in0, s1, None, op0), reads=[in0] + list(extra), writes=[out])
            else:
                Sx.op(eng, lambda e: e.tensor_scalar(out, in0, s1, s2, op0, op1), reads=[in0] + list(extra), writes=[out])

        def stt(out, in0, sc, in1, op0, op1, extra=(), accum=None):
            w = [out] + ([accum] if accum is not None else [])
            if accum is None:
                Sx.op("dve", lambda e: e.scalar_tensor_tensor(out, in0, sc, in1, op0, op1),
                      reads=[in0, in1] + list(extra), writes=w)
            else:
                Sx.op("dve", lambda e: e.scalar_tensor_tensor(out, in0, sc, in1, op0, op1, accum_out=accum),
                      reads=[in0, in1] + list(extra), writes=w)

        def cp(eng, out, in_):
            if NOPOOL and eng == "pool":
                eng = "dve"
            if eng == "act":
                act(out, in_, AF.Copy)
            else:
                Sx.op(eng, lambda e: e.tensor_copy(out, in_), reads=[in_], writes=[out])

        def recip(out, in_):
            Sx.op("dve", lambda e: e.reciprocal(out, in_), reads=[in_], writes=[out])

        def memset(eng, ap, v):
            Sx.op(eng, lambda e: e.memset(ap, v), writes=[ap])

        ident = SB("ident", [128, 128], BF16)
        bm = SB("bm", [128, 128], BF16)
        rg = SB("rg", [128, 2, 128], BF16)
        maskG = SB("maskG", [128, 512], BF16)
        maskA = SB("maskA", [128, 512], BF16)
        maskB = SB("maskB", [128, 512], BF16)
        mb3 = SB("mb3", [128, 2048], BF16)
        wout = SB("wout", [128, 8, 1024], BF16)
        convw = SB("convw", [128, 48], F32)
        gsm = SB("gsm", [128, 32], F32)
        stats = SB("stats", [128, 64], F32)
        ones16 = SB("ones16", [128, 2], BF16)
        for n, tl in (("ident", ident), ("bm", bm), ("maskG", maskG), ("maskA", maskA),
                      ("maskB", maskB), ("mb3", mb3)):
            Sx.dma(tl[:], cd[n])
        Sx.dma(convw[:], pd["convw"])
        Sx.dma(gsm[:, 0:8], pd["g_mix"])
        Sx.dma(gsm[:, 8:16], pd["g_ffn"])
        Sx.dma(gsm[:, 16:24], pd["g_out"])
        Sx.dma(gsm[:, 24:26], pd["g_qk"])
        memset("pool", ones16[:], 1.0)
        onesM = SB("onesM", [128, 128], BF16)
        memset("pool", onesM[:], 1.0)
        epsc = SB("epsc", [128, 2], F32)
        memset("pool", epsc[:], EPS)

        o = 0
        r0f = A32(o, 128); o += 512
        cosf = A32(o, S); o += 4 * S
        Sx.dma(r0f, cd["r0"])
        Sx.dma(cosf, cd["cosT"])
        cos16 = [A16(o, S), A16(o + 2 * S, S)]
        for j in range(2):
            ts("dve", rg[:, j, :], r0f, gsm[:, 24 + j:25 + j], None, ALU.mult, extra=[gsm[:, 24 + j:25 + j]])
            ts("dve", cos16[j], cosf, gsm[:, 24 + j:25 + j], None, ALU.mult, extra=[gsm[:, 24 + j:25 + j]])
            Sx.dma(rope_s[j], cos16[j])

        o = 20 * 1024
        wst = [A32(o, 3072), A32(o + 12288, 3072)]
        o += 24576
        wcv = [A16(o, 3072), A16(o + 6144, 3072)]
        o += 12288
        it = 0

        def wscale(i, out, in_, g):
            if i % 2:
                ts("dve", out, in_, g, None, ALU.mult, extra=[g])
            else:
                ts("pool", out, in_, g, 1.0, ALU.mult, ALU.mult, extra=[g])

        for kc in range(8):
            b = it % 2; it += 1
            Sx.dma(wst[b], pd["w_in"][kc * 128:(kc + 1) * 128, :], q="pool")
            wscale(it, wcv[b], wst[b], gsm[:, kc:kc + 1])
            Sx.dma(win_s[:, :, kc, :].rearrange("f p c -> p f c"),
                   wcv[b].rearrange("p (f c) -> p f c", c=128), q="sp")
        for kc in range(8):
            for hh in range(2):
                b = it % 2; it += 1
                Sx.dma(wst[b][:, 0:2048], pd["w_up"][kc * 128:(kc + 1) * 128, hh * 2048:(hh + 1) * 2048], q="pool")
                wscale(it, wcv[b][:, 0:2048], wst[b][:, 0:2048], gsm[:, 8 + kc:9 + kc])
                Sx.dma(wup_s[hh * 16:(hh + 1) * 16, :, kc, :].rearrange("f p c -> p f c"),
                       wcv[b][:, 0:2048].rearrange("p (f c) -> p f c", c=128), q="sp")
        for fch in range(0, 32, 2):
            b = it % 2; it += 1
            Sx.dma(wst[b][:, 0:2048].rearrange("p (f c) -> p f c", c=1024),
                   pd["w_down"][fch * 128:(fch + 2) * 128, :].rearrange("(f p) c -> p f c", p=128), q="pool")
            if it % 2:
                cp("dve", wcv[b][:, 0:2048], wst[b][:, 0:2048])
            else:
                cp("act", wcv[b][:, 0:2048], wst[b][:, 0:2048])
            Sx.dma(wdn_s[fch:fch + 2, :, :].rearrange("f p c -> p f c"),
                   wcv[b][:, 0:2048].rearrange("p (f c) -> p f c", c=1024), q="sp")
        for kc in range(0, 8, 2):
            b = it % 2; it += 1
            Sx.dma(wst[b][:, 0:2048].rearrange("p (f c) -> p f c", c=1024),
                   pd["w_out"][kc * 128:(kc + 2) * 128, :].rearrange("(f p) c -> p f c", p=128), q="pool")
            for j in range(2):
                ts("dve", wout[:, kc + j, :], wst[b][:, j * 1024:(j + 1) * 1024], gsm[:, 16 + kc + j:17 + kc + j],
                   None, ALU.mult, extra=[gsm[:, 16 + kc + j:17 + kc + j]])

        o = 0
        featT = A32(o, S); o += 8192
        w1 = A32(o, 64); o += 256
        w2 = A32(o, 64); o += 256
        fv = A32(o, 8); o += 32
        w3 = A32(o, 2048); o += 8192
        h1 = A32(o, S); o += 8192
        h2 = A32(o, S); o += 8192
        wtmp = A32(o, S); o += 8192
        bsd = A32(o, 2048); o += 8192
        b3b = A32(o, 2048); o += 8192
        skb = A32(o, 1024); o += 4096
        dcb = [A32(o + i * 2048, 512) for i in range(2)]; o += 4096
        ftmp = [A32(o + i * 2048, 512) for i in range(4)]; o += 8192
        hsd = A16(o, 4 * 16 * 512); o += 65536
        assert o <= ARENA_BYTES, o
        hsd4 = hsd.rearrange("p (a t c) -> p a t c", a=4, t=16)
        Sx.dma(featT[0:33, :], cd["featT"])
        Sx.dma(w1[0:33, :], pd["flt_w1"])
        Sx.dma(w2[0:64, :], pd["flt_w2"])
        Sx.dma(fv[0:64, 0:4], pd["flt_v"])
        Sx.dma(w3[0:64, :], pd["flt_w3"])
        Sx.dma(b3b, pd["flt_b3"].partition_broadcast(128)[:, 0, :])
        Sx.dma(skb, pd["hy_skip"].partition_broadcast(128)[:, 0, :])
        tt("dve", fv[0:64, 4:5], fv[0:64, 0:1], fv[0:64, 1:2], ALU.mult)
        tt("dve", fv[0:64, 5:6], fv[0:64, 2:3], fv[0:64, 3:4], ALU.mult)
        b3v = b3b.rearrange("p (o d c) -> p o d c", o=2, d=2)
        bsdv = bsd.rearrange("p (o d c) -> p o d c", o=2, d=2)
        for oo in range(2):
            tt("pool", bsdv[:, oo, 0, :], b3v[:, oo, 0, :], b3v[:, oo, 1, :], ALU.add)
            tt("pool", bsdv[:, oo, 1, :], b3v[:, oo, 0, :], b3v[:, oo, 1, :], ALU.subtract)
        PI = float(np.pi)

        def sin_layer(dst, src_w, src_k, rhs_t, fcol, fbcol):
            for tg in range(4):
                bk = banks[tg]
                Sx.mm(bk[0:64, :], src_w, rhs_t[:, tg * 512:(tg + 1) * 512])
                ts("dve", wtmp[0:64, tg * 512:(tg + 1) * 512], bk[0:64, :], fv[0:64, fcol:fcol + 1],
                   fv[0:64, fbcol:fbcol + 1], ALU.mult, ALU.add, extra=[fv[0:64, fcol:fcol + 1], fv[0:64, fbcol:fbcol + 1]])
            w_ = wtmp[0:64, :]
            m_ = dst[0:64, :]
            for _ in range(2):
                ts("dve", m_, w_, PI, 2 * PI, ALU.is_gt, ALU.mult)
                tt("dve", w_, w_, m_, ALU.subtract)
                ts("dve", m_, w_, -PI, 2 * PI, ALU.is_lt, ALU.mult)
                tt("dve", w_, w_, m_, ALU.add)
            act(m_, w_, AF.Sin)

        sin_layer(h1, w1[0:33, :], 33, featT[0:33, :], 1, 4)
        sin_layer(h2, w2[0:64, :], 64, h1[0:64, :], 3, 5)
        for t_ in range(16):
            for oo in range(2):
                pf, pbk = banks[4 + 2 * (t_ % 2)], banks[5 + 2 * (t_ % 2)]
                tcols = slice((t_ // 8) + 256 * (t_ % 8), (t_ // 8) + 256 * (t_ % 8) + 255, 2)
                Sx.mm(pf[:, :], h2[0:64, tcols], w3[0:64, (2 * oo) * 512:(2 * oo + 1) * 512])
                Sx.mm(pbk[:, :], h2[0:64, tcols], w3[0:64, (2 * oo + 1) * 512:(2 * oo + 2) * 512])
                cp("act", ftmp[0], pf[:, :])
                tt("dve", ftmp[1], pbk[:, :], ftmp[0], ALU.add)
                stt(ftmp[2], pbk[:, :], -1.0, ftmp[0], ALU.mult, ALU.add)
                tt("pool", ftmp[1], ftmp[1], bsdv[:, oo, 0, :], ALU.add)
                tt("pool", ftmp[2], ftmp[2], bsdv[:, oo, 1, :], ALU.add)
                dc = dcb[t_ % 2]
                if oo == 0:
                    Sx.dma(dc, cd["decay"][:, t_, :])
                tt("pool", hsd4[:, 2 * oo, t_, :], ftmp[1], dc, ALU.mult)
                tt("dve", hsd4[:, 2 * oo + 1, t_, :], ftmp[2], dc, ALU.mult)
        kout = [wtmp, h1]
        h2b = h2.bitcast(BF16)
        k16b = [h2b[:, 0:2048], h2b[:, 2048:4096]]
        cbk = [featT[:, 0:512], featT[:, 512:1024]]
        assert o <= 152 * 1024, o
        SLAB = 152 * 1024
        slabs = [A16(SLAB + i * 8192, 4096) for i in range(3)]
        si = 0
        for oo in range(2):
            for a in range(8):
                sl = slabs[si % 3]; si += 1
                slv = sl.rearrange("p (r q j k) -> p r q j k", r=2, q=2, j=8)
                Sx.dma(slv, cd["fwd"][a])
                bset = [banks[4 * (a % 2) + i] for i in range(4)]
                for par in range(2):
                    for cs in range(2):
                        pk = bset[2 * par + cs]
                        for j in range(8):
                            Sx.mm(pk[:, :], slv[:, par, cs, j, :], hsd4[:, 2 * oo + cs, 8 * par + j, :], start=(j == 0), stop=(j == 7))
                Ac, As_, Bc, Bs = bset
                ko = kout[a % 2]
                sk = skb[:, oo * 512:(oo + 1) * 512]
                cp("act", cbk[0], Bc[:, :])
                cp("act", cbk[1], Bs[:, :])
                tt("dve", ko[:, 0:512], Ac[:, :], cbk[0], ALU.add)
                tt("pool", ko[:, 0:512], ko[:, 0:512], sk, ALU.add)
                tt("dve", ko[:, 1024:1536], Ac[:, :], cbk[0], ALU.subtract)
                tt("pool", ko[:, 1024:1536], ko[:, 1024:1536], sk, ALU.add)
                tt("dve", ko[:, 512:1024], As_[:, :], cbk[1], ALU.add)
                stt(ko[:, 1536:2048], As_[:, :], -1.0, cbk[1], ALU.mult, ALU.add)
                k16 = k16b[a % 2]
                cp("act", k16, ko)
                Sx.dma(ksp_s[oo, a], k16.rearrange("p (j c) -> p j c", j=4), q="pool")

        R1 = 0
        R3 = 32 * 1024
        R5 = 48 * 1024
        RX = 64 * 1024
        R7 = 152 * 1024
        R8 = 176 * 1024
        xnT = A16(R1, 8 * S).rearrange("p (k t) -> p k t", k=8)
        Pbuf = A16(R1, 32 * 512).rearrange("p (j c) -> p j c", j=32)
        ffT = A16(R1, 32 * 512).rearrange("p (j c) -> p j c", j=32)
        vT = A16(R3, 4 * S).rearrange("p (k t) -> p k t", k=4)
        hyT = vT
        attnT = A16(R5, 4 * S).rearrange("p (k t) -> p k t", k=4)
        qT = A16(RX, 4 * S).rearrange("p (k t) -> p k t", k=4)
        kT = A16(RX + 16384, 4 * S).rearrange("p (k t) -> p k t", k=4)
        Vp = A16(RX + 32768, 2 * 53 * 128).rearrange("p (h t c) -> p h t c", h=2, t=53)
        TMP = RX + 32768 + 27136 + 512
        zx = A16(RX, 16 * 1536).rearrange("p (t c) -> p t c", t=16)
        UB = RX + 49152
        ubuf = A32(UB, 2050)
        c1 = A32(UB + 8448, 1024)
        c2 = A32(UB + 8448 + 4096, 1024)
        c16 = A16(UB + 8448 + 8192, 2048)
        TL = RX
        hbuf = A32(TL, 4 * 1024).rearrange("p (t c) -> p t c", t=4); TL += 16384
        hnT = A16(TL, 8 * 512).rearrange("p (k t) -> p k t", k=8); TL += 8192
        xt = [A32(TL + i * 4096, 1024) for i in range(2)]; TL += 8192
        yt = [A32(TL + i * 4096, 1024) for i in range(2)]; TL += 8192
        xn16 = [A16(TL + i * 2048, 1024) for i in range(2)]; TL += 4096
        rl = [A32(TL + i * 2048, 512) for i in range(2)]; TL += 4096
        sqj = A16(TL, 1024); TL += 2048
        wbuf = [A16(R7 + i * 2048, 1024) for i in range(4)]
        dbuf = [A16(R7 + 8192 + i * 2048, 1024) for i in range(4)]
        kbuf = [A32(R8, 1024)]
        wi = [0]
        di = [0]

        def vp_tiles():
            tl = []
            for i in range(17):
                s0, s1 = max(0, 128 * i - 64), min(S, 128 * i + 64)
                tl.append((s0, 1, s1 - s0))
            for rho in range(4):
                for i in range(5):
                    j0, j1 = max(0, 128 * i - 64), min(512, 128 * i + 64)
                    tl.append((4 * j0 + rho, 4, j1 - j0))
            for r in range(16):
                tl.append((r, 16, 128))
            return tl

        VPT = vp_tiles()

        def cols(start, step, n):
            return slice(start, start + step * (n - 1) + 1, step)

        for s in range(NS):
            Sx.tag = f"s{s}p1"
            for t_ in range(16):
                xb = xt[t_ % 2]
                Sx.dma(xb, x_d[s, t_ * 128:(t_ + 1) * 128, :])
                ssq = stats[:, t_ % 2:t_ % 2 + 1]
                act(sqj, xb, AF.Square, accum_out=ssq)
                act(stats[:, 2 + t_ % 2:3 + t_ % 2], ssq, AF.Sqrt, scale=1.0 / D, bias=EPS)
                recip(stats[:, 4 + t_ % 2:5 + t_ % 2], stats[:, 2 + t_ % 2:3 + t_ % 2])
                rs = stats[:, 4 + t_ % 2:5 + t_ % 2]
                xb16 = xn16[t_ % 2]
                ts("dve", xb16, xb, rs, None, ALU.mult, extra=[rs])
                for g in range(2):
                    pbk = pb16(6 + g)
                    for j in range(4):
                        kc = g * 4 + j
                        Sx.tr(pbk[:, j * 128:(j + 1) * 128], xb16[:, kc * 128:(kc + 1) * 128], ident[:])
                    cp("act" if g == 0 else "dve", xnT[:, g * 4:(g + 1) * 4, t_ * 128:(t_ + 1) * 128],
                       pbk[:, 0:512].rearrange("p (j c) -> p j c", j=4))

            Sx.tag = f"s{s}p2"
            bki = [0]

            def inproj_chunk(fc, consumer):
                wb = wbuf[wi[0] % 4]; wi[0] += 1
                wv = wb.rearrange("p (k c) -> p k c", k=8)
                Sx.dma(wv, win_s[fc])
                for tg in range(4):
                    bk = banks[bki[0] % 4]; bki[0] += 1
                    for kc in range(8):
                        Sx.mm(bk[:, :], wv[:, kc, :], xnT[:, kc, tg * 512:(tg + 1) * 512], start=(kc == 0), stop=(kc == 7))
                    consumer(fc, tg, bk)

            QT = TMP
            cosq = A16(RX + 32768, S)
            cosk = A16(RX + 32768 + 2 * S, S)
            sinT = A16(RX + 32768 + 4 * S, S)
            Sx.dma(cosq, rope_s[0])
            Sx.dma(cosk, rope_s[1])
            Sx.dma(sinT, cd["sinT"])
            a_sb = [A32(QT + i * 2048, 512) for i in range(2)]
            sq16 = [A16(QT + 4096 + i * 1024, 512) for i in range(2)]
            a16 = [A16(QT + 6144 + i * 1024, 512) for i in range(2)]
            sd = [A32(QT + 8192 + i * 2048, 512) for i in range(2)]
            t1 = [A32(QT + 12288 + i * 2048, 512) for i in range(2)]
            t2 = [A32(QT + 16384 + i * 2048, 512) for i in range(2)]
            qi = [0]

            def qk_consumer(fc, tg, bk):
                i = qi[0] % 2; qi[0] += 1
                isk = fc >= 4
                dst = (kT if isk else qT)[:, fc % 4, tg * 512:(tg + 1) * 512]
                ctab = (cosk if isk else cosq)[:, tg * 512:(tg + 1) * 512]
                cp("act", a_sb[i], bk[:, :])
                act(sq16[i], bk[:, :], AF.Square)
                cp("dve", a16[i], a_sb[i])
                pm, pr = banks[4 + (qi[0] % 2) * 2], banks[5 + (qi[0] % 2) * 2]
                Sx.mm(pm[:, :], bm[:], sq16[i])
                Sx.mm(pr[:, :], rg[:, 1 if isk else 0, :], a16[i])
                act(t2[i], pm[:, :], AF.Ln, bias=epsc[:, 0:1], extra=[epsc[:, 0:1]])
                act(sd[i], t2[i], AF.Exp, scale=-0.5)
                tt("pool", t1[i], a_sb[i], ctab, ALU.mult)
                tt("dve", t2[i], pr[:, :], sinT[:, tg * 512:(tg + 1) * 512], ALU.mult)
                tt("pool", t1[i], t1[i], t2[i], ALU.add)
                tt("dve", dst, t1[i], sd[i], ALU.mult)

            def v_consumer(fc, tg, bk):
                cp("act", vT[:, fc - 8, tg * 512:(tg + 1) * 512], bk[:, :])

            for fc in range(8):
                inproj_chunk(fc, qk_consumer)
            for fc in range(8, 12):
                inproj_chunk(fc, v_consumer)

            Sx.tag = f"s{s}p3"
            AT = TMP
            PT = [A16(AT + i * 1024, 512) for i in range(6)]
            tot = [A32(AT + 6144 + i * 2048, 512) for i in range(2)]
            rden = [A32(AT + 10240 + i * 2048, 512) for i in range(2)]
            lnd = [A32(AT + 14336 + i * 2048, 512) for i in range(2)]
            memset("pool", Vp[:, :, :, 64:128], 1.0)
            items = []

            def add_vp_items(hp):
                for g0 in range(0, 53, 4):
                    g1 = min(53, g0 + 4)

                    def front(bk, g0=g0, g1=g1, hp=hp):
                        pbk = banks[bk][:, :].bitcast(BF16)
                        for ti in range(g0, g1):
                            st0, stp, n = VPT[ti]
                            Sx.tr(pbk[0:n, (ti - g0) * 128:(ti - g0 + 1) * 128], vT[:, hp, cols(st0, stp, n)], ident[:])

                    def back(bk, k, g0=g0, g1=g1):
                        pbk = banks[bk][:, :].bitcast(BF16)
                        full = all(VPT[ti][2] == 128 for ti in range(g0, g1))
                        if full:
                            cp("dve" if (g0 // 4) % 2 else "act", Vp[:, :, g0:g1, 0:64],
                               pbk[:, 0:(g1 - g0) * 128].rearrange("p (t h e) -> p h t e", h=2, e=64))
                        else:
                            for ti in range(g0, g1):
                                n = VPT[ti][2]
                                cp("dve" if ti % 2 else "act", Vp[0:n, :, ti, 0:64],
                                   pbk[0:n, (ti - g0) * 128:(ti - g0 + 1) * 128].rearrange("p (h e) -> p h e", h=2))
                    items.append((front, back))

            def add_batch(hp, hh, c, smm, mask, pvs, state, last):
                def front(bk, smm=smm):
                    sbk = banks[bk]
                    for (n, cs, l, r) in smm:
                        Sx.mm(sbk[0:n, cs], l, r)

                def back(bk, k, mask=mask, pvs=pvs, state=state, last=last, hp=hp, hh=hh, c=c):
                    sbk = banks[bk]
                    pt = PT[k % 6]
                    act(pt, sbk[:, :], AF.Exp, scale=0.125)
                    tt("pool", pt, pt, mask, ALU.mult)
                    accs = (banks[0], banks[1], banks[2])
                    for (co, nk, ncl, vti, ai, acs) in pvs:
                        Sx.mm(accs[ai][:, acs], Vp[0:nk, hh, vti, :], pt[0:nk, co:co + ncl], start=state[ai], stop=False)
                        state[ai] = False
                    if last:
                        i2 = (2 * hp + hh + c) % 2
                        tv = tot[i2]
                        cp("dve", tv, accs[0][:, :])
                        tt("dve", tv.rearrange("p (j r) -> p j r", r=4), tv.rearrange("p (j r) -> p j r", r=4),
                           accs[1][:, :].rearrange("p (r j) -> p j r", r=4), ALU.add)
                        tt("dve", tv.rearrange("p (j r) -> p j r", r=16), tv.rearrange("p (j r) -> p j r", r=16),
                           accs[2][:, :].rearrange("p (r j) -> p j r", r=16), ALU.add)

                        def fin(hp=hp, hh=hh, c=c, i2=i2):
                            r0_, r1_ = hh * 64, hh * 64 + 64
                            tv, rv, lv = tot[i2], rden[i2], lnd[i2]
                            act(lv[0:64, :], tv[64:128, :], AF.Ln)
                            act(rv[0:64, :], lv[0:64, :], AF.Exp, scale=-1.0)
                            tt("pool", attnT[r0_:r1_, hp, c * 512:(c + 1) * 512], tv[0:64, :], rv[0:64, :], ALU.mult)
                        deferred.append((k + 2, fin))
                items.append((front, back))

            for hp in range(4):
                add_vp_items(hp)
                for hh in range(2):
                    r0_, r1_ = hh * 64, hh * 64 + 64
                    qh = qT[r0_:r1_, hp, :]
                    kh = kT[r0_:r1_, hp, :]
                    for c in range(4):
                        state = [True, True, True]
                        for half in range(2):
                            smm, pvs = [], []
                            for qa in range(2):
                                a = 4 * c + 2 * half + qa
                                for kt_ in range(2):
                                    ti = a + kt_
                                    st0, stp, n = VPT[ti]
                                    co = (qa * 2 + kt_) * 128
                                    smm.append((n, slice(co, co + 128), kh[:, cols(st0, 1, n)], qh[:, a * 128:(a + 1) * 128]))
                                    pvs.append((co, n, 128, ti, 0, slice((2 * half + qa) * 128, (2 * half + qa + 1) * 128)))
                            add_batch(hp, hh, c, smm, (maskA if (c == 0 and half == 0) else maskG)[:], pvs, state, False)
                        for half in range(2):
                            smm, pvs = [], []
                            for qa in range(2):
                                rho = 2 * half + qa
                                for kt_ in range(2):
                                    ti = 17 + 5 * rho + c + kt_
                                    st0, stp, n = VPT[ti]
                                    co = (qa * 2 + kt_) * 128
                                    smm.append((n, slice(co, co + 128), kh[:, cols(st0, 4, n)], qh[:, cols(512 * c + rho, 4, 128)]))
                                    pvs.append((co, n, 128, ti, 1, slice(rho * 128, (rho + 1) * 128)))
                            add_batch(hp, hh, c, smm, (maskB if c == 0 else maskG)[:], pvs, state, False)
                        smm, pvs = [], []
                        for r in range(16):
                            smm.append((128, slice(r * 32, (r + 1) * 32), kh[:, cols(r, 16, 128)], qh[:, cols(512 * c + r, 16, 32)]))
                            pvs.append((r * 32, 128, 32, 37 + r, 2, slice(r * 32, (r + 1) * 32)))
                        add_batch(hp, hh, c, smm, mb3[:, c * 512:(c + 1) * 512], pvs, state, True)
            LA = 3
            deferred = []
            for i in range(len(items) + LA):
                if i < len(items):
                    items[i][0](3 + i % 5)
                if i - LA >= 0:
                    items[i - LA][1](3 + (i - LA) % 5, i - LA)
                    while deferred and deferred[0][0] <= i - LA:
                        deferred.pop(0)[1]()
            while deferred:
                deferred.pop(0)[1]()

            if dbg and "d_attn" in dbg_d and s == 0:
                for kc in range(4):
                    dtmp = A32(TMP, 2048)
                    cp("dve", dtmp, attnT[:, kc, :])
                    Sx.dma(dbg_d["d_attn"][kc * 128:(kc + 1) * 128, :], dtmp)

            Sx.tag = f"s{s}p4"
            memset("pool", ubuf[:, 0:1], 0.0)
            memset("pool", ubuf[:, 2049:2050], 0.0)
            tri = [0]

            def hy_consumer(fc, tg, bk):
                cp("act", ubuf[:, 1 + tg * 512:1 + (tg + 1) * 512], bk[:, :])
                if tg != 3:
                    return
                ch = fc - 12
                w = lambda k: convw[:, ch * 4 + k:ch * 4 + k + 1]
                for hf in range(2):
                    u0 = 1024 * hf
                    ts("dve", c1, ubuf[:, u0:u0 + 1024], w(0), w(3), ALU.mult, ALU.add, extra=[w(0), w(3)])
                    stt(c2, ubuf[:, u0 + 1:u0 + 1025], w(1), c1, ALU.mult, ALU.add, extra=[w(1)])
                    stt(c16[:, u0:u0 + 1024], ubuf[:, u0 + 2:u0 + 1026], w(2), c2, ALU.mult, ALU.add, extra=[w(2)])
                for g in range(4):
                    pbk = pb16(4 + tri[0] % 4); tri[0] += 1
                    for j in range(4):
                        t_ = 4 * g + j
                        Sx.tr(pbk[:, j * 128:(j + 1) * 128], c16[:, cols((t_ // 8) + 256 * (t_ % 8), 2, 128)], ident[:])
                    cp("act" if g % 2 else "dve", zx[:, 4 * g:4 * g + 4, ch * 128:(ch + 1) * 128],
                       pbk[:, 0:512].rearrange("p (j c) -> p j c", j=4))

            sqFa = A16(RX + 71680, 4 * 2048).rearrange("p (g k t) -> p g k t", g=4, k=4)
            for tg in range(4):
                tgs = slice(tg * 512, (tg + 1) * 512)
                tt("pool", sqFa[:, tg], attnT[:, :, tgs], attnT[:, :, tgs], ALU.mult)

            def attn_norm(tgl):
                lnb = A32(R8, 512)
                rab = A32(R8 + 2048, 512)
                for tg in tgl:
                    tgs = slice(tg * 512, (tg + 1) * 512)
                    sqF = sqFa[:, tg]
                    pbc = banks[6 + tg % 2]
                    for kc in range(4):
                        Sx.mm(pbc[:, :], onesM[:], sqF[:, kc, :], start=(kc == 0), stop=(kc == 3))
                    act(lnb, pbc[:, :], AF.Ln, scale=1.0 / 512, bias=epsc[:, 0:1], extra=[epsc[:, 0:1]])
                    act(rab, lnb, AF.Exp, scale=-0.5)
                    for kc in range(4):
                        tt("pool" if kc % 2 else "dve", attnT[:, kc, tgs], attnT[:, kc, tgs], rab, ALU.mult)

            for fc in range(12, 24):
                inproj_chunk(fc, hy_consumer)
                if fc == 12:
                    attn_norm((0, 1))
                if fc == 13:
                    attn_norm((2, 3))

            Sx.tag = f"s{s}p5"
            DT = RX + 49152
            cAB = [A32(DT + i * 2048, 512) for i in range(4)]
            Zt = [A32(DT + 8192 + i * 2048, 512) for i in range(4)]
            mt = [A32(DT + 16384 + i * 2048, 512) for i in range(4)]
            Pr = [A32(DT + 24576 + i * 2048, 512) for i in range(4)]
            kb2 = A16(DT + 32768, 1024)
            Zb = [A16(DT + 34816 + i * 1024, 512) for i in range(4)]
            z2b = [A16(DT + 38912 + i * 1024, 512) for i in range(2)]
            Pv = Pbuf.rearrange("p (e q a) c -> p e q a c", e=2, q=2)
            slab_i = [0]
            for oo in range(2):
                for a in range(8):
                    sl = slabs[slab_i[0] % 3]; slab_i[0] += 1
                    slv = sl.rearrange("p (r q j k) -> p r q j k", r=2, q=2, j=8)
                    Sx.dma(slv, cd["fwd"][a])
                    kb = kbuf[0].bitcast(BF16)[:, 0:1024]
                    Sx.dma(kb.rearrange("p (j c) -> p j c", j=2), ksp_s[oo, a, :, 0:2, :], q="pool")
                    Sx.dma(kb2.rearrange("p (j c) -> p j c", j=2), ksp_s[oo, a, :, 2:4, :], q="pool")
                    bset = [banks[4 * (a % 2) + i] for i in range(4)]
                    for par_ in range(2):
                        for cs in range(2):
                            pk = bset[2 * par_ + cs]
                            for j in range(8):
                                Sx.mm(pk[:, :], slv[:, par_, cs, j, :], zx[:, 8 * par_ + j, 0:512], start=(j == 0), stop=(j == 7))
                    cp("act", cAB[2], bset[2][:, :])
                    cp("act", cAB[3], bset[3][:, :])
                    Br, Bi = cAB[2], cAB[3]
                    tt("dve", Zb[0], bset[0][:, :], Br, ALU.add)
                    tt("dve", Zb[1], bset[1][:, :], Bi, ALU.add)
                    tt("dve", Zb[2], bset[0][:, :], Br, ALU.subtract)
                    stt(Zb[3], bset[1][:, :], -1.0, Bi, ALU.mult, ALU.add)
                    for half, (kk_, zz) in enumerate(((kb, Zb[0:2]), (kb2, Zb[2:4]))):
                        kre, kim = kk_[:, 0:512], kk_[:, 512:1024]
                        tt("dve", mt[0], zz[0], kre, ALU.mult)
                        tt("pool", mt[1], zz[1], kim, ALU.mult)
                        tt("dve", mt[2], zz[0], kim, ALU.mult)
                        tt("pool", mt[3], zz[1], kre, ALU.mult)
                        tt("dve", Pr[2 * half], mt[0], mt[1], ALU.subtract)
                        tt("pool", Pr[2 * half + 1], mt[2], mt[3], ALU.add)
                    tt("dve", Pv[:, 0, 0, a, :], Pr[0], Pr[2], ALU.add)
                    tt("pool", Pv[:, 0, 1, a, :], Pr[1], Pr[3], ALU.subtract)
                    tt("dve", Pv[:, 1, 0, a, :], Pr[0], Pr[2], ALU.subtract)
                    tt("pool", Pv[:, 1, 1, a, :], Pr[1], Pr[3], ALU.add)
                for b in range(16):
                    sl = slabs[slab_i[0] % 3]; slab_i[0] += 1
                    slv = sl[:, 0:2048].rearrange("p (q a t) -> p q a t", q=2, a=8)
                    Sx.dma(slv, cd["inv"][b])
                    py = banks[b % 4]
                    e_ = b // 8
                    n = 0
                    for q_ in range(2):
                        for a in range(8):
                            Sx.mm(py[:, :], slv[:, q_, a, :], Pv[:, e_, q_, a, :], start=(n == 0), stop=(n == 15))
                            n += 1
                    if oo == 0:
                        tt("dve", zx[:, b, 0:512], py[:, :], zx[:, b, 512:1024], ALU.mult)
                    else:
                        zb = z2b[b % 2]
                        z2t = (cAB + Zt)[b % 8]
                        tt("dve", z2t, py[:, :], zx[:, b, 1024:1536], ALU.mult)
                        ssq = stats[:, 16 + b % 4:17 + b % 4]
                        rsq = stats[:, 20 + b % 4:21 + b % 4]
                        act(zb, z2t, AF.Square, accum_out=ssq)
                        act(rsq, ssq, AF.Ln, scale=1.0 / 512, bias=epsc[:, 0:1], extra=[epsc[:, 0:1]])
                        act(rsq, rsq, AF.Exp, scale=-0.5)
                        ts("dve", zb, z2t, rsq, None, ALU.mult, extra=[rsq])
                        pbk = pb16(4 + b % 2)
                        for j in range(4):
                            Sx.tr(pbk[:, j * 128:(j + 1) * 128], zb[:, j * 128:(j + 1) * 128], ident[:])
                        cp("act", hyT[:, :, cols((b // 8) + 256 * (b % 8), 2, 128)], pbk[:, 0:512].rearrange("p (j c) -> p j c", j=4))

            if dbg and "d_hy" in dbg_d and s == 0:
                for kc in range(4):
                    dtmp = A32(DT, 2048)
                    cp("dve", dtmp, hyT[:, kc, :])
                    Sx.dma(dbg_d["d_hy"][kc * 128:(kc + 1) * 128, :], dtmp)

            Sx.tag = f"s{s}p6"
            for g in range(4):
                def oproj(j, g=g):
                    t_ = 4 * g + j
                    tsl = slice(t_ * 128, (t_ + 1) * 128)
                    Sx.dma(xt[t_ % 2], x_d[s, tsl, :])
                    for hf in range(2):
                        pa = banks[2 * (j % 2) + hf]
                        for kc in range(8):
                            src = attnT if kc < 4 else hyT
                            Sx.mm(pa[:, :], src[:, kc % 4, tsl], wout[:, kc, hf * 512:(hf + 1) * 512], start=(kc == 0), stop=(kc == 7))

                def post(j, g=g):
                    t_ = 4 * g + j
                    xb = xt[t_ % 2]
                    for hf in range(2):
                        pa = banks[2 * (j % 2) + hf]
                        tt("dve", hbuf[:, j, hf * 512:(hf + 1) * 512], pa[:, :], xb[:, hf * 512:(hf + 1) * 512], ALU.add)
                    ssq = stats[:, 34 + t_ % 2:35 + t_ % 2]
                    act(sqj, hbuf[:, j, :], AF.Square, accum_out=ssq)
                    rf = stats[:, 36 + t_ % 2:37 + t_ % 2]
                    act(rf, ssq, AF.Ln, scale=1.0 / D, bias=epsc[:, 0:1], extra=[epsc[:, 0:1]])
                    act(rf, rf, AF.Exp, scale=-0.5)
                    xb16 = xn16[t_ % 2]
                    ts("dve", xb16, hbuf[:, j, :], rf, None, ALU.mult, extra=[rf])
                    for gg in range(2):
                        pbk = pb16(4 + gg)
                        for jj in range(4):
                            kc = gg * 4 + jj
                            Sx.tr(pbk[:, jj * 128:(jj + 1) * 128], xb16[:, kc * 128:(kc + 1) * 128], ident[:])
                        cp("act" if gg == 0 else "dve", hnT[:, gg * 4:(gg + 1) * 4, j * 128:(j + 1) * 128],
                           pbk[:, 0:512].rearrange("p (j c) -> p j c", j=4))

                oproj(0)
                oproj(1)
                post(0)
                oproj(2)
                post(1)
                oproj(3)
                post(2)
                post(3)
                for fch in range(32):
                    wb = wbuf[wi[0] % 4]; wi[0] += 1
                    wv = wb.rearrange("p (k c) -> p k c", k=8)
                    Sx.dma(wv, wup_s[fch])
                    bk = banks[6 + fch % 2]
                    for kc in range(8):
                        Sx.mm(bk[:, :], wv[:, kc, :], hnT[:, kc, :], start=(kc == 0), stop=(kc == 7))
                    rb = rl[fch % 2]
                    act(rb, bk[:, :], AF.Relu)
                    tt("pool", ffT[:, fch, :], rb, rb, ALU.mult)
                for fch in range(32):
                    db = dbuf[di[0] % 4]; di[0] += 1
                    Sx.dma(db, wdn_s[fch])
                    for j in range(4):
                        for hf in range(2):
                            Sx.mm(banks[2 * j + hf][:, :], ffT[:, fch, j * 128:(j + 1) * 128], db[:, hf * 512:(hf + 1) * 512],
                                  start=(fch == 0), stop=(fch == 31))
                for j in range(4):
                    t_ = 4 * g + j
                    yb = yt[t_ % 2]
                    for hf in range(2):
                        tt("dve", yb[:, hf * 512:(hf + 1) * 512], banks[2 * j + hf][:, :], hbuf[:, j, hf * 512:(hf + 1) * 512], ALU.add)
                    Sx.dma(y_d[s, t_ * 128:(t_ + 1) * 128, :], yb, q="pool")

        Sx.emit()
        import os
        if os.environ.get("KDUMP"):
            import json
            json.dump({e: [Sx.ops[i]["tag"] for i in Sx.streams[e]] for e in ENGS}, open(os.environ["KDUMP"], "w"))
    return nc


_PROG = {}


def _get_prog(NS):
    if NS not in _PROG:
        _PROG[NS] = build_program(NS)
    return _PROG[NS]


def kernel(**inputs):
    xp = np.asarray(inputs["x_prompt"], dtype=np.float32)
    xs = np.asarray(inputs["x_sample"], dtype=np.float32)
    xall = np.concatenate([xp, xs], axis=0)
    nseq = xall.shape[0]
    NS = nseq // NCORES
    consts = make_consts()
    params = layout_params(inputs)
    nc = _get_prog(NS)
    in_maps = []
    for c in range(NCORES):
        m = {"x": np.ascontiguousarray(xall[c * NS:(c + 1) * NS])}
        m.update(consts)
        m.update(params)
        in_maps.append(m)
    res = run_bass_kernel_spmd(nc, in_maps, core_ids=list(range(NCORES)))
    yall = np.concatenate([np.asarray(r["y"]) for r in res.results], axis=0)
    nb = xp.shape[0]
    return (np.ascontiguousarray(yall[:nb]).astype(np.float32), np.ascontiguousarray(yall[nb:]).astype(np.float32))
```

```python
import contextlib
import numpy as np
import ml_dtypes
import concourse.bass as bass
import concourse.mybir as mybir
from concourse.bass_utils import run_bass_kernel_spmd

F32 = mybir.dt.float32
BF16 = mybir.dt.bfloat16
AF = mybir.ActivationFunctionType
ALU = mybir.AluOpType
NCORES = 8
S = 2048
D = 1024
EPS = 1e-6

ENGS = ("pe", "act", "dve", "pool", "sp")


def _interval(ap):
    pat = ap.ap
    name = ap.tensor.name
    esz = mybir.dt.size(ap.dtype)
    sp = str(ap.space).upper()
    off = ap.offset
    if "SB" in sp or "PSUM" in sp:
        row = pat[0][0]
        p0 = off // row
        lo = off - p0 * row
        ext = 0
        for st, n in pat[1:]:
            ext += abs(st) * (n - 1)
        if "PSUM" in sp:
            return name, 0, 2048, (p0 // 32) * 32, ((p0 + pat[0][1] + 31) // 32) * 32
        return name, lo * esz, (lo + ext + 1) * esz, p0, p0 + pat[0][1]
    ext = 0
    for st, n in pat:
        ext += abs(st) * (n - 1)
    return name, off * esz, (off + ext + 1) * esz, 0, 1


class Sched:
    def __init__(self, nc, n_dma_sems=8):
        self.nc = nc
        self.ops = []
        self.streams = {e: [] for e in ENGS}
        self.wr = {}
        self.rd = {}
        self.K = n_dma_sems
        self.notrack = set()
        self.tag = "setup"

    def _deps_for(self, opid, reads, writes):
        deps = set()
        ri = [_interval(a) for a in reads]
        wi = [_interval(a) for a in writes]
        for key, lo, hi, plo, phi in ri:
            if key in self.notrack:
                continue
            for (l, h, pl, ph, w) in self.wr.get(key, ()):
                if l < hi and lo < h and pl < phi and plo < ph:
                    deps.add(w)
        for key, lo, hi, plo, phi in wi:
            if key in self.notrack:
                continue
            for (l, h, pl, ph, w) in self.wr.get(key, ()):
                if l < hi and lo < h and pl < phi and plo < ph:
                    deps.add(w)
            for (l, h, pl, ph, r) in self.rd.get(key, ()):
                if l < hi and lo < h and pl < phi and plo < ph:
                    deps.add(r)
        deps.discard(opid)
        for key, lo, hi, plo, phi in wi:
            if key in self.notrack:
                continue
            wl = self.wr.setdefault(key, [])
            wl[:] = [x for x in wl if not (lo <= x[0] and x[1] <= hi and plo <= x[2] and x[3] <= phi)]
            wl.append((lo, hi, plo, phi, opid))
            rl = self.rd.setdefault(key, [])
            rl[:] = [x for x in rl if not (lo <= x[0] and x[1] <= hi and plo <= x[2] and x[3] <= phi)]
        eng = self.ops[opid]["eng"]
        isdma = self.ops[opid]["dma"]
        for key, lo, hi, plo, phi in ri:
            if key in self.notrack:
                continue
            rl = self.rd.setdefault(key, [])
            if not isdma:
                rl[:] = [x for x in rl if not (x[0] == lo and x[1] == hi and x[2] == plo and x[3] == phi
                                               and self.ops[x[4]]["eng"] == eng and not self.ops[x[4]]["dma"])]
            rl.append((lo, hi, plo, phi, opid))
        return deps

    def op(self, eng, fn, reads=(), writes=(), dma=False):
        opid = len(self.ops)
        rec = dict(eng=eng, fn=fn, deps=None, dma=dma, tag=self.tag)
        self.ops.append(rec)
        rec["deps"] = self._deps_for(opid, list(reads), list(writes))
        self.streams[eng].append(opid)
        return opid

    def dma(self, out, in_, q="sp"):
        return self.op(q, lambda e: e.dma_start(out=out, in_=in_), reads=[in_], writes=[out], dma=True)

    def mm(self, out, lhsT, rhs, start=True, stop=True):
        return self.op("pe", lambda e: e.matmul(out, lhsT, rhs, start=start, stop=stop, skip_group_check=True),
                       reads=[lhsT, rhs], writes=[out])

    def tr(self, out, in_, ident):
        return self.op("pe", lambda e: e.transpose(out, in_, ident), reads=[in_, ident], writes=[out])

    def emit(self):
        nc = self.nc
        ops = self.ops
        needed = set()
        for o in ops:
            for d in o["deps"]:
                if not (o["eng"] == "pe" and ops[d]["eng"] == "pe"):
                    needed.add(d)
        cnt = {e: 0 for e in ENGS}
        dcnt = {e: 0 for e in ENGS}
        for e in ENGS:
            for i in self.streams[e]:
                o = ops[i]
                o["thr"] = None
                if o["dma"]:
                    k = dcnt[e]
                    dcnt[e] += 1
                    o["sem"] = ("d", e, k % self.K)
                    o["val"] = 16 * (k // self.K + 1)
                    if k >= self.K:
                        o["thr"] = (("d", e, k % self.K), 16 * (k // self.K))
                elif i in needed:
                    cnt[e] += 1
                    o["sem"] = ("c", e, 0)
                    o["val"] = cnt[e]
                else:
                    o["sem"] = None
        with contextlib.ExitStack() as st:
            sems = {}
            for e in ENGS:
                sems[("c", e, 0)] = st.enter_context(nc.semaphore(f"c_{e}"))
                if dcnt[e]:
                    for k in range(self.K):
                        sems[("d", e, k)] = st.enter_context(nc.semaphore(f"d_{e}{k}"))
            block = st.enter_context(nc.Block())

            def run(e, engobj):
                waited = {}
                for i in self.streams[e]:
                    o = ops[i]
                    req = {}
                    for d in o["deps"]:
                        od = ops[d]
                        if od["eng"] == "pe" and e == "pe":
                            continue
                        s = od["sem"]
                        if s is not None and req.get(s, 0) < od["val"]:
                            req[s] = od["val"]
                    if o["thr"] is not None:
                        s, v = o["thr"]
                        if req.get(s, 0) < v:
                            req[s] = v
                    for s, v in req.items():
                        if waited.get(s, 0) < v:
                            engobj.wait_ge(sems[s], v)
                            waited[s] = v
                    ins = o["fn"](engobj)
                    if o["sem"] is not None:
                        ins.then_inc(sems[o["sem"]], 16 if o["dma"] else 1)
                if dcnt[e]:
                    for k in range(self.K):
                        n = len(range(k, dcnt[e], self.K))
                        if n:
                            engobj.wait_ge(sems[("d", e, k)], 16 * n)

            @block.tensor
            def _(eng):
                run("pe", eng)

            @block.scalar
            def _(eng):
                run("act", eng)

            @block.vector
            def _(eng):
                run("dve", eng)

            @block.gpsimd
            def _(eng):
                run("pool", eng)

            @block.sync
            def _(eng):
                run("sp", eng)


_CONST = None


def make_consts():
    global _CONST
    if _CONST is not None:
        return _CONST
    bf = ml_dtypes.bfloat16
    c = {}
    c["ident"] = np.eye(128, dtype=np.float32).astype(bf)
    bm = np.zeros((128, 128), np.float32)
    bm[:64, :64] = 1.0 / 64
    bm[64:, 64:] = 1.0 / 64
    c["bm"] = bm.astype(bf)
    r0 = np.zeros((128, 128), np.float32)
    for m in range(128):
        hb, e = (m // 64) * 64, m % 64
        if e < 32:
            r0[hb + e + 32, m] = -1.0
        else:
            r0[hb + e - 32, m] = 1.0
    c["r0"] = r0
    half = 32
    inv_freq = (np.float32(10000.0) ** (-(np.arange(half, dtype=np.float32) / np.float32(half)))).astype(np.float32)
    ang = (np.arange(S, dtype=np.float32)[:, None] * inv_freq[None, :]).astype(np.float32)
    cs = np.cos(ang.astype(np.float64)).astype(np.float32).T
    sn = np.sin(ang.astype(np.float64)).astype(np.float32).T
    c["cosT"] = np.tile(cs, (4, 1))
    c["sinT"] = np.tile(sn, (4, 1)).astype(bf)
    kk = np.arange(128)[:, None]
    qq = np.arange(128)[None, :]
    m1 = (qq <= kk).astype(np.float32)
    m2 = (qq >= kk).astype(np.float32)
    m1e = np.zeros((128, 128), np.float32)
    m1e[:64] = m1[64:]
    c["maskG"] = np.concatenate([m1, m2, m1, m2], 1).astype(bf)
    c["maskA"] = np.concatenate([m1e, m2, m1, m2], 1).astype(bf)
    c["maskB"] = np.concatenate([m1e, m2, m1e, m2], 1).astype(bf)
    mb = (np.abs(qq - kk) <= 64).astype(np.float32)
    mb3 = np.zeros((128, 4, 16, 32), np.float32)
    for cc in range(4):
        mb3[:, cc, :, :] = mb[:, None, 32 * cc:32 * cc + 32]
    c["mb3"] = mb3.reshape(128, 4 * 512).astype(bf)
    par = np.arange(2, dtype=np.int64)[:, None, None]
    jj = np.arange(8, dtype=np.int64)[None, :, None]
    pp = np.arange(128, dtype=np.int64)[None, None, :]
    tok = par + 256 * jj + 2 * pp
    c["tok"] = tok
    k = np.arange(1024, dtype=np.int64)
    m = ((2 * k[None, None, None, :] + 1) * tok[..., None]) % 8192
    th = m.astype(np.float64) * (2.0 * np.pi / 8192.0)
    C = np.cos(th).reshape(2, 8, 128, 8, 128)
    Sn = np.sin(th).reshape(2, 8, 128, 8, 128)
    cf = C.transpose(3, 2, 0, 1, 4)
    sf = (-Sn).transpose(3, 2, 0, 1, 4)
    c["fwd"] = np.ascontiguousarray(np.stack([cf, sf], 3)).astype(bf)
    sc = 2.0 / 4096.0
    ci = (sc * C).transpose(0, 1, 4, 3, 2).reshape(16, 128, 8, 128)
    si = (-sc * Sn).transpose(0, 1, 4, 3, 2).reshape(16, 128, 8, 128)
    c["inv"] = np.ascontiguousarray(np.stack([ci, si], 2)).astype(bf)
    L = S
    pos = np.arange(L, dtype=np.float32)
    tt = (pos / np.float32(L - 1)).astype(np.float32)
    bands = 16
    fr = np.linspace(1e-4, bands - 1, bands, dtype=np.float32)
    angf = (np.float32(2.0 * np.pi) * pos[:, None] * fr[None, :] / np.float32(L)).astype(np.float32)
    feat = np.concatenate([tt[:, None], np.cos(angf.astype(np.float64)).astype(np.float32),
                           -np.sin(angf.astype(np.float64)).astype(np.float32)], -1)
    c["featT"] = np.ascontiguousarray(feat.T).astype(np.float32)
    deltas = np.linspace(np.log(1e-2) / 0.3, np.log(1e-2) / 1.5, 512, dtype=np.float32)
    decay = np.exp(-(tt[:, None] * np.abs(deltas)[None, :]).astype(np.float32).astype(np.float64)).astype(np.float32)
    c["decay"] = np.ascontiguousarray(decay[tok.reshape(16, 128)].transpose(1, 0, 2))
    del c["tok"]
    _CONST = c
    return c


CONST_SPECS = [("ident", [128, 128], BF16), ("bm", [128, 128], BF16), ("r0", [128, 128], F32),
               ("cosT", [128, S], F32), ("sinT", [128, S], BF16),
               ("maskG", [128, 512], BF16), ("maskA", [128, 512], BF16), ("maskB", [128, 512], BF16),
               ("mb3", [128, 2048], BF16), ("fwd", [8, 128, 2, 2, 8, 128], BF16),
               ("inv", [16, 128, 2, 8, 128], BF16), ("featT", [33, S], F32), ("decay", [128, 16, 512], F32)]

PARAM_SPECS = [("w_in", [D, 3072]), ("w_out", [D, D]), ("w_up", [D, 4096]), ("w_down", [4096, D]),
               ("g_mix", [128, 8]), ("g_ffn", [128, 8]), ("g_out", [128, 8]), ("g_qk", [128, 2]),
               ("convw", [128, 48]), ("flt_w1", [33, 64]), ("flt_w2", [64, 64]), ("flt_w3", [64, 2048]),
               ("flt_v", [64, 4]), ("flt_b3", [1, 2048]), ("hy_skip", [1, 1024])]


def layout_params(p):
    f = lambda a: np.ascontiguousarray(np.asarray(a, dtype=np.float32))
    out = {}
    out["w_in"] = f(p["w_in"][0])
    out["w_out"] = f(p["w_out"][0])
    out["w_up"] = f(p["w_up"][0])
    out["w_down"] = f(p["w_down"][0])
    out["g_mix"] = f(np.asarray(p["mix_norm"][0]).reshape(8, 128).T)
    out["g_ffn"] = f(np.asarray(p["ffn_norm"][0]).reshape(8, 128).T)
    gcat = np.concatenate([np.asarray(p["attn_out_norm"][0]), np.asarray(p["hy_out_norm"][0])])
    out["g_out"] = f(gcat.reshape(8, 128).T)
    gq = np.tile(np.asarray(p["q_norm"][0]), 2)
    gk = np.tile(np.asarray(p["k_norm"][0]), 2)
    out["g_qk"] = f(np.stack([gq, gk], 1))
    cw = np.asarray(p["hy_conv_w"][0])
    cb = np.asarray(p["hy_conv_b"][0])
    cwb = np.concatenate([cw, cb[None, :]], 0)
    out["convw"] = f(cwb.reshape(4, 12, 128).transpose(2, 1, 0).reshape(128, 48))
    out["flt_w1"] = f(p["flt_w1"][0])
    out["flt_w2"] = f(p["flt_w2"][0])
    out["flt_w3"] = f(p["flt_w3"][0])
    out["flt_v"] = f(np.stack([np.asarray(p["flt_b1"][0]), np.asarray(p["flt_freq1"][0]),
                               np.asarray(p["flt_b2"][0]), np.asarray(p["flt_freq2"][0])], 1))
    out["flt_b3"] = f(np.asarray(p["flt_b3"][0])[None, :])
    out["hy_skip"] = f(np.asarray(p["hy_skip"][0]).reshape(1, 1024))
    return out


ARENA_BYTES = 180 * 1024


def build_program(NS, dbg=None):
    nc = bass.Bass("TRN2", target_bir_lowering=False)
    x_d = nc.dram_tensor("x", [NS, S, D], F32, kind="ExternalInput").ap()
    y_d = nc.dram_tensor("y", [NS, S, D], F32, kind="ExternalOutput").ap()
    cd = {n: nc.dram_tensor(n, shp, dt, kind="ExternalInput").ap() for n, shp, dt in CONST_SPECS}
    pd = {n: nc.dram_tensor(n, shp, F32, kind="ExternalInput").ap() for n, shp in PARAM_SPECS}
    win_s = nc.dram_tensor("win_s", [24, 128, 8, 128], BF16).ap()
    wup_s = nc.dram_tensor("wup_s", [32, 128, 8, 128], BF16).ap()
    wdn_s = nc.dram_tensor("wdn_s", [32, 128, 1024], BF16).ap()
    ksp_s = nc.dram_tensor("ksp_s", [2, 8, 128, 4, 512], BF16).ap()
    rope_s = nc.dram_tensor("rope_s", [2, 128, S], BF16).ap()
    dbg_d = {}
    if dbg:
        for n, shp in dbg.items():
            dbg_d[n] = nc.dram_tensor(n, shp, F32, kind="ExternalOutput").ap()

    with contextlib.ExitStack() as st:
        SB = lambda n, s, d: st.enter_context(nc.sbuf_tensor(n + "_sb", s, d))
        arena = SB("arena", [128, ARENA_BYTES // 2], BF16)
        banks = [st.enter_context(nc.psum_tensor(f"bank{i}", [128, 512], F32)) for i in range(8)]
        Sx = Sched(nc)
        Sx.notrack.update(["x"] + [n for n, _, _ in CONST_SPECS] + [n for n, _ in PARAM_SPECS])

        def A16(off, n):
            assert off % 4 == 0 and off + 2 * n <= ARENA_BYTES, (off, n)
            return arena[:, off // 2: off // 2 + n]

        def A32(off, n):
            assert off % 4 == 0 and off + 4 * n <= ARENA_BYTES, (off, n)
            return arena[:, off // 2: off // 2 + 2 * n].bitcast(F32)

        def pb16(b):
            return banks[b][:, :].bitcast(BF16)

        def act(out, in_, func, extra=(), **kw):
            w = [out] + ([kw["accum_out"]] if "accum_out" in kw else [])
            Sx.op("act", lambda e: e.activation(out, in_, func, **kw), reads=[in_] + list(extra), writes=w)

        NOPOOL = True

        def tt(eng, out, in0, in1, op):
            if NOPOOL and eng == "pool":
                eng = "dve"
            Sx.op(eng, lambda e: e.tensor_tensor(out, in0, in1, op), reads=[in0, in1], writes=[out])

        def ts(eng, out, in0, s1, s2, op0, op1=None, extra=()):
            if NOPOOL and eng == "pool":
                eng = "dve"
                if op1 == ALU.mult and s2 == 1.0:
                    op1, s2 = None, None
            if op1 is None:
                Sx.op(eng, lambda e: e.tensor_scalar(out, in0, s1, None, op0), reads=[in0] + list(extra), writes=[out])
            else:
                Sx.op(eng, lambda e: e.tensor_scalar(out, in0, s1, s2, op0, op1), reads=[in0] + list(extra), writes=[out])

        def stt(out, in0, sc, in1, op0, op1, extra=(), accum=None):
            w = [out] + ([accum] if accum is not None else [])
            if accum is None:
                Sx.op("dve", lambda e: e.scalar_tensor_tensor(out, in0, sc, in1, op0, op1),
                      reads=[in0, in1] + list(extra), writes=w)
            else:
                Sx.op("dve", lambda e: e.scalar_tensor_tensor(out, in0, sc, in1, op0, op1, accum_out=accum),
                      reads=[in0, in1] + list(extra), writes=w)

        def cp(eng, out, in_):
            if NOPOOL and eng == "pool":
                eng = "dve"
            if eng == "act":
                act(out, in_, AF.Copy)
            else:
                Sx.op(eng, lambda e: e.tensor_copy(out, in_), reads=[in_], writes=[out])

        def recip(out, in_):
            Sx.op("dve", lambda e: e.reciprocal(out, in_), reads=[in_], writes=[out])

        def memset(eng, ap, v):
            Sx.op(eng, lambda e: e.memset(ap, v), writes=[ap])

        ident = SB("ident", [128, 128], BF16)
        bm = SB("bm", [128, 128], BF16)
        rg = SB("rg", [128, 2, 128], BF16)
        maskG = SB("maskG", [128, 512], BF16)
        maskA = SB("maskA", [128, 512], BF16)
        maskB = SB("maskB", [128, 512], BF16)
        mb3 = SB("mb3", [128, 2048], BF16)
        wout = SB("wout", [128, 8, 1024], BF16)
        convw = SB("convw", [128, 48], F32)
        gsm = SB("gsm", [128, 32], F32)
        stats = SB("stats", [128, 64], F32)
        ones16 = SB("ones16", [128, 2], BF16)
        for n, tl in (("ident", ident), ("bm", bm), ("maskG", maskG), ("maskA", maskA),
                      ("maskB", maskB), ("mb3", mb3)):
            Sx.dma(tl[:], cd[n])
        Sx.dma(convw[:], pd["convw"])
        Sx.dma(gsm[:, 0:8], pd["g_mix"])
        Sx.dma(gsm[:, 8:16], pd["g_ffn"])
        Sx.dma(gsm[:, 16:24], pd["g_out"])
        Sx.dma(gsm[:, 24:26], pd["g_qk"])
        memset("pool", ones16[:], 1.0)
        onesM = SB("onesM", [128, 128], BF16)
        memset("pool", onesM[:], 1.0)
        epsc = SB("epsc", [128, 2], F32)
        memset("pool", epsc[:], EPS)

        o = 0
        r0f = A32(o, 128); o += 512
        cosf = A32(o, S); o += 4 * S
        Sx.dma(r0f, cd["r0"])
        Sx.dma(cosf, cd["cosT"])
        cos16 = [A16(o, S), A16(o + 2 * S, S)]
        for j in range(2):
            ts("dve", rg[:, j, :], r0f, gsm[:, 24 + j:25 + j], None, ALU.mult, extra=[gsm[:, 24 + j:25 + j]])
            ts("dve", cos16[j], cosf, gsm[:, 24 + j:25 + j], None, ALU.mult, extra=[gsm[:, 24 + j:25 + j]])
            Sx.dma(rope_s[j], cos16[j])

        o = 20 * 1024
        wst = [A32(o, 3072), A32(o + 12288, 3072)]
        o += 24576
        wcv = [A16(o, 3072), A16(o + 6144, 3072)]
        o += 12288
        it = 0

        def wscale(i, out, in_, g):
            if i % 2:
                ts("dve", out, in_, g, None, ALU.mult, extra=[g])
            else:
                ts("pool", out, in_, g, 1.0, ALU.mult, ALU.mult, extra=[g])

        for kc in range(8):
            b = it % 2; it += 1
            Sx.dma(wst[b], pd["w_in"][kc * 128:(kc + 1) * 128, :], q="pool")
            wscale(it, wcv[b], wst[b], gsm[:, kc:kc + 1])
            Sx.dma(win_s[:, :, kc, :].rearrange("f p c -> p f c"),
                   wcv[b].rearrange("p (f c) -> p f c", c=128), q="sp")
        for kc in range(8):
            for hh in range(2):
                b = it % 2; it += 1
                Sx.dma(wst[b][:, 0:2048], pd["w_up"][kc * 128:(kc + 1) * 128, hh * 2048:(hh + 1) * 2048], q="pool")
                wscale(it, wcv[b][:, 0:2048], wst[b][:, 0:2048], gsm[:, 8 + kc:9 + kc])
                Sx.dma(wup_s[hh * 16:(hh + 1) * 16, :, kc, :].rearrange("f p c -> p f c"),
                       wcv[b][:, 0:2048].rearrange("p (f c) -> p f c", c=128), q="sp")
        for fch in range(0, 32, 2):
            b = it % 2; it += 1
            Sx.dma(wst[b][:, 0:2048].rearrange("p (f c) -> p f c", c=1024),
                   pd["w_down"][fch * 128:(fch + 2) * 128, :].rearrange("(f p) c -> p f c", p=128), q="pool")
            if it % 2:
                cp("dve", wcv[b][:, 0:2048], wst[b][:, 0:2048])
            else:
                cp("act", wcv[b][:, 0:2048], wst[b][:, 0:2048])
            Sx.dma(wdn_s[fch:fch + 2, :, :].rearrange("f p c -> p f c"),
                   wcv[b][:, 0:2048].rearrange("p (f c) -> p f c", c=1024), q="sp")
        for kc in range(0, 8, 2):
            b = it % 2; it += 1
            Sx.dma(wst[b][:, 0:2048].rearrange("p (f c) -> p f c", c=1024),
                   pd["w_out"][kc * 128:(kc + 2) * 128, :].rearrange("(f p) c -> p f c", p=128), q="pool")
            for j in range(2):
                ts("dve", wout[:, kc + j, :], wst[b][:, j * 1024:(j + 1) * 1024], gsm[:, 16 + kc + j:17 + kc + j],
                   None, ALU.mult, extra=[gsm[:, 16 + kc + j:17 + kc + j]])

        o = 0
        featT = A32(o, S); o += 8192
        w1 = A32(o, 64); o += 256
        w2 = A32(o, 64); o += 256
        fv = A32(o, 8); o += 32
        w3 = A32(o, 2048); o += 8192
        h1 = A32(o, S); o += 8192
        h2 = A32(o, S); o += 8192
        wtmp = A32(o, S); o += 8192
        bsd = A32(o, 2048); o += 8192
        b3b = A32(o, 2048); o += 8192
        skb = A32(o, 1024); o += 4096
        dcb = [A32(o + i * 2048, 512) for i in range(2)]; o += 4096
        ftmp = [A32(o + i * 2048, 512) for i in range(4)]; o += 8192
        hsd = A16(o, 4 * 16 * 512); o += 65536
        assert o <= ARENA_BYTES, o
        hsd4 = hsd.rearrange("p (a t c) -> p a t c", a=4, t=16)
        Sx.dma(featT[0:33, :], cd["featT"])
        Sx.dma(w1[0:33, :], pd["flt_w1"])
        Sx.dma(w2[0:64, :], pd["flt_w2"])
        Sx.dma(fv[0:64, 0:4], pd["flt_v"])
        Sx.dma(w3[0:64, :], pd["flt_w3"])
        Sx.dma(b3b, pd["flt_b3"].partition_broadcast(128)[:, 0, :])
        Sx.dma(skb, pd["hy_skip"].partition_broadcast(128)[:, 0, :])
        tt("dve", fv[0:64, 4:5], fv[0:64, 0:1], fv[0:64, 1:2], ALU.mult)
        tt("dve", fv[0:64, 5:6], fv[0:64, 2:3], fv[0:64, 3:4], ALU.mult)
        b3v = b3b.rearrange("p (o d c) -> p o d c", o=2, d=2)
        bsdv = bsd.rearrange("p (o d c) -> p o d c", o=2, d=2)
        for oo in range(2):
            tt("pool", bsdv[:, oo, 0, :], b3v[:, oo, 0, :], b3v[:, oo, 1, :], ALU.add)
            tt("pool", bsdv[:, oo, 1, :], b3v[:, oo, 0, :], b3v[:, oo, 1, :], ALU.subtract)
        PI = float(np.pi)

        def sin_layer(dst, src_w, src_k, rhs_t, fcol, fbcol):
            for tg in range(4):
                bk = banks[tg]
                Sx.mm(bk[0:64, :], src_w, rhs_t[:, tg * 512:(tg + 1) * 512])
                ts("dve", wtmp[0:64, tg * 512:(tg + 1) * 512], bk[0:64, :], fv[0:64, fcol:fcol + 1],
                   fv[0:64, fbcol:fbcol + 1], ALU.mult, ALU.add, extra=[fv[0:64, fcol:fcol + 1], fv[0:64, fbcol:fbcol + 1]])
            w_ = wtmp[0:64, :]
            m_ = dst[0:64, :]
            for _ in range(2):
                ts("dve", m_, w_, PI, 2 * PI, ALU.is_gt, ALU.mult)
                tt("dve", w_, w_, m_, ALU.subtract)
                ts("dve", m_, w_, -PI, 2 * PI, ALU.is_lt, ALU.mult)
                tt("dve", w_, w_, m_, ALU.add)
            act(m_, w_, AF.Sin)

        sin_layer(h1, w1[0:33, :], 33, featT[0:33, :], 1, 4)
        sin_layer(h2, w2[0:64, :], 64, h1[0:64, :], 3, 5)
        for t_ in range(16):
            for oo in range(2):
                pf, pbk = banks[4 + 2 * (t_ % 2)], banks[5 + 2 * (t_ % 2)]
                tcols = slice((t_ // 8) + 256 * (t_ % 8), (t_ // 8) + 256 * (t_ % 8) + 255, 2)
                Sx.mm(pf[:, :], h2[0:64, tcols], w3[0:64, (2 * oo) * 512:(2 * oo + 1) * 512])
                Sx.mm(pbk[:, :], h2[0:64, tcols], w3[0:64, (2 * oo + 1) * 512:(2 * oo + 2) * 512])
                cp("act", ftmp[0], pf[:, :])
                tt("dve", ftmp[1], pbk[:, :], ftmp[0], ALU.add)
                stt(ftmp[2], pbk[:, :], -1.0, ftmp[0], ALU.mult, ALU.add)
                tt("pool", ftmp[1], ftmp[1], bsdv[:, oo, 0, :], ALU.add)
                tt("pool", ftmp[2], ftmp[2], bsdv[:, oo, 1, :], ALU.add)
                dc = dcb[t_ % 2]
                if oo == 0:
                    Sx.dma(dc, cd["decay"][:, t_, :])
                tt("pool", hsd4[:, 2 * oo, t_, :], ftmp[1], dc, ALU.mult)
                tt("dve", hsd4[:, 2 * oo + 1, t_, :], ftmp[2], dc, ALU.mult)
        kout = [wtmp, h1]
        h2b = h2.bitcast(BF16)
        k16b = [h2b[:, 0:2048], h2b[:, 2048:4096]]
        cbk = [featT[:, 0:512], featT[:, 512:1024]]
        assert o <= 152 * 1024, o
        SLAB = 152 * 1024
        slabs = [A16(SLAB + i * 8192, 4096) for i in range(3)]
        si = 0
        for oo in range(2):
            for a in range(8):
                sl = slabs[si % 3]; si += 1
                slv = sl.rearrange("p (r q j k) -> p r q j k", r=2, q=2, j=8)
                Sx.dma(slv, cd["fwd"][a])
                bset = [banks[4 * (a % 2) + i] for i in range(4)]
                for par in range(2):
                    for cs in range(2):
                        pk = bset[2 * par + cs]
                        for j in range(8):
                            Sx.mm(pk[:, :], slv[:, par, cs, j, :], hsd4[:, 2 * oo + cs, 8 * par + j, :], start=(j == 0), stop=(j == 7))
                Ac, As_, Bc, Bs = bset
                ko = kout[a % 2]
                sk = skb[:, oo * 512:(oo + 1) * 512]
                cp("act", cbk[0], Bc[:, :])
                cp("act", cbk[1], Bs[:, :])
                tt("dve", ko[:, 0:512], Ac[:, :], cbk[0], ALU.add)
                tt("pool", ko[:, 0:512], ko[:, 0:512], sk, ALU.add)
                tt("dve", ko[:, 1024:1536], Ac[:, :], cbk[0], ALU.subtract)
                tt("pool", ko[:, 1024:1536], ko[:, 1024:1536], sk, ALU.add)
                tt("dve", ko[:, 512:1024], As_[:, :], cbk[1], ALU.add)
                stt(ko[:, 1536:2048], As_[:, :], -1.0, cbk[1], ALU.mult, ALU.add)
                k16 = k16b[a % 2]
                cp("act", k16, ko)
                Sx.dma(ksp_s[oo, a], k16.rearrange("p (j c) -> p j c", j=4), q="pool")

        R1 = 0
        R3 = 32 * 1024
        R5 = 48 * 1024
        RX = 64 * 1024
        R7 = 152 * 1024
        R8 = 176 * 1024
        xnT = A16(R1, 8 * S).rearrange("p (k t) -> p k t", k=8)
        Pbuf = A16(R1, 32 * 512).rearrange("p (j c) -> p j c", j=32)
        ffT = A16(R1, 32 * 512).rearrange("p (j c) -> p j c", j=32)
        vT = A16(R3, 4 * S).rearrange("p (k t) -> p k t", k=4)
        hyT = vT
        attnT = A16(R5, 4 * S).rearrange("p (k t) -> p k t", k=4)
        qT = A16(RX, 4 * S).rearrange("p (k t) -> p k t", k=4)
        kT = A16(RX + 16384, 4 * S).rearrange("p (k t) -> p k t", k=4)
        Vp = A16(RX + 32768, 2 * 53 * 128).rearrange("p (h t c) -> p h t c", h=2, t=53)
        TMP = RX + 32768 + 27136 + 512
        zx = A16(RX, 16 * 1536).rearrange("p (t c) -> p t c", t=16)
        UB = RX + 49152
        ubuf = A32(UB, 2050)
        c1 = A32(UB + 8448, 1024)
        c2 = A32(UB + 8448 + 4096, 1024)
        c16 = A16(UB + 8448 + 8192, 2048)
        TL = RX
        hbuf = A32(TL, 4 * 1024).rearrange("p (t c) -> p t c", t=4); TL += 16384
        hnT = A16(TL, 8 * 512).rearrange("p (k t) -> p k t", k=8); TL += 8192
        xt = [A32(TL + i * 4096, 1024) for i in range(2)]; TL += 8192
        yt = [A32(TL + i * 4096, 1024) for i in range(2)]; TL += 8192
        xn16 = [A16(TL + i * 2048, 1024) for i in range(2)]; TL += 4096
        rl = [A32(TL + i * 2048, 512) for i in range(2)]; TL += 4096
        sqj = A16(TL, 1024); TL += 2048
        wbuf = [A16(R7 + i * 2048, 1024) for i in range(4)]
        dbuf = [A16(R7 + 8192 + i * 2048, 1024) for i in range(4)]
        kbuf = [A32(R8, 1024)]
        wi = [0]
        di = [0]

        def vp_tiles():
            tl = []
            for i in range(17):
                s0, s1 = max(0, 128 * i - 64), min(S, 128 * i + 64)
                tl.append((s0, 1, s1 - s0))
            for rho in range(4):
                for i in range(5):
                    j0, j1 = max(0, 128 * i - 64), min(512, 128 * i + 64)
                    tl.append((4 * j0 + rho, 4, j1 - j0))
            for r in range(16):
                tl.append((r, 16, 128))
            return tl

        VPT = vp_tiles()

        def cols(start, step, n):
            return slice(start, start + step * (n - 1) + 1, step)

        for s in range(NS):
            Sx.tag = f"s{s}p1"
            for t_ in range(16):
                xb = xt[t_ % 2]
                Sx.dma(xb, x_d[s, t_ * 128:(t_ + 1) * 128, :])
                ssq = stats[:, t_ % 2:t_ % 2 + 1]
                act(sqj, xb, AF.Square, accum_out=ssq)
                act(stats[:, 2 + t_ % 2:3 + t_ % 2], ssq, AF.Sqrt, scale=1.0 / D, bias=EPS)
                recip(stats[:, 4 + t_ % 2:5 + t_ % 2], stats[:, 2 + t_ % 2:3 + t_ % 2])
                rs = stats[:, 4 + t_ % 2:5 + t_ % 2]
                xb16 = xn16[t_ % 2]
                ts("dve", xb16, xb, rs, None, ALU.mult, extra=[rs])
                for g in range(2):
                    pbk = pb16(6 + g)
                    for j in range(4):
                        kc = g * 4 + j
                        Sx.tr(pbk[:, j * 128:(j + 1) * 128], xb16[:, kc * 128:(kc + 1) * 128], ident[:])
                    cp("act" if g == 0 else "dve", xnT[:, g * 4:(g + 1) * 4, t_ * 128:(t_ + 1) * 128],
                       pbk[:, 0:512].rearrange("p (j c) -> p j c", j=4))

            Sx.tag = f"s{s}p2"
            bki = [0]

            def inproj_chunk(fc, consumer):
                wb = wbuf[wi[0] % 4]; wi[0] += 1
                wv = wb.rearrange("p (k c) -> p k c", k=8)
                Sx.dma(wv, win_s[fc])
                for tg in range(4):
                    bk = banks[bki[0] % 4]; bki[0] += 1
                    for kc in range(8):
                        Sx.mm(bk[:, :], wv[:, kc, :], xnT[:, kc, tg * 512:(tg + 1) * 512], start=(kc == 0), stop=(kc == 7))
                    consumer(fc, tg, bk)

            QT = TMP
            cosq = A16(RX + 32768, S)
            cosk = A16(RX + 32768 + 2 * S, S)
            sinT = A16(RX + 32768 + 4 * S, S)
            Sx.dma(cosq, rope_s[0])
            Sx.dma(cosk, rope_s[1])
            Sx.dma(sinT, cd["sinT"])
            a_sb = [A32(QT + i * 2048, 512) for i in range(2)]
            sq16 = [A16(QT + 4096 + i * 1024, 512) for i in range(2)]
            a16 = [A16(QT + 6144 + i * 1024, 512) for i in range(2)]
            sd = [A32(QT + 8192 + i * 2048, 512) for i in range(2)]
            t1 = [A32(QT + 12288 + i * 2048, 512) for i in range(2)]
            t2 = [A32(QT + 16384 + i * 2048, 512) for i in range(2)]
            qi = [0]

            def qk_consumer(fc, tg, bk):
                i = qi[0] % 2; qi[0] += 1
                isk = fc >= 4
                dst = (kT if isk else qT)[:, fc % 4, tg * 512:(tg + 1) * 512]
                ctab = (cosk if isk else cosq)[:, tg * 512:(tg + 1) * 512]
                cp("act", a_sb[i], bk[:, :])
                act(sq16[i], bk[:, :], AF.Square)
                cp("dve", a16[i], a_sb[i])
                pm, pr = banks[4 + (qi[0] % 2) * 2], banks[5 + (qi[0] % 2) * 2]
                Sx.mm(pm[:, :], bm[:], sq16[i])
                Sx.mm(pr[:, :], rg[:, 1 if isk else 0, :], a16[i])
                act(t2[i], pm[:, :], AF.Ln, bias=epsc[:, 0:1], extra=[epsc[:, 0:1]])
                act(sd[i], t2[i], AF.Exp, scale=-0.5)
                tt("pool", t1[i], a_sb[i], ctab, ALU.mult)
                tt("dve", t2[i], pr[:, :], sinT[:, tg * 512:(tg + 1) * 512], ALU.mult)
                tt("pool", t1[i], t1[i], t2[i], ALU.add)
                tt("dve", dst, t1[i], sd[i], ALU.mult)

            def v_consumer(fc, tg, bk):
                cp("act", vT[:, fc - 8, tg * 512:(tg + 1) * 512], bk[:, :])

            for fc in range(8):
                inproj_chunk(fc, qk_consumer)
            for fc in range(8, 12):
                inproj_chunk(fc, v_consumer)

            Sx.tag = f"s{s}p3"
            AT = TMP
            PT = [A16(AT + i * 1024, 512) for i in range(6)]
            tot = [A32(AT + 6144 + i * 2048, 512) for i in range(2)]
            rden = [A32(AT + 10240 + i * 2048, 512) for i in range(2)]
            lnd = [A32(AT + 14336 + i * 2048, 512) for i in range(2)]
            memset("pool", Vp[:, :, :, 64:128], 1.0)
            items = []

            def add_vp_items(hp):
                for g0 in range(0, 53, 4):
                    g1 = min(53, g0 + 4)

                    def front(bk, g0=g0, g1=g1, hp=hp):
                        pbk = banks[bk][:, :].bitcast(BF16)
                        for ti in range(g0, g1):
                            st0, stp, n = VPT[ti]
                            Sx.tr(pbk[0:n, (ti - g0) * 128:(ti - g0 + 1) * 128], vT[:, hp, cols(st0, stp, n)], ident[:])

                    def back(bk, k, g0=g0, g1=g1):
                        pbk = banks[bk][:, :].bitcast(BF16)
                        full = all(VPT[ti][2] == 128 for ti in range(g0, g1))
                        if full:
                            cp("dve" if (g0 // 4) % 2 else "act", Vp[:, :, g0:g1, 0:64],
                               pbk[:, 0:(g1 - g0) * 128].rearrange("p (t h e) -> p h t e", h=2, e=64))
                        else:
                            for ti in range(g0, g1):
                                n = VPT[ti][2]
                                cp("dve" if ti % 2 else "act", Vp[0:n, :, ti, 0:64],
                                   pbk[0:n, (ti - g0) * 128:(ti - g0 + 1) * 128].rearrange("p (h e) -> p h e", h=2))
                    items.append((front, back))

            def add_batch(hp, hh, c, smm, mask, pvs, state, last):
                def front(bk, smm=smm):
                    sbk = banks[bk]
                    for (n, cs, l, r) in smm:
                        Sx.mm(sbk[0:n, cs], l, r)

                def back(bk, k, mask=mask, pvs=pvs, state=state, last=last, hp=hp, hh=hh, c=c):
                    sbk = banks[bk]
                    pt = PT[k % 6]
                    act(pt, sbk[:, :], AF.Exp, scale=0.125)
                    tt("pool", pt, pt, mask, ALU.mult)
                    accs = (banks[0], banks[1], banks[2])
                    for (co, nk, ncl, vti, ai, acs) in pvs:
                        Sx.mm(accs[ai][:, acs], Vp[0:nk, hh, vti, :], pt[0:nk, co:co + ncl], start=state[ai], stop=False)
                        state[ai] = False
                    if last:
                        i2 = (2 * hp + hh + c) % 2
                        tv = tot[i2]
                        cp("dve", tv, accs[0][:, :])
                        tt("dve", tv.rearrange("p (j r) -> p j r", r=4), tv.rearrange("p (j r) -> p j r", r=4),
                           accs[1][:, :].rearrange("p (r j) -> p j r", r=4), ALU.add)
                        tt("dve", tv.rearrange("p (j r) -> p j r", r=16), tv.rearrange("p (j r) -> p j r", r=16),
                           accs[2][:, :].rearrange("p (r j) -> p j r", r=16), ALU.add)

                        def fin(hp=hp, hh=hh, c=c, i2=i2):
                            r0_, r1_ = hh * 64, hh * 64 + 64
                            tv, rv, lv = tot[i2], rden[i2], lnd[i2]
                            act(lv[0:64, :], tv[64:128, :], AF.Ln)
                            act(rv[0:64, :], lv[0:64, :], AF.Exp, scale=-1.0)
                            tt("pool", attnT[r0_:r1_, hp, c * 512:(c + 1) * 512], tv[0:64, :], rv[0:64, :], ALU.mult)
                        deferred.append((k + 2, fin))
                items.append((front, back))

            for hp in range(4):
                add_vp_items(hp)
                for hh in range(2):
                    r0_, r1_ = hh * 64, hh * 64 + 64
                    qh = qT[r0_:r1_, hp, :]
                    kh = kT[r0_:r1_, hp, :]
                    for c in range(4):
                        state = [True, True, True]
                        for half in range(2):
                            smm, pvs = [], []
                            for qa in range(2):
                                a = 4 * c + 2 * half + qa
                                for kt_ in range(2):
                                    ti = a + kt_
                                    st0, stp, n = VPT[ti]
                                    co = (qa * 2 + kt_) * 128
                                    smm.append((n, slice(co, co + 128), kh[:, cols(st0, 1, n)], qh[:, a * 128:(a + 1) * 128]))
                                    pvs.append((co, n, 128, ti, 0, slice((2 * half + qa) * 128, (2 * half + qa + 1) * 128)))
                            add_batch(hp, hh, c, smm, (maskA if (c == 0 and half == 0) else maskG)[:], pvs, state, False)
                        for half in range(2):
                            smm, pvs = [], []
                            for qa in range(2):
                                rho = 2 * half + qa
                                for kt_ in range(2):
                                    ti = 17 + 5 * rho + c + kt_
                                    st0, stp, n = VPT[ti]
                                    co = (qa * 2 + kt_) * 128
                                    smm.append((n, slice(co, co + 128), kh[:, cols(st0, 4, n)], qh[:, cols(512 * c + rho, 4, 128)]))
                                    pvs.append((co, n, 128, ti, 1, slice(rho * 128, (rho + 1) * 128)))
                            add_batch(hp, hh, c, smm, (maskB if c == 0 else maskG)[:], pvs, state, False)
                        smm, pvs = [], []
                        for r in range(16):
                            smm.append((128, slice(r * 32, (r + 1) * 32), kh[:, cols(r, 16, 128)], qh[:, cols(512 * c + r, 16, 32)]))
                            pvs.append((r * 32, 128, 32, 37 + r, 2, slice(r * 32, (r + 1) * 32)))
                        add_batch(hp, hh, c, smm, mb3[:, c * 512:(c + 1) * 512], pvs, state, True)
            LA = 3
            deferred = []
            for i in range(len(items) + LA):
                if i < len(items):
                    items[i][0](3 + i % 5)
                if i - LA >= 0:
                    items[i - LA][1](3 + (i - LA) % 5, i - LA)
                    while deferred and deferred[0][0] <= i - LA:
                        deferred.pop(0)[1]()
            while deferred:
                deferred.pop(0)[1]()

            if dbg and "d_attn" in dbg_d and s == 0:
                for kc in range(4):
                    dtmp = A32(TMP, 2048)
                    cp("dve", dtmp, attnT[:, kc, :])
                    Sx.dma(dbg_d["d_attn"][kc * 128:(kc + 1) * 128, :], dtmp)

            Sx.tag = f"s{s}p4"
            memset("pool", ubuf[:, 0:1], 0.0)
            memset("pool", ubuf[:, 2049:2050], 0.0)
            tri = [0]

            def hy_consumer(fc, tg, bk):
                cp("act", ubuf[:, 1 + tg * 512:1 + (tg + 1) * 512], bk[:, :])
                if tg != 3:
                    return
                ch = fc - 12
                w = lambda k: convw[:, ch * 4 + k:ch * 4 + k + 1]
                for hf in range(2):
                    u0 = 1024 * hf
                    ts("dve", c1, ubuf[:, u0:u0 + 1024], w(0), w(3), ALU.mult, ALU.add, extra=[w(0), w(3)])
                    stt(c2, ubuf[:, u0 + 1:u0 + 1025], w(1), c1, ALU.mult, ALU.add, extra=[w(1)])
                    stt(c16[:, u0:u0 + 1024], ubuf[:, u0 + 2:u0 + 1026], w(2), c2, ALU.mult, ALU.add, extra=[w(2)])
                for g in range(4):
                    pbk = pb16(4 + tri[0] % 4); tri[0] += 1
                    for j in range(4):
                        t_ = 4 * g + j
                        Sx.tr(pbk[:, j * 128:(j + 1) * 128], c16[:, cols((t_ // 8) + 256 * (t_ % 8), 2, 128)], ident[:])
                    cp("act" if g % 2 else "dve", zx[:, 4 * g:4 * g + 4, ch * 128:(ch + 1) * 128],
                       pbk[:, 0:512].rearrange("p (j c) -> p j c", j=4))

            sqFa = A16(RX + 71680, 4 * 2048).rearrange("p (g k t) -> p g k t", g=4, k=4)
            for tg in range(4):
                tgs = slice(tg * 512, (tg + 1) * 512)
                tt("pool", sqFa[:, tg], attnT[:, :, tgs], attnT[:, :, tgs], ALU.mult)

            def attn_norm(tgl):
                lnb = A32(R8, 512)
                rab = A32(R8 + 2048, 512)
                for tg in tgl:
                    tgs = slice(tg * 512, (tg + 1) * 512)
                    sqF = sqFa[:, tg]
                    pbc = banks[6 + tg % 2]
                    for kc in range(4):
                        Sx.mm(pbc[:, :], onesM[:], sqF[:, kc, :], start=(kc == 0), stop=(kc == 3))
                    act(lnb, pbc[:, :], AF.Ln, scale=1.0 / 512, bias=epsc[:, 0:1], extra=[epsc[:, 0:1]])
                    act(rab, lnb, AF.Exp, scale=-0.5)
                    for kc in range(4):
                        tt("pool" if kc % 2 else "dve", attnT[:, kc, tgs], attnT[:, kc, tgs], rab, ALU.mult)

            for fc in range(12, 24):
                inproj_chunk(fc, hy_consumer)
                if fc == 12:
                    attn_norm((0, 1))
                if fc == 13:
                    attn_norm((2, 3))

            Sx.tag = f"s{s}p5"
            DT = RX + 49152
            cB = [[A32(DT + (2 * i + j) * 2048, 512) for j in range(2)] for i in range(2)]
            Zb = [[A16(DT + 8192 + (4 * i + j) * 1024, 512) for j in range(4)] for i in range(2)]
            mt = [A32(DT + 16384 + i * 2048, 512) for i in range(4)]
            Pr = [A32(DT + 24576 + i * 2048, 512) for i in range(4)]
            kb2s = [A16(DT + 32768 + i * 2048, 1024) for i in range(2)]
            kb1s = [kbuf[0].bitcast(BF16)[:, i * 1024:(i + 1) * 1024] for i in range(2)]
            z2b = [A16(DT + 36864 + i * 1024, 512) for i in range(4)]
            Pv = Pbuf.rearrange("p (e q a) c -> p e q a c", e=2, q=2)
            slab_i = [0]
            for oo in range(2):
                def mmA(a, oo=oo):
                    sl = slabs[slab_i[0] % 3]; slab_i[0] += 1
                    slv = sl.rearrange("p (r q j k) -> p r q j k", r=2, q=2, j=8)
                    Sx.dma(slv, cd["fwd"][a])
                    Sx.dma(kb1s[a % 2].rearrange("p (j c) -> p j c", j=2), ksp_s[oo, a, :, 0:2, :], q="pool")
                    Sx.dma(kb2s[a % 2].rearrange("p (j c) -> p j c", j=2), ksp_s[oo, a, :, 2:4, :], q="pool")
                    bset = [banks[4 * (a % 2) + i] for i in range(4)]
                    for par_ in range(2):
                        for cs in range(2):
                            pk = bset[2 * par_ + cs]
                            for j in range(8):
                                Sx.mm(pk[:, :], slv[:, par_, cs, j, :], zx[:, 8 * par_ + j, 0:512], start=(j == 0), stop=(j == 7))

                def st1(a):
                    bset = [banks[4 * (a % 2) + i] for i in range(4)]
                    Br, Bi = cB[a % 2]
                    Z = Zb[a % 2]
                    cp("act", Br, bset[2][:, :])
                    cp("act", Bi, bset[3][:, :])
                    tt("dve", Z[0], bset[0][:, :], Br, ALU.add)
                    tt("dve", Z[1], bset[1][:, :], Bi, ALU.add)
                    tt("dve", Z[2], bset[0][:, :], Br, ALU.subtract)
                    stt(Z[3], bset[1][:, :], -1.0, Bi, ALU.mult, ALU.add)

                def st2(a):
                    Z = Zb[a % 2]
                    for half, (kk_, zz) in enumerate(((kb1s[a % 2], Z[0:2]), (kb2s[a % 2], Z[2:4]))):
                        kre, kim = kk_[:, 0:512], kk_[:, 512:1024]
                        tt("dve", mt[0], zz[0], kre, ALU.mult)
                        tt("dve", mt[1], zz[1], kim, ALU.mult)
                        tt("dve", mt[2], zz[0], kim, ALU.mult)
                        tt("dve", mt[3], zz[1], kre, ALU.mult)
                        tt("dve", Pr[2 * half], mt[0], mt[1], ALU.subtract)
                        tt("dve", Pr[2 * half + 1], mt[2], mt[3], ALU.add)
                    tt("dve", Pv[:, 0, 0, a, :], Pr[0], Pr[2], ALU.add)
                    tt("dve", Pv[:, 0, 1, a, :], Pr[1], Pr[3], ALU.subtract)
                    tt("dve", Pv[:, 1, 0, a, :], Pr[0], Pr[2], ALU.subtract)
                    tt("dve", Pv[:, 1, 1, a, :], Pr[1], Pr[3], ALU.add)

                for a in range(8):
                    mmA(a)
                    st1(a)
                    if a >= 1:
                        st2(a - 1)
                st2(7)

                def mmI(b):
                    sl = slabs[slab_i[0] % 3]; slab_i[0] += 1
                    slv = sl[:, 0:2048].rearrange("p (q a t) -> p q a t", q=2, a=8)
                    Sx.dma(slv, cd["inv"][b])
                    py = banks[b % 4]
                    e_ = b // 8
                    order = [(q_, a) for a in range(8) for q_ in range(2)]
                    for n, (q_, a) in enumerate(order):
                        Sx.mm(py[:, :], slv[:, q_, a, :], Pv[:, e_, q_, a, :], start=(n == 0), stop=(n == 15))

                def part1(b, oo=oo):
                    py = banks[b % 4]
                    if oo == 0:
                        tt("dve", zx[:, b, 0:512], py[:, :], zx[:, b, 512:1024], ALU.mult)
                        return
                    zb = z2b[b % 4]
                    z2t = (mt + Pr)[b % 8]
                    tt("dve", z2t, py[:, :], zx[:, b, 1024:1536], ALU.mult)
                    ssq = stats[:, 16 + b % 4:17 + b % 4]
                    rsq = stats[:, 20 + b % 4:21 + b % 4]
                    act(zb, z2t, AF.Square, accum_out=ssq)
                    act(rsq, ssq, AF.Ln, scale=1.0 / 512, bias=epsc[:, 0:1], extra=[epsc[:, 0:1]])
                    act(rsq, rsq, AF.Exp, scale=-0.5)
                    ts("dve", zb, z2t, rsq, None, ALU.mult, extra=[rsq])

                def part2(b):
                    zb = z2b[b % 4]
                    pbk = pb16(4 + b % 2)
                    for j in range(4):
                        Sx.tr(pbk[:, j * 128:(j + 1) * 128], zb[:, j * 128:(j + 1) * 128], ident[:])
                    cp("act", hyT[:, :, cols((b // 8) + 256 * (b % 8), 2, 128)], pbk[:, 0:512].rearrange("p (j c) -> p j c", j=4))

                for b_ in range(16):
                    mmI(b_)
                    part1(b_)
                    if oo == 1 and b_ >= 2:
                        part2(b_ - 2)
                if oo == 1:
                    part2(14)
                    part2(15)

            if dbg and "d_hy" in dbg_d and s == 0:
                for kc in range(4):
                    dtmp = A32(DT, 2048)
                    cp("dve", dtmp, hyT[:, kc, :])
                    Sx.dma(dbg_d["d_hy"][kc * 128:(kc + 1) * 128, :], dtmp)

            Sx.tag = f"s{s}p6"
            for g in range(4):
                def oproj(j, g=g):
                    t_ = 4 * g + j
                    tsl = slice(t_ * 128, (t_ + 1) * 128)
                    Sx.dma(xt[t_ % 2], x_d[s, tsl, :])
                    for hf in range(2):
                        pa = banks[2 * (j % 2) + hf]
                        for kc in range(8):
                            src = attnT if kc < 4 else hyT
                            Sx.mm(pa[:, :], src[:, kc % 4, tsl], wout[:, kc, hf * 512:(hf + 1) * 512], start=(kc == 0), stop=(kc == 7))

                def post(j, g=g):
                    t_ = 4 * g + j
                    xb = xt[t_ % 2]
                    for hf in range(2):
                        pa = banks[2 * (j % 2) + hf]
                        tt("dve", hbuf[:, j, hf * 512:(hf + 1) * 512], pa[:, :], xb[:, hf * 512:(hf + 1) * 512], ALU.add)
                    ssq = stats[:, 34 + t_ % 2:35 + t_ % 2]
                    act(sqj, hbuf[:, j, :], AF.Square, accum_out=ssq)
                    rf = stats[:, 36 + t_ % 2:37 + t_ % 2]
                    act(rf, ssq, AF.Ln, scale=1.0 / D, bias=epsc[:, 0:1], extra=[epsc[:, 0:1]])
                    act(rf, rf, AF.Exp, scale=-0.5)
                    xb16 = xn16[t_ % 2]
                    ts("dve", xb16, hbuf[:, j, :], rf, None, ALU.mult, extra=[rf])
                    for gg in range(2):
                        pbk = pb16(4 + gg)
                        for jj in range(4):
                            kc = gg * 4 + jj
                            Sx.tr(pbk[:, jj * 128:(jj + 1) * 128], xb16[:, kc * 128:(kc + 1) * 128], ident[:])
                        cp("act" if gg == 0 else "dve", hnT[:, gg * 4:(gg + 1) * 4, j * 128:(j + 1) * 128],
                           pbk[:, 0:512].rearrange("p (j c) -> p j c", j=4))

                oproj(0)
                oproj(1)
                post(0)
                oproj(2)
                post(1)
                oproj(3)
                post(2)
                post(3)
                for fch in range(32):
                    wb = wbuf[wi[0] % 4]; wi[0] += 1
                    wv = wb.rearrange("p (k c) -> p k c", k=8)
                    Sx.dma(wv, wup_s[fch])
                    bk = banks[6 + fch % 2]
                    for kc in range(8):
                        Sx.mm(bk[:, :], wv[:, kc, :], hnT[:, kc, :], start=(kc == 0), stop=(kc == 7))
                    rb = rl[fch % 2]
                    act(rb, bk[:, :], AF.Relu)
                    tt("pool", ffT[:, fch, :], rb, rb, ALU.mult)
                for fch in range(32):
                    db = dbuf[di[0] % 4]; di[0] += 1
                    Sx.dma(db, wdn_s[fch])
                    for j in range(4):
                        for hf in range(2):
                            Sx.mm(banks[2 * j + hf][:, :], ffT[:, fch, j * 128:(j + 1) * 128], db[:, hf * 512:(hf + 1) * 512],
                                  start=(fch == 0), stop=(fch == 31))
                for j in range(4):
                    t_ = 4 * g + j
                    yb = yt[t_ % 2]
                    for hf in range(2):
                        tt("dve", yb[:, hf * 512:(hf + 1) * 512], banks[2 * j + hf][:, :], hbuf[:, j, hf * 512:(hf + 1) * 512], ALU.add)
                    Sx.dma(y_d[s, t_ * 128:(t_ + 1) * 128, :], yb, q="pool")

        Sx.emit()
        import os
        if os.environ.get("KDUMP"):
            import json
            json.dump({e: [Sx.ops[i]["tag"] for i in Sx.streams[e]] for e in ENGS}, open(os.environ["KDUMP"], "w"))
    return nc


_PROG = {}


def _get_prog(NS):
    if NS not in _PROG:
        _PROG[NS] = build_program(NS)
    return _PROG[NS]


def kernel(**inputs):
    xp = np.asarray(inputs["x_prompt"], dtype=np.float32)
    xs = np.asarray(inputs["x_sample"], dtype=np.float32)
    xall = np.concatenate([xp, xs], axis=0)
    nseq = xall.shape[0]
    NS = nseq // NCORES
    consts = make_consts()
    params = layout_params(inputs)
    nc = _get_prog(NS)
    in_maps = []
    for c in range(NCORES):
        m = {"x": np.ascontiguousarray(xall[c * NS:(c + 1) * NS])}
        m.update(consts)
        m.update(params)
        in_maps.append(m)
    res = run_bass_kernel_spmd(nc, in_maps, core_ids=list(range(NCORES)))
    yall = np.concatenate([np.asarray(r["y"]) for r in res.results], axis=0)
    nb = xp.shape[0]
    return (np.ascontiguousarray(yall[:nb]).astype(np.float32), np.ascontiguousarray(yall[nb:]).astype(np.float32))
```

```python
import contextlib
import numpy as np
import ml_dtypes
import concourse.bass as bass
import concourse.mybir as mybir
from concourse.bass_utils import run_bass_kernel_spmd

F32 = mybir.dt.float32
BF16 = mybir.dt.bfloat16
AF = mybir.ActivationFunctionType
ALU = mybir.AluOpType
NCORES = 8
S = 2048
D = 1024
EPS = 1e-6

ENGS = ("pe", "act", "dve", "pool", "sp")


def _interval(ap):
    pat = ap.ap
    name = ap.tensor.name
    esz = mybir.dt.size(ap.dtype)
    sp = str(ap.space).upper()
    off = ap.offset
    if "SB" in sp or "PSUM" in sp:
        row = pat[0][0]
        p0 = off // row
        lo = off - p0 * row
        ext = 0
        for st, n in pat[1:]:
            ext += abs(st) * (n - 1)
        if "PSUM" in sp:
            return name, 0, 2048, (p0 // 32) * 32, ((p0 + pat[0][1] + 31) // 32) * 32
        return name, lo * esz, (lo + ext + 1) * esz, p0, p0 + pat[0][1]
    ext = 0
    for st, n in pat:
        ext += abs(st) * (n - 1)
    return name, off * esz, (off + ext + 1) * esz, 0, 1


class Sched:
    def __init__(self, nc, n_dma_sems=8):
        self.nc = nc
        self.ops = []
        self.streams = {e: [] for e in ENGS}
        self.wr = {}
        self.rd = {}
        self.K = n_dma_sems
        self.notrack = set()
        self.tag = "setup"

    def _deps_for(self, opid, reads, writes):
        deps = set()
        ri = [_interval(a) for a in reads]
        wi = [_interval(a) for a in writes]
        for key, lo, hi, plo, phi in ri:
            if key in self.notrack:
                continue
            for (l, h, pl, ph, w) in self.wr.get(key, ()):
                if l < hi and lo < h and pl < phi and plo < ph:
                    deps.add(w)
        for key, lo, hi, plo, phi in wi:
            if key in self.notrack:
                continue
            for (l, h, pl, ph, w) in self.wr.get(key, ()):
                if l < hi and lo < h and pl < phi and plo < ph:
                    deps.add(w)
            for (l, h, pl, ph, r) in self.rd.get(key, ()):
                if l < hi and lo < h and pl < phi and plo < ph:
                    deps.add(r)
        deps.discard(opid)
        for key, lo, hi, plo, phi in wi:
            if key in self.notrack:
                continue
            wl = self.wr.setdefault(key, [])
            wl[:] = [x for x in wl if not (lo <= x[0] and x[1] <= hi and plo <= x[2] and x[3] <= phi)]
            wl.append((lo, hi, plo, phi, opid))
            rl = self.rd.setdefault(key, [])
            rl[:] = [x for x in rl if not (lo <= x[0] and x[1] <= hi and plo <= x[2] and x[3] <= phi)]
        eng = self.ops[opid]["eng"]
        isdma = self.ops[opid]["dma"]
        for key, lo, hi, plo, phi in ri:
            if key in self.notrack:
                continue
            rl = self.rd.setdefault(key, [])
            if not isdma:
                rl[:] = [x for x in rl if not (x[0] == lo and x[1] == hi and x[2] == plo and x[3] == phi
                                               and self.ops[x[4]]["eng"] == eng and not self.ops[x[4]]["dma"])]
            rl.append((lo, hi, plo, phi, opid))
        return deps

    def op(self, eng, fn, reads=(), writes=(), dma=False):
        opid = len(self.ops)
        rec = dict(eng=eng, fn=fn, deps=None, dma=dma, tag=self.tag)
        self.ops.append(rec)
        rec["deps"] = self._deps_for(opid, list(reads), list(writes))
        self.streams[eng].append(opid)
        return opid

    def dma(self, out, in_, q="sp"):
        return self.op(q, lambda e: e.dma_start(out=out, in_=in_), reads=[in_], writes=[out], dma=True)

    def mm(self, out, lhsT, rhs, start=True, stop=True):
        return self.op("pe", lambda e: e.matmul(out, lhsT, rhs, start=start, stop=stop, skip_group_check=True),
                       reads=[lhsT, rhs], writes=[out])

    def tr(self, out, in_, ident):
        return self.op("pe", lambda e: e.transpose(out, in_, ident), reads=[in_, ident], writes=[out])

    def emit(self):
        nc = self.nc
        ops = self.ops
        needed = set()
        for o in ops:
            for d in o["deps"]:
                if not (o["eng"] == "pe" and ops[d]["eng"] == "pe"):
                    needed.add(d)
        cnt = {e: 0 for e in ENGS}
        dcnt = {e: 0 for e in ENGS}
        for e in ENGS:
            for i in self.streams[e]:
                o = ops[i]
                o["thr"] = None
                if o["dma"]:
                    k = dcnt[e]
                    dcnt[e] += 1
                    o["sem"] = ("d", e, k % self.K)
                    o["val"] = 16 * (k // self.K + 1)
                    if k >= self.K:
                        o["thr"] = (("d", e, k % self.K), 16 * (k // self.K))
                elif i in needed:
                    cnt[e] += 1
                    o["sem"] = ("c", e, 0)
                    o["val"] = cnt[e]
                else:
                    o["sem"] = None
        with contextlib.ExitStack() as st:
            sems = {}
            for e in ENGS:
                sems[("c", e, 0)] = st.enter_context(nc.semaphore(f"c_{e}"))
                if dcnt[e]:
                    for k in range(self.K):
                        sems[("d", e, k)] = st.enter_context(nc.semaphore(f"d_{e}{k}"))
            block = st.enter_context(nc.Block())

            def run(e, engobj):
                waited = {}
                for i in self.streams[e]:
                    o = ops[i]
                    req = {}
                    for d in o["deps"]:
                        od = ops[d]
                        if od["eng"] == "pe" and e == "pe":
                            continue
                        s = od["sem"]
                        if s is not None and req.get(s, 0) < od["val"]:
                            req[s] = od["val"]
                    if o["thr"] is not None:
                        s, v = o["thr"]
                        if req.get(s, 0) < v:
                            req[s] = v
                    for s, v in req.items():
                        if waited.get(s, 0) < v:
                            engobj.wait_ge(sems[s], v)
                            waited[s] = v
                    ins = o["fn"](engobj)
                    if o["sem"] is not None:
                        ins.then_inc(sems[o["sem"]], 16 if o["dma"] else 1)
                if dcnt[e]:
                    for k in range(self.K):
                        n = len(range(k, dcnt[e], self.K))
                        if n:
                            engobj.wait_ge(sems[("d", e, k)], 16 * n)

            @block.tensor
            def _(eng):
                run("pe", eng)

            @block.scalar
            def _(eng):
                run("act", eng)

            @block.vector
            def _(eng):
                run("dve", eng)

            @block.gpsimd
            def _(eng):
                run("pool", eng)

            @block.sync
            def _(eng):
                run("sp", eng)


_CONST = None


def make_consts():
    global _CONST
    if _CONST is not None:
        return _CONST
    bf = ml_dtypes.bfloat16
    c = {}
    c["ident"] = np.eye(128, dtype=np.float32).astype(bf)
    bm = np.zeros((128, 128), np.float32)
    bm[:64, :64] = 1.0 / 64
    bm[64:, 64:] = 1.0 / 64
    c["bm"] = bm.astype(bf)
    r0 = np.zeros((128, 128), np.float32)
    for m in range(128):
        hb, e = (m // 64) * 64, m % 64
        if e < 32:
            r0[hb + e + 32, m] = -1.0
        else:
            r0[hb + e - 32, m] = 1.0
    c["r0"] = r0
    half = 32
    inv_freq = (np.float32(10000.0) ** (-(np.arange(half, dtype=np.float32) / np.float32(half)))).astype(np.float32)
    ang = (np.arange(S, dtype=np.float32)[:, None] * inv_freq[None, :]).astype(np.float32)
    cs = np.cos(ang.astype(np.float64)).astype(np.float32).T
    sn = np.sin(ang.astype(np.float64)).astype(np.float32).T
    c["cosT"] = np.tile(cs, (4, 1))
    c["sinT"] = np.tile(sn, (4, 1)).astype(bf)
    kk = np.arange(128)[:, None]
    qq = np.arange(128)[None, :]
    m1 = (qq <= kk).astype(np.float32)
    m2 = (qq >= kk).astype(np.float32)
    m1e = np.zeros((128, 128), np.float32)
    m1e[:64] = m1[64:]
    c["maskG"] = np.concatenate([m1, m2, m1, m2], 1).astype(bf)
    c["maskA"] = np.concatenate([m1e, m2, m1, m2], 1).astype(bf)
    c["maskB"] = np.concatenate([m1e, m2, m1e, m2], 1).astype(bf)
    mb = (np.abs(qq - kk) <= 64).astype(np.float32)
    mb3 = np.zeros((128, 4, 16, 32), np.float32)
    for cc in range(4):
        mb3[:, cc, :, :] = mb[:, None, 32 * cc:32 * cc + 32]
    c["mb3"] = mb3.reshape(128, 4 * 512).astype(bf)
    par = np.arange(2, dtype=np.int64)[:, None, None]
    jj = np.arange(8, dtype=np.int64)[None, :, None]
    pp = np.arange(128, dtype=np.int64)[None, None, :]
    tok = par + 256 * jj + 2 * pp
    c["tok"] = tok
    k = np.arange(1024, dtype=np.int64)
    m = ((2 * k[None, None, None, :] + 1) * tok[..., None]) % 8192
    th = m.astype(np.float64) * (2.0 * np.pi / 8192.0)
    C = np.cos(th).reshape(2, 8, 128, 8, 128)
    Sn = np.sin(th).reshape(2, 8, 128, 8, 128)
    cf = C.transpose(3, 2, 0, 1, 4)
    sf = (-Sn).transpose(3, 2, 0, 1, 4)
    c["fwd"] = np.ascontiguousarray(np.stack([cf, sf], 3)).astype(bf)
    sc = 2.0 / 4096.0
    ci = (sc * C).transpose(0, 1, 4, 3, 2).reshape(16, 128, 8, 128)
    si = (-sc * Sn).transpose(0, 1, 4, 3, 2).reshape(16, 128, 8, 128)
    c["inv"] = np.ascontiguousarray(np.stack([ci, si], 2)).astype(bf)
    L = S
    pos = np.arange(L, dtype=np.float32)
    tt = (pos / np.float32(L - 1)).astype(np.float32)
    bands = 16
    fr = np.linspace(1e-4, bands - 1, bands, dtype=np.float32)
    angf = (np.float32(2.0 * np.pi) * pos[:, None] * fr[None, :] / np.float32(L)).astype(np.float32)
    feat = np.concatenate([tt[:, None], np.cos(angf.astype(np.float64)).astype(np.float32),
                           -np.sin(angf.astype(np.float64)).astype(np.float32)], -1)
    c["featT"] = np.ascontiguousarray(feat.T).astype(np.float32)
    deltas = np.linspace(np.log(1e-2) / 0.3, np.log(1e-2) / 1.5, 512, dtype=np.float32)
    decay = np.exp(-(tt[:, None] * np.abs(deltas)[None, :]).astype(np.float32).astype(np.float64)).astype(np.float32)
    c["decay"] = np.ascontiguousarray(decay[tok.reshape(16, 128)].transpose(1, 0, 2))
    del c["tok"]
    _CONST = c
    return c


CONST_SPECS = [("ident", [128, 128], BF16), ("bm", [128, 128], BF16), ("r0", [128, 128], F32),
               ("cosT", [128, S], F32), ("sinT", [128, S], BF16),
               ("maskG", [128, 512], BF16), ("maskA", [128, 512], BF16), ("maskB", [128, 512], BF16),
               ("mb3", [128, 2048], BF16), ("fwd", [8, 128, 2, 2, 8, 128], BF16),
               ("inv", [16, 128, 2, 8, 128], BF16), ("featT", [33, S], F32), ("decay", [128, 16, 512], F32)]

PARAM_SPECS = [("w_in", [D, 3072]), ("w_out", [D, D]), ("w_up", [D, 4096]), ("w_down", [4096, D]),
               ("g_mix", [128, 8]), ("g_ffn", [128, 8]), ("g_out", [128, 8]), ("g_qk", [128, 2]),
               ("convw", [128, 48]), ("flt_w1", [33, 64]), ("flt_w2", [64, 64]), ("flt_w3", [64, 2048]),
               ("flt_v", [64, 4]), ("flt_b3", [1, 2048]), ("hy_skip", [1, 1024])]


def layout_params(p):
    f = lambda a: np.ascontiguousarray(np.asarray(a, dtype=np.float32))
    out = {}
    out["w_in"] = f(p["w_in"][0])
    out["w_out"] = f(p["w_out"][0])
    out["w_up"] = f(p["w_up"][0])
    out["w_down"] = f(p["w_down"][0])
    out["g_mix"] = f(np.asarray(p["mix_norm"][0]).reshape(8, 128).T)
    out["g_ffn"] = f(np.asarray(p["ffn_norm"][0]).reshape(8, 128).T)
    gcat = np.concatenate([np.asarray(p["attn_out_norm"][0]), np.asarray(p["hy_out_norm"][0])])
    out["g_out"] = f(gcat.reshape(8, 128).T)
    gq = np.tile(np.asarray(p["q_norm"][0]), 2)
    gk = np.tile(np.asarray(p["k_norm"][0]), 2)
    out["g_qk"] = f(np.stack([gq, gk], 1))
    cw = np.asarray(p["hy_conv_w"][0])
    cb = np.asarray(p["hy_conv_b"][0])
    cwb = np.concatenate([cw, cb[None, :]], 0)
    out["convw"] = f(cwb.reshape(4, 12, 128).transpose(2, 1, 0).reshape(128, 48))
    out["flt_w1"] = f(p["flt_w1"][0])
    out["flt_w2"] = f(p["flt_w2"][0])
    out["flt_w3"] = f(p["flt_w3"][0])
    out["flt_v"] = f(np.stack([np.asarray(p["flt_b1"][0]), np.asarray(p["flt_freq1"][0]),
                               np.asarray(p["flt_b2"][0]), np.asarray(p["flt_freq2"][0])], 1))
    out["flt_b3"] = f(np.asarray(p["flt_b3"][0])[None, :])
    out["hy_skip"] = f(np.asarray(p["hy_skip"][0]).reshape(1, 1024))
    return out


ARENA_BYTES = 180 * 1024


def build_program(NS, dbg=None):
    nc = bass.Bass("TRN2", target_bir_lowering=False)
    x_d = nc.dram_tensor("x", [NS, S, D], F32, kind="ExternalInput").ap()
    y_d = nc.dram_tensor("y", [NS, S, D], F32, kind="ExternalOutput").ap()
    cd = {n: nc.dram_tensor(n, shp, dt, kind="ExternalInput").ap() for n, shp, dt in CONST_SPECS}
    pd = {n: nc.dram_tensor(n, shp, F32, kind="ExternalInput").ap() for n, shp in PARAM_SPECS}
    win_s = nc.dram_tensor("win_s", [24, 128, 8, 128], BF16).ap()
    wup_s = nc.dram_tensor("wup_s", [32, 128, 8, 128], BF16).ap()
    wdn_s = nc.dram_tensor("wdn_s", [32, 128, 1024], BF16).ap()
    ksp_s = nc.dram_tensor("ksp_s", [2, 8, 128, 4, 512], BF16).ap()
    rope_s = nc.dram_tensor("rope_s", [2, 128, S], BF16).ap()
    dbg_d = {}
    if dbg:
        for n, shp in dbg.items():
            dbg_d[n] = nc.dram_tensor(n, shp, F32, kind="ExternalOutput").ap()

    with contextlib.ExitStack() as st:
        SB = lambda n, s, d: st.enter_context(nc.sbuf_tensor(n + "_sb", s, d))
        arena = SB("arena", [128, ARENA_BYTES // 2], BF16)
        banks = [st.enter_context(nc.psum_tensor(f"bank{i}", [128, 512], F32)) for i in range(8)]
        Sx = Sched(nc)
        Sx.notrack.update(["x"] + [n for n, _, _ in CONST_SPECS] + [n for n, _ in PARAM_SPECS])

        def A16(off, n):
            assert off % 4 == 0 and off + 2 * n <= ARENA_BYTES, (off, n)
            return arena[:, off // 2: off // 2 + n]

        def A32(off, n):
            assert off % 4 == 0 and off + 4 * n <= ARENA_BYTES, (off, n)
            return arena[:, off // 2: off // 2 + 2 * n].bitcast(F32)

        def pb16(b):
            return banks[b][:, :].bitcast(BF16)

        def act(out, in_, func, extra=(), **kw):
            w = [out] + ([kw["accum_out"]] if "accum_out" in kw else [])
            Sx.op("act", lambda e: e.activation(out, in_, func, **kw), reads=[in_] + list(extra), writes=w)

        NOPOOL = True

        def tt(eng, out, in0, in1, op):
            if NOPOOL and eng == "pool":
                eng = "dve"
            Sx.op(eng, lambda e: e.tensor_tensor(out, in0, in1, op), reads=[in0, in1], writes=[out])

        def ts(eng, out, in0, s1, s2, op0, op1=None, extra=()):
            if NOPOOL and eng == "pool":
                eng = "dve"
                if op1 == ALU.mult and s2 == 1.0:
                    op1, s2 = None, None
            if op1 is None:
                Sx.op(eng, lambda e: e.tensor_scalar(out, in0, s1, None, op0), reads=[in0] + list(extra), writes=[out])
            else:
                Sx.op(eng, lambda e: e.tensor_scalar(out, in0, s1, s2, op0, op1), reads=[in0] + list(extra), writes=[out])

        def stt(out, in0, sc, in1, op0, op1, extra=(), accum=None):
            w = [out] + ([accum] if accum is not None else [])
            if accum is None:
                Sx.op("dve", lambda e: e.scalar_tensor_tensor(out, in0, sc, in1, op0, op1),
                      reads=[in0, in1] + list(extra), writes=w)
            else:
                Sx.op("dve", lambda e: e.scalar_tensor_tensor(out, in0, sc, in1, op0, op1, accum_out=accum),
                      reads=[in0, in1] + list(extra), writes=w)

        def cp(eng, out, in_):
            if NOPOOL and eng == "pool":
                eng = "dve"
            if eng == "act":
                act(out, in_, AF.Copy)
            else:
                Sx.op(eng, lambda e: e.tensor_copy(out, in_), reads=[in_], writes=[out])

        def recip(out, in_):
            Sx.op("dve", lambda e: e.reciprocal(out, in_), reads=[in_], writes=[out])

        def memset(eng, ap, v):
            Sx.op(eng, lambda e: e.memset(ap, v), writes=[ap])

        ident = SB("ident", [128, 128], BF16)
        bm = SB("bm", [128, 128], BF16)
        rg = SB("rg", [128, 2, 128], BF16)
        maskG = SB("maskG", [128, 512], BF16)
        maskA = SB("maskA", [128, 512], BF16)
        maskB = SB("maskB", [128, 512], BF16)
        mb3 = SB("mb3", [128, 2048], BF16)
        wout = SB("wout", [128, 8, 1024], BF16)
        convw = SB("convw", [128, 48], F32)
        gsm = SB("gsm", [128, 32], F32)
        stats = SB("stats", [128, 64], F32)
        ones16 = SB("ones16", [128, 2], BF16)
        for n, tl in (("ident", ident), ("bm", bm), ("maskG", maskG), ("maskA", maskA),
                      ("maskB", maskB), ("mb3", mb3)):
            Sx.dma(tl[:], cd[n])
        Sx.dma(convw[:], pd["convw"])
        Sx.dma(gsm[:, 0:8], pd["g_mix"])
        Sx.dma(gsm[:, 8:16], pd["g_ffn"])
        Sx.dma(gsm[:, 16:24], pd["g_out"])
        Sx.dma(gsm[:, 24:26], pd["g_qk"])
        memset("pool", ones16[:], 1.0)
        onesM = SB("onesM", [128, 128], BF16)
        memset("pool", onesM[:], 1.0)
        epsc = SB("epsc", [128, 2], F32)
        memset("pool", epsc[:], EPS)

        o = 0
        r0f = A32(o, 128); o += 512
        cosf = A32(o, S); o += 4 * S
        Sx.dma(r0f, cd["r0"])
        Sx.dma(cosf, cd["cosT"])
        cos16 = [A16(o, S), A16(o + 2 * S, S)]
        for j in range(2):
            ts("dve", rg[:, j, :], r0f, gsm[:, 24 + j:25 + j], None, ALU.mult, extra=[gsm[:, 24 + j:25 + j]])
            ts("dve", cos16[j], cosf, gsm[:, 24 + j:25 + j], None, ALU.mult, extra=[gsm[:, 24 + j:25 + j]])
            Sx.dma(rope_s[j], cos16[j])

        o = 20 * 1024
        wst = [A32(o, 3072), A32(o + 12288, 3072)]
        o += 24576
        wcv = [A16(o, 3072), A16(o + 6144, 3072)]
        o += 12288
        it = 0

        def wscale(i, out, in_, g):
            if i % 2:
                ts("dve", out, in_, g, None, ALU.mult, extra=[g])
            else:
                ts("pool", out, in_, g, 1.0, ALU.mult, ALU.mult, extra=[g])

        for kc in range(8):
            b = it % 2; it += 1
            Sx.dma(wst[b], pd["w_in"][kc * 128:(kc + 1) * 128, :], q="pool")
            wscale(it, wcv[b], wst[b], gsm[:, kc:kc + 1])
            Sx.dma(win_s[:, :, kc, :].rearrange("f p c -> p f c"),
                   wcv[b].rearrange("p (f c) -> p f c", c=128), q="sp")
        for kc in range(8):
            for hh in range(2):
                b = it % 2; it += 1
                Sx.dma(wst[b][:, 0:2048], pd["w_up"][kc * 128:(kc + 1) * 128, hh * 2048:(hh + 1) * 2048], q="pool")
                wscale(it, wcv[b][:, 0:2048], wst[b][:, 0:2048], gsm[:, 8 + kc:9 + kc])
                Sx.dma(wup_s[hh * 16:(hh + 1) * 16, :, kc, :].rearrange("f p c -> p f c"),
                       wcv[b][:, 0:2048].rearrange("p (f c) -> p f c", c=128), q="sp")
        for fch in range(0, 32, 2):
            b = it % 2; it += 1
            Sx.dma(wst[b][:, 0:2048].rearrange("p (f c) -> p f c", c=1024),
                   pd["w_down"][fch * 128:(fch + 2) * 128, :].rearrange("(f p) c -> p f c", p=128), q="pool")
            if it % 2:
                cp("dve", wcv[b][:, 0:2048], wst[b][:, 0:2048])
            else:
                cp("act", wcv[b][:, 0:2048], wst[b][:, 0:2048])
            Sx.dma(wdn_s[fch:fch + 2, :, :].rearrange("f p c -> p f c"),
                   wcv[b][:, 0:2048].rearrange("p (f c) -> p f c", c=1024), q="sp")
        for kc in range(0, 8, 2):
            b = it % 2; it += 1
            Sx.dma(wst[b][:, 0:2048].rearrange("p (f c) -> p f c", c=1024),
                   pd["w_out"][kc * 128:(kc + 2) * 128, :].rearrange("(f p) c -> p f c", p=128), q="pool")
            for j in range(2):
                ts("dve", wout[:, kc + j, :], wst[b][:, j * 1024:(j + 1) * 1024], gsm[:, 16 + kc + j:17 + kc + j],
                   None, ALU.mult, extra=[gsm[:, 16 + kc + j:17 + kc + j]])

        o = 0
        featT = A32(o, S); o += 8192
        w1 = A32(o, 64); o += 256
        w2 = A32(o, 64); o += 256
        fv = A32(o, 8); o += 32
        w3 = A32(o, 2048); o += 8192
        h1 = A32(o, S); o += 8192
        h2 = A32(o, S); o += 8192
        wtmp = A32(o, S); o += 8192
        bsd = A32(o, 2048); o += 8192
        b3b = A32(o, 2048); o += 8192
        skb = A32(o, 1024); o += 4096
        dcb = [A32(o + i * 2048, 512) for i in range(2)]; o += 4096
        ftmp = [A32(o + i * 2048, 512) for i in range(4)]; o += 8192
        hsd = A16(o, 4 * 16 * 512); o += 65536
        assert o <= ARENA_BYTES, o
        hsd4 = hsd.rearrange("p (a t c) -> p a t c", a=4, t=16)
        Sx.dma(featT[0:33, :], cd["featT"])
        Sx.dma(w1[0:33, :], pd["flt_w1"])
        Sx.dma(w2[0:64, :], pd["flt_w2"])
        Sx.dma(fv[0:64, 0:4], pd["flt_v"])
        Sx.dma(w3[0:64, :], pd["flt_w3"])
        Sx.dma(b3b, pd["flt_b3"].partition_broadcast(128)[:, 0, :])
        Sx.dma(skb, pd["hy_skip"].partition_broadcast(128)[:, 0, :])
        tt("dve", fv[0:64, 4:5], fv[0:64, 0:1], fv[0:64, 1:2], ALU.mult)
        tt("dve", fv[0:64, 5:6], fv[0:64, 2:3], fv[0:64, 3:4], ALU.mult)
        b3v = b3b.rearrange("p (o d c) -> p o d c", o=2, d=2)
        bsdv = bsd.rearrange("p (o d c) -> p o d c", o=2, d=2)
        for oo in range(2):
            tt("pool", bsdv[:, oo, 0, :], b3v[:, oo, 0, :], b3v[:, oo, 1, :], ALU.add)
            tt("pool", bsdv[:, oo, 1, :], b3v[:, oo, 0, :], b3v[:, oo, 1, :], ALU.subtract)
        PI = float(np.pi)

        def sin_layer(dst, src_w, src_k, rhs_t, fcol, fbcol):
            for tg in range(4):
                bk = banks[tg]
                Sx.mm(bk[0:64, :], src_w, rhs_t[:, tg * 512:(tg + 1) * 512])
                ts("dve", wtmp[0:64, tg * 512:(tg + 1) * 512], bk[0:64, :], fv[0:64, fcol:fcol + 1],
                   fv[0:64, fbcol:fbcol + 1], ALU.mult, ALU.add, extra=[fv[0:64, fcol:fcol + 1], fv[0:64, fbcol:fbcol + 1]])
            w_ = wtmp[0:64, :]
            m_ = dst[0:64, :]
            for _ in range(2):
                ts("dve", m_, w_, PI, 2 * PI, ALU.is_gt, ALU.mult)
                tt("dve", w_, w_, m_, ALU.subtract)
                ts("dve", m_, w_, -PI, 2 * PI, ALU.is_lt, ALU.mult)
                tt("dve", w_, w_, m_, ALU.add)
            act(m_, w_, AF.Sin)

        sin_layer(h1, w1[0:33, :], 33, featT[0:33, :], 1, 4)
        sin_layer(h2, w2[0:64, :], 64, h1[0:64, :], 3, 5)
        for t_ in range(16):
            for oo in range(2):
                pf, pbk = banks[4 + 2 * (t_ % 2)], banks[5 + 2 * (t_ % 2)]
                tcols = slice((t_ // 8) + 256 * (t_ % 8), (t_ // 8) + 256 * (t_ % 8) + 255, 2)
                Sx.mm(pf[:, :], h2[0:64, tcols], w3[0:64, (2 * oo) * 512:(2 * oo + 1) * 512])
                Sx.mm(pbk[:, :], h2[0:64, tcols], w3[0:64, (2 * oo + 1) * 512:(2 * oo + 2) * 512])
                cp("act", ftmp[0], pf[:, :])
                tt("dve", ftmp[1], pbk[:, :], ftmp[0], ALU.add)
                stt(ftmp[2], pbk[:, :], -1.0, ftmp[0], ALU.mult, ALU.add)
                tt("pool", ftmp[1], ftmp[1], bsdv[:, oo, 0, :], ALU.add)
                tt("pool", ftmp[2], ftmp[2], bsdv[:, oo, 1, :], ALU.add)
                dc = dcb[t_ % 2]
                if oo == 0:
                    Sx.dma(dc, cd["decay"][:, t_, :])
                tt("pool", hsd4[:, 2 * oo, t_, :], ftmp[1], dc, ALU.mult)
                tt("dve", hsd4[:, 2 * oo + 1, t_, :], ftmp[2], dc, ALU.mult)
        kout = [wtmp, h1]
        h2b = h2.bitcast(BF16)
        k16b = [h2b[:, 0:2048], h2b[:, 2048:4096]]
        cbk = [featT[:, 0:512], featT[:, 512:1024]]
        assert o <= 152 * 1024, o
        SLAB = 152 * 1024
        slabs = [A16(SLAB + i * 8192, 4096) for i in range(3)]
        si = 0
        for oo in range(2):
            for a in range(8):
                sl = slabs[si % 3]; si += 1
                slv = sl.rearrange("p (r q j k) -> p r q j k", r=2, q=2, j=8)
                Sx.dma(slv, cd["fwd"][a])
                bset = [banks[4 * (a % 2) + i] for i in range(4)]
                for par in range(2):
                    for cs in range(2):
                        pk = bset[2 * par + cs]
                        for j in range(8):
                            Sx.mm(pk[:, :], slv[:, par, cs, j, :], hsd4[:, 2 * oo + cs, 8 * par + j, :], start=(j == 0), stop=(j == 7))
                Ac, As_, Bc, Bs = bset
                ko = kout[a % 2]
                sk = skb[:, oo * 512:(oo + 1) * 512]
                cp("act", cbk[0], Bc[:, :])
                cp("act", cbk[1], Bs[:, :])
                tt("dve", ko[:, 0:512], Ac[:, :], cbk[0], ALU.add)
                tt("pool", ko[:, 0:512], ko[:, 0:512], sk, ALU.add)
                tt("dve", ko[:, 1024:1536], Ac[:, :], cbk[0], ALU.subtract)
                tt("pool", ko[:, 1024:1536], ko[:, 1024:1536], sk, ALU.add)
                tt("dve", ko[:, 512:1024], As_[:, :], cbk[1], ALU.add)
                stt(ko[:, 1536:2048], As_[:, :], -1.0, cbk[1], ALU.mult, ALU.add)
                k16 = k16b[a % 2]
                cp("act", k16, ko)
                Sx.dma(ksp_s[oo, a], k16.rearrange("p (j c) -> p j c", j=4), q="pool")

        R1 = 0
        R3 = 32 * 1024
        R5 = 48 * 1024
        RX = 64 * 1024
        R7 = 152 * 1024
        R8 = 176 * 1024
        xnT = A16(R1, 8 * S).rearrange("p (k t) -> p k t", k=8)
        Pbuf = A16(R1, 32 * 512).rearrange("p (j c) -> p j c", j=32)
        ffT = A16(R1, 32 * 512).rearrange("p (j c) -> p j c", j=32)
        vT = A16(R3, 4 * S).rearrange("p (k t) -> p k t", k=4)
        hyT = vT
        attnT = A16(R5, 4 * S).rearrange("p (k t) -> p k t", k=4)
        qT = A16(RX, 4 * S).rearrange("p (k t) -> p k t", k=4)
        kT = A16(RX + 16384, 4 * S).rearrange("p (k t) -> p k t", k=4)
        Vp = A16(RX + 32768, 2 * 53 * 128).rearrange("p (h t c) -> p h t c", h=2, t=53)
        TMP = RX + 32768 + 27136 + 512
        zx = A16(RX, 16 * 1536).rearrange("p (t c) -> p t c", t=16)
        UB = RX + 49152
        ubuf = A32(UB, 2050)
        c1 = A32(UB + 8448, 1024)
        c2 = A32(UB + 8448 + 4096, 1024)
        c16 = A16(UB + 8448 + 8192, 2048)
        TL = RX
        hbuf = A32(TL, 4 * 1024).rearrange("p (t c) -> p t c", t=4); TL += 16384
        hnT = A16(TL, 8 * 512).rearrange("p (k t) -> p k t", k=8); TL += 8192
        xt = [A32(TL + i * 4096, 1024) for i in range(2)]; TL += 8192
        yt = [A32(TL + i * 4096, 1024) for i in range(2)]; TL += 8192
        xn16 = [A16(TL + i * 2048, 1024) for i in range(2)]; TL += 4096
        rl = [A32(TL + i * 2048, 512) for i in range(2)]; TL += 4096
        sqj = A16(TL, 1024); TL += 2048
        wbuf = [A16(R7 + i * 2048, 1024) for i in range(4)]
        dbuf = [A16(R7 + 8192 + i * 2048, 1024) for i in range(4)]
        kbuf = [A32(R8, 1024)]
        wi = [0]
        di = [0]

        def vp_tiles():
            tl = []
            for i in range(17):
                s0, s1 = max(0, 128 * i - 64), min(S, 128 * i + 64)
                tl.append((s0, 1, s1 - s0))
            for rho in range(4):
                for i in range(5):
                    j0, j1 = max(0, 128 * i - 64), min(512, 128 * i + 64)
                    tl.append((4 * j0 + rho, 4, j1 - j0))
            for r in range(16):
                tl.append((r, 16, 128))
            return tl

        VPT = vp_tiles()

        def cols(start, step, n):
            return slice(start, start + step * (n - 1) + 1, step)

        for s in range(NS):
            Sx.tag = f"s{s}p1"
            for t_ in range(16):
                xb = xt[t_ % 2]
                Sx.dma(xb, x_d[s, t_ * 128:(t_ + 1) * 128, :])
                ssq = stats[:, t_ % 2:t_ % 2 + 1]
                act(sqj, xb, AF.Square, accum_out=ssq)
                act(stats[:, 2 + t_ % 2:3 + t_ % 2], ssq, AF.Sqrt, scale=1.0 / D, bias=EPS)
                recip(stats[:, 4 + t_ % 2:5 + t_ % 2], stats[:, 2 + t_ % 2:3 + t_ % 2])
                rs = stats[:, 4 + t_ % 2:5 + t_ % 2]
                xb16 = xn16[t_ % 2]
                ts("dve", xb16, xb, rs, None, ALU.mult, extra=[rs])
                for g in range(2):
                    pbk = pb16(6 + g)
                    for j in range(4):
                        kc = g * 4 + j
                        Sx.tr(pbk[:, j * 128:(j + 1) * 128], xb16[:, kc * 128:(kc + 1) * 128], ident[:])
                    cp("act" if g == 0 else "dve", xnT[:, g * 4:(g + 1) * 4, t_ * 128:(t_ + 1) * 128],
                       pbk[:, 0:512].rearrange("p (j c) -> p j c", j=4))

            Sx.tag = f"s{s}p2"
            bki = [0]

            def inproj_chunk(fc, consumer):
                wb = wbuf[wi[0] % 4]; wi[0] += 1
                wv = wb.rearrange("p (k c) -> p k c", k=8)
                Sx.dma(wv, win_s[fc])
                for tg in range(4):
                    bk = banks[bki[0] % 4]; bki[0] += 1
                    for kc in range(8):
                        Sx.mm(bk[:, :], wv[:, kc, :], xnT[:, kc, tg * 512:(tg + 1) * 512], start=(kc == 0), stop=(kc == 7))
                    nxt = consumer(fc, tg, bk)
                    for e_ in pend:
                        e_[0] -= 1
                    while pend and pend[0][0] <= 0:
                        pend.pop(0)[1]()
                    if nxt is not None:
                        pend.append([nxt[0], nxt[1]])

            pend = []

            def flush_pend():
                while pend:
                    pend.pop(0)[1]()

            QT = TMP
            cosq = A16(RX + 32768, S)
            cosk = A16(RX + 32768 + 2 * S, S)
            sinT = A16(RX + 32768 + 4 * S, S)
            Sx.dma(cosq, rope_s[0])
            Sx.dma(cosk, rope_s[1])
            Sx.dma(sinT, cd["sinT"])
            a_sb = [A32(QT + i * 2048, 512) for i in range(2)]
            sq16 = [A16(QT + 4096 + i * 1024, 512) for i in range(2)]
            a16 = [A16(QT + 6144 + i * 1024, 512) for i in range(2)]
            sd = [A32(QT + 8192 + i * 2048, 512) for i in range(2)]
            t1 = [A32(QT + 12288 + i * 2048, 512) for i in range(2)]
            t2 = [A32(QT + 16384 + i * 2048, 512) for i in range(2)]
            qi = [0]

            def qk_consumer(fc, tg, bk):
                i = qi[0] % 2; qi[0] += 1
                isk = fc >= 4
                dst = (kT if isk else qT)[:, fc % 4, tg * 512:(tg + 1) * 512]
                ctab = (cosk if isk else cosq)[:, tg * 512:(tg + 1) * 512]
                cp("act", a_sb[i], bk[:, :])
                act(sq16[i], bk[:, :], AF.Square)
                cp("dve", a16[i], a_sb[i])
                tt("dve", t1[i], a_sb[i], ctab, ALU.mult)
                pm, pr = banks[4 + (qi[0] % 2) * 2], banks[5 + (qi[0] % 2) * 2]

                def tail():
                    Sx.mm(pm[:, :], bm[:], sq16[i])
                    Sx.mm(pr[:, :], rg[:, 1 if isk else 0, :], a16[i])
                    act(t2[i], pm[:, :], AF.Ln, bias=epsc[:, 0:1], extra=[epsc[:, 0:1]])
                    act(sd[i], t2[i], AF.Exp, scale=-0.5)
                    tt("dve", t2[i], pr[:, :], sinT[:, tg * 512:(tg + 1) * 512], ALU.mult)
                    tt("dve", t1[i], t1[i], t2[i], ALU.add)
                    tt("dve", dst, t1[i], sd[i], ALU.mult)
                return (1, tail)

            def v_consumer(fc, tg, bk):
                cp("act", vT[:, fc - 8, tg * 512:(tg + 1) * 512], bk[:, :])

            for fc in range(8):
                inproj_chunk(fc, qk_consumer)
            for fc in range(8, 12):
                inproj_chunk(fc, v_consumer)
            flush_pend()

            Sx.tag = f"s{s}p3"
            AT = TMP
            PT = [A16(AT + i * 1024, 512) for i in range(6)]
            tot = [A32(AT + 6144 + i * 2048, 512) for i in range(2)]
            rden = [A32(AT + 10240 + i * 2048, 512) for i in range(2)]
            lnd = [A32(AT + 14336 + i * 2048, 512) for i in range(2)]
            memset("pool", Vp[:, :, :, 64:128], 1.0)
            items = []

            def add_vp_items(hp):
                for g0 in range(0, 53, 4):
                    g1 = min(53, g0 + 4)

                    def front(bk, g0=g0, g1=g1, hp=hp):
                        pbk = banks[bk][:, :].bitcast(BF16)
                        for ti in range(g0, g1):
                            st0, stp, n = VPT[ti]
                            Sx.tr(pbk[0:n, (ti - g0) * 128:(ti - g0 + 1) * 128], vT[:, hp, cols(st0, stp, n)], ident[:])

                    def back(bk, k, g0=g0, g1=g1):
                        pbk = banks[bk][:, :].bitcast(BF16)
                        full = all(VPT[ti][2] == 128 for ti in range(g0, g1))
                        if full:
                            cp("dve" if (g0 // 4) % 2 else "act", Vp[:, :, g0:g1, 0:64],
                               pbk[:, 0:(g1 - g0) * 128].rearrange("p (t h e) -> p h t e", h=2, e=64))
                        else:
                            for ti in range(g0, g1):
                                n = VPT[ti][2]
                                cp("dve" if ti % 2 else "act", Vp[0:n, :, ti, 0:64],
                                   pbk[0:n, (ti - g0) * 128:(ti - g0 + 1) * 128].rearrange("p (h e) -> p h e", h=2))
                    items.append((front, back))

            def add_batch(hp, hh, c, smm, mask, pvs, state, last):
                def front(bk, smm=smm):
                    sbk = banks[bk]
                    for (n, cs, l, r) in smm:
                        Sx.mm(sbk[0:n, cs], l, r)

                def back(bk, k, mask=mask, pvs=pvs, state=state, last=last, hp=hp, hh=hh, c=c):
                    sbk = banks[bk]
                    pt = PT[k % 6]
                    act(pt, sbk[:, :], AF.Exp, scale=0.125)
                    tt("pool", pt, pt, mask, ALU.mult)
                    accs = (banks[0], banks[1], banks[2])
                    for (co, nk, ncl, vti, ai, acs) in pvs:
                        Sx.mm(accs[ai][:, acs], Vp[0:nk, hh, vti, :], pt[0:nk, co:co + ncl], start=state[ai], stop=False)
                        state[ai] = False
                    if last:
                        i2 = (2 * hp + hh + c) % 2
                        tv = tot[i2]
                        cp("dve", tv, accs[0][:, :])
                        tt("dve", tv.rearrange("p (j r) -> p j r", r=4), tv.rearrange("p (j r) -> p j r", r=4),
                           accs[1][:, :].rearrange("p (r j) -> p j r", r=4), ALU.add)
                        tt("dve", tv.rearrange("p (j r) -> p j r", r=16), tv.rearrange("p (j r) -> p j r", r=16),
                           accs[2][:, :].rearrange("p (r j) -> p j r", r=16), ALU.add)

                        def fin(hp=hp, hh=hh, c=c, i2=i2):
                            r0_, r1_ = hh * 64, hh * 64 + 64
                            tv, rv, lv = tot[i2], rden[i2], lnd[i2]
                            act(lv[0:64, :], tv[64:128, :], AF.Ln)
                            act(rv[0:64, :], lv[0:64, :], AF.Exp, scale=-1.0)
                            tt("pool", attnT[r0_:r1_, hp, c * 512:(c + 1) * 512], tv[0:64, :], rv[0:64, :], ALU.mult)
                        deferred.append((k + 2, fin))
                items.append((front, back))

            for hp in range(4):
                add_vp_items(hp)
                for hh in range(2):
                    r0_, r1_ = hh * 64, hh * 64 + 64
                    qh = qT[r0_:r1_, hp, :]
                    kh = kT[r0_:r1_, hp, :]
                    for c in range(4):
                        state = [True, True, True]
                        for half in range(2):
                            smm, pvs = [], []
                            for qa in range(2):
                                a = 4 * c + 2 * half + qa
                                for kt_ in range(2):
                                    ti = a + kt_
                                    st0, stp, n = VPT[ti]
                                    co = (qa * 2 + kt_) * 128
                                    smm.append((n, slice(co, co + 128), kh[:, cols(st0, 1, n)], qh[:, a * 128:(a + 1) * 128]))
                                    pvs.append((co, n, 128, ti, 0, slice((2 * half + qa) * 128, (2 * half + qa + 1) * 128)))
                            add_batch(hp, hh, c, smm, (maskA if (c == 0 and half == 0) else maskG)[:], pvs, state, False)
                        for half in range(2):
                            smm, pvs = [], []
                            for qa in range(2):
                                rho = 2 * half + qa
                                for kt_ in range(2):
                                    ti = 17 + 5 * rho + c + kt_
                                    st0, stp, n = VPT[ti]
                                    co = (qa * 2 + kt_) * 128
                                    smm.append((n, slice(co, co + 128), kh[:, cols(st0, 4, n)], qh[:, cols(512 * c + rho, 4, 128)]))
                                    pvs.append((co, n, 128, ti, 1, slice(rho * 128, (rho + 1) * 128)))
                            add_batch(hp, hh, c, smm, (maskB if c == 0 else maskG)[:], pvs, state, False)
                        smm, pvs = [], []
                        for r in range(16):
                            smm.append((128, slice(r * 32, (r + 1) * 32), kh[:, cols(r, 16, 128)], qh[:, cols(512 * c + r, 16, 32)]))
                            pvs.append((r * 32, 128, 32, 37 + r, 2, slice(r * 32, (r + 1) * 32)))
                        add_batch(hp, hh, c, smm, mb3[:, c * 512:(c + 1) * 512], pvs, state, True)
            LA = 3
            deferred = []
            for i in range(len(items) + LA):
                if i < len(items):
                    items[i][0](3 + i % 5)
                if i - LA >= 0:
                    items[i - LA][1](3 + (i - LA) % 5, i - LA)
                    while deferred and deferred[0][0] <= i - LA:
                        deferred.pop(0)[1]()
            while deferred:
                deferred.pop(0)[1]()

            if dbg and "d_attn" in dbg_d and s == 0:
                for kc in range(4):
                    dtmp = A32(TMP, 2048)
                    cp("dve", dtmp, attnT[:, kc, :])
                    Sx.dma(dbg_d["d_attn"][kc * 128:(kc + 1) * 128, :], dtmp)

            Sx.tag = f"s{s}p4"
            memset("pool", ubuf[:, 0:1], 0.0)
            memset("pool", ubuf[:, 2049:2050], 0.0)
            tri = [0]

            ubufs = [ubuf, A32(RX + 71680, 2050)]
            c16s = [c16, A16(RX + 71680 + 8448, 2048)]
            memset("pool", ubufs[1][:, 0:1], 0.0)
            memset("pool", ubufs[1][:, 2049:2050], 0.0)

            def hy_consumer(fc, tg, bk):
                ch = fc - 12
                ub, cc = ubufs[ch % 2], c16s[ch % 2]
                cp("act", ub[:, 1 + tg * 512:1 + (tg + 1) * 512], bk[:, :])
                if tg != 3:
                    return None
                w = lambda k: convw[:, ch * 4 + k:ch * 4 + k + 1]
                for hf in range(2):
                    u0 = 1024 * hf
                    ts("dve", c1, ub[:, u0:u0 + 1024], w(0), w(3), ALU.mult, ALU.add, extra=[w(0), w(3)])
                    stt(c2, ub[:, u0 + 1:u0 + 1025], w(1), c1, ALU.mult, ALU.add, extra=[w(1)])
                    stt(cc[:, u0:u0 + 1024], ub[:, u0 + 2:u0 + 1026], w(2), c2, ALU.mult, ALU.add, extra=[w(2)])

                def tail():
                    for g in range(4):
                        pbk = pb16(4 + tri[0] % 4); tri[0] += 1
                        for j in range(4):
                            t_ = 4 * g + j
                            Sx.tr(pbk[:, j * 128:(j + 1) * 128], cc[:, cols((t_ // 8) + 256 * (t_ % 8), 2, 128)], ident[:])
                        cp("act" if g % 2 else "dve", zx[:, 4 * g:4 * g + 4, ch * 128:(ch + 1) * 128],
                           pbk[:, 0:512].rearrange("p (j c) -> p j c", j=4))
                return (4, tail)

            sqFa = A16(R3, 4 * 2048).rearrange("p (g k t) -> p g k t", g=4, k=4)
            for tg in range(4):
                tgs = slice(tg * 512, (tg + 1) * 512)
                tt("pool", sqFa[:, tg], attnT[:, :, tgs], attnT[:, :, tgs], ALU.mult)

            def attn_norm(tgl):
                lnb = A32(R8, 512)
                rab = A32(R8 + 2048, 512)
                for tg in tgl:
                    tgs = slice(tg * 512, (tg + 1) * 512)
                    sqF = sqFa[:, tg]
                    pbc = banks[6 + tg % 2]
                    for kc in range(4):
                        Sx.mm(pbc[:, :], onesM[:], sqF[:, kc, :], start=(kc == 0), stop=(kc == 3))
                    act(lnb, pbc[:, :], AF.Ln, scale=1.0 / 512, bias=epsc[:, 0:1], extra=[epsc[:, 0:1]])
                    act(rab, lnb, AF.Exp, scale=-0.5)
                    for kc in range(4):
                        tt("pool" if kc % 2 else "dve", attnT[:, kc, tgs], attnT[:, kc, tgs], rab, ALU.mult)

            for fc in range(12, 24):
                inproj_chunk(fc, hy_consumer)
                if fc == 12:
                    attn_norm((0, 1))
                if fc == 13:
                    attn_norm((2, 3))
            flush_pend()

            Sx.tag = f"s{s}p5"
            DT = RX + 49152
            cB = [[A32(DT + (2 * i + j) * 2048, 512) for j in range(2)] for i in range(2)]
            Zb = [[A16(DT + 8192 + (4 * i + j) * 1024, 512) for j in range(4)] for i in range(2)]
            mt = [A32(DT + 16384 + i * 2048, 512) for i in range(4)]
            Pr = [A32(DT + 24576 + i * 2048, 512) for i in range(4)]
            kb2s = [A16(DT + 32768 + i * 2048, 1024) for i in range(2)]
            kb1s = [kbuf[0].bitcast(BF16)[:, i * 1024:(i + 1) * 1024] for i in range(2)]
            z2b = [A16(DT + 36864 + i * 1024, 512) for i in range(4)]
            Pv = Pbuf.rearrange("p (e q a) c -> p e q a c", e=2, q=2)
            slab_i = [0]
            for oo in range(2):
                def mmA(a, oo=oo):
                    sl = slabs[slab_i[0] % 3]; slab_i[0] += 1
                    slv = sl.rearrange("p (r q j k) -> p r q j k", r=2, q=2, j=8)
                    Sx.dma(slv, cd["fwd"][a])
                    Sx.dma(kb1s[a % 2].rearrange("p (j c) -> p j c", j=2), ksp_s[oo, a, :, 0:2, :], q="pool")
                    Sx.dma(kb2s[a % 2].rearrange("p (j c) -> p j c", j=2), ksp_s[oo, a, :, 2:4, :], q="pool")
                    bset = [banks[4 * (a % 2) + i] for i in range(4)]
                    for par_ in range(2):
                        for cs in range(2):
                            pk = bset[2 * par_ + cs]
                            for j in range(8):
                                Sx.mm(pk[:, :], slv[:, par_, cs, j, :], zx[:, 8 * par_ + j, 0:512], start=(j == 0), stop=(j == 7))

                def st1(a):
                    bset = [banks[4 * (a % 2) + i] for i in range(4)]
                    Br, Bi = cB[a % 2]
                    Z = Zb[a % 2]
                    cp("act", Br, bset[2][:, :])
                    cp("act", Bi, bset[3][:, :])
                    tt("dve", Z[0], bset[0][:, :], Br, ALU.add)
                    tt("dve", Z[1], bset[1][:, :], Bi, ALU.add)
                    tt("dve", Z[2], bset[0][:, :], Br, ALU.subtract)
                    stt(Z[3], bset[1][:, :], -1.0, Bi, ALU.mult, ALU.add)

                def st2(a):
                    Z = Zb[a % 2]
                    for half, (kk_, zz) in enumerate(((kb1s[a % 2], Z[0:2]), (kb2s[a % 2], Z[2:4]))):
                        kre, kim = kk_[:, 0:512], kk_[:, 512:1024]
                        tt("dve", mt[0], zz[0], kre, ALU.mult)
                        tt("dve", mt[1], zz[1], kim, ALU.mult)
                        tt("dve", mt[2], zz[0], kim, ALU.mult)
                        tt("dve", mt[3], zz[1], kre, ALU.mult)
                        tt("dve", Pr[2 * half], mt[0], mt[1], ALU.subtract)
                        tt("dve", Pr[2 * half + 1], mt[2], mt[3], ALU.add)
                    tt("dve", Pv[:, 0, 0, a, :], Pr[0], Pr[2], ALU.add)
                    tt("dve", Pv[:, 0, 1, a, :], Pr[1], Pr[3], ALU.subtract)
                    tt("dve", Pv[:, 1, 0, a, :], Pr[0], Pr[2], ALU.subtract)
                    tt("dve", Pv[:, 1, 1, a, :], Pr[1], Pr[3], ALU.add)

                for a in range(8):
                    mmA(a)
                    st1(a)
                    if a >= 1:
                        st2(a - 1)
                st2(7)

                def mmI(b):
                    sl = slabs[slab_i[0] % 3]; slab_i[0] += 1
                    slv = sl[:, 0:2048].rearrange("p (q a t) -> p q a t", q=2, a=8)
                    Sx.dma(slv, cd["inv"][b])
                    py = banks[b % 4]
                    e_ = b // 8
                    order = [(q_, a) for a in range(8) for q_ in range(2)]
                    for n, (q_, a) in enumerate(order):
                        Sx.mm(py[:, :], slv[:, q_, a, :], Pv[:, e_, q_, a, :], start=(n == 0), stop=(n == 15))

                def part1(b, oo=oo):
                    py = banks[b % 4]
                    if oo == 0:
                        tt("dve", zx[:, b, 0:512], py[:, :], zx[:, b, 512:1024], ALU.mult)
                        return
                    zb = z2b[b % 4]
                    z2t = (mt + Pr)[b % 8]
                    tt("dve", z2t, py[:, :], zx[:, b, 1024:1536], ALU.mult)
                    ssq = stats[:, 16 + b % 4:17 + b % 4]
                    rsq = stats[:, 20 + b % 4:21 + b % 4]
                    act(zb, z2t, AF.Square, accum_out=ssq)
                    act(rsq, ssq, AF.Ln, scale=1.0 / 512, bias=epsc[:, 0:1], extra=[epsc[:, 0:1]])
                    act(rsq, rsq, AF.Exp, scale=-0.5)
                    ts("dve", zb, z2t, rsq, None, ALU.mult, extra=[rsq])

                def part2(b):
                    zb = z2b[b % 4]
                    pbk = pb16(4 + b % 2)
                    for j in range(4):
                        Sx.tr(pbk[:, j * 128:(j + 1) * 128], zb[:, j * 128:(j + 1) * 128], ident[:])
                    cp("act", hyT[:, :, cols((b // 8) + 256 * (b % 8), 2, 128)], pbk[:, 0:512].rearrange("p (j c) -> p j c", j=4))

                for b_ in range(16):
                    mmI(b_)
                    part1(b_)
                    if oo == 1 and b_ >= 2:
                        part2(b_ - 2)
                if oo == 1:
                    part2(14)
                    part2(15)

            if dbg and "d_hy" in dbg_d and s == 0:
                for kc in range(4):
                    dtmp = A32(DT, 2048)
                    cp("dve", dtmp, hyT[:, kc, :])
                    Sx.dma(dbg_d["d_hy"][kc * 128:(kc + 1) * 128, :], dtmp)

            Sx.tag = f"s{s}p6"
            for g in range(4):
                def oproj(j, g=g):
                    t_ = 4 * g + j
                    tsl = slice(t_ * 128, (t_ + 1) * 128)
                    Sx.dma(xt[t_ % 2], x_d[s, tsl, :])
                    for hf in range(2):
                        pa = banks[2 * (j % 2) + hf]
                        for kc in range(8):
                            src = attnT if kc < 4 else hyT
                            Sx.mm(pa[:, :], src[:, kc % 4, tsl], wout[:, kc, hf * 512:(hf + 1) * 512], start=(kc == 0), stop=(kc == 7))

                def post(j, g=g):
                    t_ = 4 * g + j
                    xb = xt[t_ % 2]
                    for hf in range(2):
                        pa = banks[2 * (j % 2) + hf]
                        tt("dve", hbuf[:, j, hf * 512:(hf + 1) * 512], pa[:, :], xb[:, hf * 512:(hf + 1) * 512], ALU.add)
                    ssq = stats[:, 34 + t_ % 2:35 + t_ % 2]
                    act(sqj, hbuf[:, j, :], AF.Square, accum_out=ssq)
                    rf = stats[:, 36 + t_ % 2:37 + t_ % 2]
                    act(rf, ssq, AF.Ln, scale=1.0 / D, bias=epsc[:, 0:1], extra=[epsc[:, 0:1]])
                    act(rf, rf, AF.Exp, scale=-0.5)
                    xb16 = xn16[t_ % 2]
                    ts("dve", xb16, hbuf[:, j, :], rf, None, ALU.mult, extra=[rf])
                    for gg in range(2):
                        pbk = pb16(4 + gg)
                        for jj in range(4):
                            kc = gg * 4 + jj
                            Sx.tr(pbk[:, jj * 128:(jj + 1) * 128], xb16[:, kc * 128:(kc + 1) * 128], ident[:])
                        cp("act" if gg == 0 else "dve", hnT[:, gg * 4:(gg + 1) * 4, j * 128:(j + 1) * 128],
                           pbk[:, 0:512].rearrange("p (j c) -> p j c", j=4))

                oproj(0)
                oproj(1)
                post(0)
                oproj(2)
                post(1)
                oproj(3)
                post(2)
                post(3)
                for fch in range(32):
                    wb = wbuf[wi[0] % 4]; wi[0] += 1
                    wv = wb.rearrange("p (k c) -> p k c", k=8)
                    Sx.dma(wv, wup_s[fch])
                    bk = banks[6 + fch % 2]
                    for kc in range(8):
                        Sx.mm(bk[:, :], wv[:, kc, :], hnT[:, kc, :], start=(kc == 0), stop=(kc == 7))
                    rb = rl[fch % 2]
                    act(rb, bk[:, :], AF.Relu)
                    tt("pool", ffT[:, fch, :], rb, rb, ALU.mult)
                for fch in range(32):
                    db = dbuf[di[0] % 4]; di[0] += 1
                    Sx.dma(db, wdn_s[fch])
                    for j in range(4):
                        for hf in range(2):
                            Sx.mm(banks[2 * j + hf][:, :], ffT[:, fch, j * 128:(j + 1) * 128], db[:, hf * 512:(hf + 1) * 512],
                                  start=(fch == 0), stop=(fch == 31))
                for j in range(4):
                    t_ = 4 * g + j
                    yb = yt[t_ % 2]
                    for hf in range(2):
                        tt("dve", yb[:, hf * 512:(hf + 1) * 512], banks[2 * j + hf][:, :], hbuf[:, j, hf * 512:(hf + 1) * 512], ALU.add)
                    Sx.dma(y_d[s, t_ * 128:(t_ + 1) * 128, :], yb, q="pool")

        Sx.emit()
        import os
        if os.environ.get("KDUMP"):
            import json
            json.dump({e: [Sx.ops[i]["tag"] for i in Sx.streams[e]] for e in ENGS}, open(os.environ["KDUMP"], "w"))
    return nc


_PROG = {}


def _get_prog(NS):
    if NS not in _PROG:
        _PROG[NS] = build_program(NS)
    return _PROG[NS]


def kernel(**inputs):
    xp = np.asarray(inputs["x_prompt"], dtype=np.float32)
    xs = np.asarray(inputs["x_sample"], dtype=np.float32)
    xall = np.concatenate([xp, xs], axis=0)
    nseq = xall.shape[0]
    NS = nseq // NCORES
    consts = make_consts()
    params = layout_params(inputs)
    nc = _get_prog(NS)
    in_maps = []
    for c in range(NCORES):
        m = {"x": np.ascontiguousarray(xall[c * NS:(c + 1) * NS])}
        m.update(consts)
        m.update(params)
        in_maps.append(m)
    res = run_bass_kernel_spmd(nc, in_maps, core_ids=list(range(NCORES)))
    yall = np.concatenate([np.asarray(r["y"]) for r in res.results], axis=0)
    nb = xp.shape[0]
    return (np.ascontiguousarray(yall[:nb]).astype(np.float32), np.ascontiguousarray(yall[nb:]).astype(np.float32))
```

```python
import contextlib
import numpy as np
import ml_dtypes
import concourse.bass as bass
import concourse.mybir as mybir
from concourse.bass_utils import run_bass_kernel_spmd

F32 = mybir.dt.float32
BF16 = mybir.dt.bfloat16
AF = mybir.ActivationFunctionType
ALU = mybir.AluOpType
NCORES = 8
S = 2048
D = 1024
EPS = 1e-6

ENGS = ("pe", "act", "dve", "pool", "sp")


def _interval(ap):
    pat = ap.ap
    name = ap.tensor.name
    esz = mybir.dt.size(ap.dtype)
    sp = str(ap.space).upper()
    off = ap.offset
    if "SB" in sp or "PSUM" in sp:
        row = pat[0][0]
        p0 = off // row
        lo = off - p0 * row
        ext = 0
        for st, n in pat[1:]:
            ext += abs(st) * (n - 1)
        if "PSUM" in sp:
            return name, 0, 2048, (p0 // 32) * 32, ((p0 + pat[0][1] + 31) // 32) * 32
        return name, lo * esz, (lo + ext + 1) * esz, p0, p0 + pat[0][1]
    ext = 0
    for st, n in pat:
        ext += abs(st) * (n - 1)
    return name, off * esz, (off + ext + 1) * esz, 0, 1


class Sched:
    def __init__(self, nc, n_dma_sems=8):
        self.nc = nc
        self.ops = []
        self.streams = {e: [] for e in ENGS}
        self.wr = {}
        self.rd = {}
        self.K = n_dma_sems
        self.notrack = set()
        self.tag = "setup"

    def _deps_for(self, opid, reads, writes):
        deps = set()
        ri = [_interval(a) for a in reads]
        wi = [_interval(a) for a in writes]
        for key, lo, hi, plo, phi in ri:
            if key in self.notrack:
                continue
            for (l, h, pl, ph, w) in self.wr.get(key, ()):
                if l < hi and lo < h and pl < phi and plo < ph:
                    deps.add(w)
        for key, lo, hi, plo, phi in wi:
            if key in self.notrack:
                continue
            for (l, h, pl, ph, w) in self.wr.get(key, ()):
                if l < hi and lo < h and pl < phi and plo < ph:
                    deps.add(w)
            for (l, h, pl, ph, r) in self.rd.get(key, ()):
                if l < hi and lo < h and pl < phi and plo < ph:
                    deps.add(r)
        deps.discard(opid)
        for key, lo, hi, plo, phi in wi:
            if key in self.notrack:
                continue
            wl = self.wr.setdefault(key, [])
            wl[:] = [x for x in wl if not (lo <= x[0] and x[1] <= hi and plo <= x[2] and x[3] <= phi)]
            wl.append((lo, hi, plo, phi, opid))
            rl = self.rd.setdefault(key, [])
            rl[:] = [x for x in rl if not (lo <= x[0] and x[1] <= hi and plo <= x[2] and x[3] <= phi)]
        eng = self.ops[opid]["eng"]
        isdma = self.ops[opid]["dma"]
        for key, lo, hi, plo, phi in ri:
            if key in self.notrack:
                continue
            rl = self.rd.setdefault(key, [])
            if not isdma:
                rl[:] = [x for x in rl if not (x[0] == lo and x[1] == hi and x[2] == plo and x[3] == phi
                                               and self.ops[x[4]]["eng"] == eng and not self.ops[x[4]]["dma"])]
            rl.append((lo, hi, plo, phi, opid))
        return deps

    def op(self, eng, fn, reads=(), writes=(), dma=False):
        opid = len(self.ops)
        rec = dict(eng=eng, fn=fn, deps=None, dma=dma, tag=self.tag)
        self.ops.append(rec)
        rec["deps"] = self._deps_for(opid, list(reads), list(writes))
        self.streams[eng].append(opid)
        return opid

    def dma(self, out, in_, q="sp"):
        return self.op(q, lambda e: e.dma_start(out=out, in_=in_), reads=[in_], writes=[out], dma=True)

    def mm(self, out, lhsT, rhs, start=True, stop=True):
        return self.op("pe", lambda e: e.matmul(out, lhsT, rhs, start=start, stop=stop, skip_group_check=True),
                       reads=[lhsT, rhs], writes=[out])

    def tr(self, out, in_, ident):
        return self.op("pe", lambda e: e.transpose(out, in_, ident), reads=[in_, ident], writes=[out])

    def emit(self):
        nc = self.nc
        ops = self.ops
        needed = set()
        for o in ops:
            for d in o["deps"]:
                if not (o["eng"] == "pe" and ops[d]["eng"] == "pe"):
                    needed.add(d)
        cnt = {e: 0 for e in ENGS}
        dcnt = {e: 0 for e in ENGS}
        for e in ENGS:
            for i in self.streams[e]:
                o = ops[i]
                o["thr"] = None
                if o["dma"]:
                    k = dcnt[e]
                    dcnt[e] += 1
                    o["sem"] = ("d", e, k % self.K)
                    o["val"] = 16 * (k // self.K + 1)
                    if k >= self.K:
                        o["thr"] = (("d", e, k % self.K), 16 * (k // self.K))
                elif i in needed:
                    cnt[e] += 1
                    o["sem"] = ("c", e, 0)
                    o["val"] = cnt[e]
                else:
                    o["sem"] = None
        with contextlib.ExitStack() as st:
            sems = {}
            for e in ENGS:
                sems[("c", e, 0)] = st.enter_context(nc.semaphore(f"c_{e}"))
                if dcnt[e]:
                    for k in range(self.K):
                        sems[("d", e, k)] = st.enter_context(nc.semaphore(f"d_{e}{k}"))
            block = st.enter_context(nc.Block())

            def run(e, engobj):
                waited = {}
                for i in self.streams[e]:
                    o = ops[i]
                    req = {}
                    for d in o["deps"]:
                        od = ops[d]
                        if od["eng"] == "pe" and e == "pe":
                            continue
                        s = od["sem"]
                        if s is not None and req.get(s, 0) < od["val"]:
                            req[s] = od["val"]
                    if o["thr"] is not None:
                        s, v = o["thr"]
                        if req.get(s, 0) < v:
                            req[s] = v
                    for s, v in req.items():
                        if waited.get(s, 0) < v:
                            engobj.wait_ge(sems[s], v)
                            waited[s] = v
                    ins = o["fn"](engobj)
                    if o["sem"] is not None:
                        ins.then_inc(sems[o["sem"]], 16 if o["dma"] else 1)
                if dcnt[e]:
                    for k in range(self.K):
                        n = len(range(k, dcnt[e], self.K))
                        if n:
                            engobj.wait_ge(sems[("d", e, k)], 16 * n)

            @block.tensor
            def _(eng):
                run("pe", eng)

            @block.scalar
            def _(eng):
                run("act", eng)

            @block.vector
            def _(eng):
                run("dve", eng)

            @block.gpsimd
            def _(eng):
                run("pool", eng)

            @block.sync
            def _(eng):
                run("sp", eng)


_CONST = None


def make_consts():
    global _CONST
    if _CONST is not None:
        return _CONST
    bf = ml_dtypes.bfloat16
    c = {}
    c["ident"] = np.eye(128, dtype=np.float32).astype(bf)
    bm = np.zeros((128, 128), np.float32)
    bm[:64, :64] = 1.0 / 64
    bm[64:, 64:] = 1.0 / 64
    c["bm"] = bm.astype(bf)
    r0 = np.zeros((128, 128), np.float32)
    for m in range(128):
        hb, e = (m // 64) * 64, m % 64
        if e < 32:
            r0[hb + e + 32, m] = -1.0
        else:
            r0[hb + e - 32, m] = 1.0
    c["r0"] = r0
    half = 32
    inv_freq = (np.float32(10000.0) ** (-(np.arange(half, dtype=np.float32) / np.float32(half)))).astype(np.float32)
    ang = (np.arange(S, dtype=np.float32)[:, None] * inv_freq[None, :]).astype(np.float32)
    cs = np.cos(ang.astype(np.float64)).astype(np.float32).T
    sn = np.sin(ang.astype(np.float64)).astype(np.float32).T
    c["cosT"] = np.tile(cs, (4, 1))
    c["sinT"] = np.tile(sn, (4, 1)).astype(bf)
    kk = np.arange(128)[:, None]
    qq = np.arange(128)[None, :]
    m1 = (qq <= kk).astype(np.float32)
    m2 = (qq >= kk).astype(np.float32)
    m1e = np.zeros((128, 128), np.float32)
    m1e[:64] = m1[64:]
    c["maskG"] = np.concatenate([m1, m2, m1, m2], 1).astype(bf)
    c["maskA"] = np.concatenate([m1e, m2, m1, m2], 1).astype(bf)
    c["maskB"] = np.concatenate([m1e, m2, m1e, m2], 1).astype(bf)
    mb = (np.abs(qq - kk) <= 64).astype(np.float32)
    mb3 = np.zeros((128, 4, 16, 32), np.float32)
    for cc in range(4):
        mb3[:, cc, :, :] = mb[:, None, 32 * cc:32 * cc + 32]
    c["mb3"] = mb3.reshape(128, 4 * 512).astype(bf)
    par = np.arange(2, dtype=np.int64)[:, None, None]
    jj = np.arange(8, dtype=np.int64)[None, :, None]
    pp = np.arange(128, dtype=np.int64)[None, None, :]
    tok = par + 256 * jj + 2 * pp
    c["tok"] = tok
    k = np.arange(1024, dtype=np.int64)
    m = ((2 * k[None, None, None, :] + 1) * tok[..., None]) % 8192
    th = m.astype(np.float64) * (2.0 * np.pi / 8192.0)
    C = np.cos(th).reshape(2, 8, 128, 8, 128)
    Sn = np.sin(th).reshape(2, 8, 128, 8, 128)
    cf = C.transpose(3, 2, 0, 1, 4)
    sf = (-Sn).transpose(3, 2, 0, 1, 4)
    c["fwd"] = np.ascontiguousarray(np.stack([cf, sf], 3)).astype(bf)
    sc = 2.0 / 4096.0
    ci = (sc * C).transpose(0, 1, 4, 3, 2).reshape(16, 128, 8, 128)
    si = (-sc * Sn).transpose(0, 1, 4, 3, 2).reshape(16, 128, 8, 128)
    c["inv"] = np.ascontiguousarray(np.stack([ci, si], 2)).astype(bf)
    L = S
    pos = np.arange(L, dtype=np.float32)
    tt = (pos / np.float32(L - 1)).astype(np.float32)
    bands = 16
    fr = np.linspace(1e-4, bands - 1, bands, dtype=np.float32)
    angf = (np.float32(2.0 * np.pi) * pos[:, None] * fr[None, :] / np.float32(L)).astype(np.float32)
    feat = np.concatenate([tt[:, None], np.cos(angf.astype(np.float64)).astype(np.float32),
                           -np.sin(angf.astype(np.float64)).astype(np.float32)], -1)
    c["featT"] = np.ascontiguousarray(feat.T).astype(np.float32)
    deltas = np.linspace(np.log(1e-2) / 0.3, np.log(1e-2) / 1.5, 512, dtype=np.float32)
    decay = np.exp(-(tt[:, None] * np.abs(deltas)[None, :]).astype(np.float32).astype(np.float64)).astype(np.float32)
    c["decay"] = np.ascontiguousarray(decay[tok.reshape(16, 128)].transpose(1, 0, 2))
    del c["tok"]
    _CONST = c
    return c


CONST_SPECS = [("ident", [128, 128], BF16), ("bm", [128, 128], BF16), ("r0", [128, 128], F32),
               ("cosT", [128, S], F32), ("sinT", [128, S], BF16),
               ("maskG", [128, 512], BF16), ("maskA", [128, 512], BF16), ("maskB", [128, 512], BF16),
               ("mb3", [128, 2048], BF16), ("fwd", [8, 128, 2, 2, 8, 128], BF16),
               ("inv", [16, 128, 2, 8, 128], BF16), ("featT", [33, S], F32), ("decay", [128, 16, 512], F32)]

PARAM_SPECS = [("w_in", [D, 3072]), ("w_out", [D, D]), ("w_up", [D, 4096]), ("w_down", [4096, D]),
               ("g_mix", [128, 8]), ("g_ffn", [128, 8]), ("g_out", [128, 8]), ("g_qk", [128, 2]),
               ("convw", [128, 48]), ("flt_w1", [33, 64]), ("flt_w2", [64, 64]), ("flt_w3", [64, 2048]),
               ("flt_v", [64, 4]), ("flt_b3", [1, 2048]), ("hy_skip", [1, 1024])]


def layout_params(p):
    f = lambda a: np.ascontiguousarray(np.asarray(a, dtype=np.float32))
    out = {}
    out["w_in"] = f(p["w_in"][0])
    out["w_out"] = f(p["w_out"][0])
    out["w_up"] = f(p["w_up"][0])
    out["w_down"] = f(p["w_down"][0])
    out["g_mix"] = f(np.asarray(p["mix_norm"][0]).reshape(8, 128).T)
    out["g_ffn"] = f(np.asarray(p["ffn_norm"][0]).reshape(8, 128).T)
    gcat = np.concatenate([np.asarray(p["attn_out_norm"][0]), np.asarray(p["hy_out_norm"][0])])
    out["g_out"] = f(gcat.reshape(8, 128).T)
    gq = np.tile(np.asarray(p["q_norm"][0]), 2)
    gk = np.tile(np.asarray(p["k_norm"][0]), 2)
    out["g_qk"] = f(np.stack([gq, gk], 1))
    cw = np.asarray(p["hy_conv_w"][0])
    cb = np.asarray(p["hy_conv_b"][0])
    cwb = np.concatenate([cw, cb[None, :]], 0)
    out["convw"] = f(cwb.reshape(4, 12, 128).transpose(2, 1, 0).reshape(128, 48))
    out["flt_w1"] = f(p["flt_w1"][0])
    out["flt_w2"] = f(p["flt_w2"][0])
    out["flt_w3"] = f(p["flt_w3"][0])
    out["flt_v"] = f(np.stack([np.asarray(p["flt_b1"][0]), np.asarray(p["flt_freq1"][0]),
                               np.asarray(p["flt_b2"][0]), np.asarray(p["flt_freq2"][0])], 1))
    out["flt_b3"] = f(np.asarray(p["flt_b3"][0])[None, :])
    out["hy_skip"] = f(np.asarray(p["hy_skip"][0]).reshape(1, 1024))
    return out


ARENA_BYTES = 180 * 1024


def build_program(NS, dbg=None):
    nc = bass.Bass("TRN2", target_bir_lowering=False)
    x_d = nc.dram_tensor("x", [NS, S, D], F32, kind="ExternalInput").ap()
    y_d = nc.dram_tensor("y", [NS, S, D], F32, kind="ExternalOutput").ap()
    cd = {n: nc.dram_tensor(n, shp, dt, kind="ExternalInput").ap() for n, shp, dt in CONST_SPECS}
    pd = {n: nc.dram_tensor(n, shp, F32, kind="ExternalInput").ap() for n, shp in PARAM_SPECS}
    win_s = nc.dram_tensor("win_s", [24, 128, 8, 128], BF16).ap()
    wup_s = nc.dram_tensor("wup_s", [32, 128, 8, 128], BF16).ap()
    wdn_s = nc.dram_tensor("wdn_s", [32, 128, 1024], BF16).ap()
    ksp_s = nc.dram_tensor("ksp_s", [2, 8, 128, 4, 512], BF16).ap()
    rope_s = nc.dram_tensor("rope_s", [2, 128, S], BF16).ap()
    dbg_d = {}
    if dbg:
        for n, shp in dbg.items():
            dbg_d[n] = nc.dram_tensor(n, shp, F32, kind="ExternalOutput").ap()

    with contextlib.ExitStack() as st:
        SB = lambda n, s, d: st.enter_context(nc.sbuf_tensor(n + "_sb", s, d))
        arena = SB("arena", [128, ARENA_BYTES // 2], BF16)
        banks = [st.enter_context(nc.psum_tensor(f"bank{i}", [128, 512], F32)) for i in range(8)]
        Sx = Sched(nc)
        Sx.notrack.update(["x"] + [n for n, _, _ in CONST_SPECS] + [n for n, _ in PARAM_SPECS])

        def A16(off, n):
            assert off % 4 == 0 and off + 2 * n <= ARENA_BYTES, (off, n)
            return arena[:, off // 2: off // 2 + n]

        def A32(off, n):
            assert off % 4 == 0 and off + 4 * n <= ARENA_BYTES, (off, n)
            return arena[:, off // 2: off // 2 + 2 * n].bitcast(F32)

        def pb16(b):
            return banks[b][:, :].bitcast(BF16)

        def act(out, in_, func, extra=(), **kw):
            w = [out] + ([kw["accum_out"]] if "accum_out" in kw else [])
            Sx.op("act", lambda e: e.activation(out, in_, func, **kw), reads=[in_] + list(extra), writes=w)

        NOPOOL = True

        def tt(eng, out, in0, in1, op):
            if NOPOOL and eng == "pool":
                eng = "dve"
            Sx.op(eng, lambda e: e.tensor_tensor(out, in0, in1, op), reads=[in0, in1], writes=[out])

        def ts(eng, out, in0, s1, s2, op0, op1=None, extra=()):
            if NOPOOL and eng == "pool":
                eng = "dve"
                if op1 == ALU.mult and s2 == 1.0:
                    op1, s2 = None, None
            if op1 is None:
                Sx.op(eng, lambda e: e.tensor_scalar(out, in0, s1, None, op0), reads=[in0] + list(extra), writes=[out])
            else:
                Sx.op(eng, lambda e: e.tensor_scalar(out, in0, s1, s2, op0, op1), reads=[in0] + list(extra), writes=[out])

        def stt(out, in0, sc, in1, op0, op1, extra=(), accum=None):
            w = [out] + ([accum] if accum is not None else [])
            if accum is None:
                Sx.op("dve", lambda e: e.scalar_tensor_tensor(out, in0, sc, in1, op0, op1),
                      reads=[in0, in1] + list(extra), writes=w)
            else:
                Sx.op("dve", lambda e: e.scalar_tensor_tensor(out, in0, sc, in1, op0, op1, accum_out=accum),
                      reads=[in0, in1] + list(extra), writes=w)

        def cp(eng, out, in_):
            if NOPOOL and eng == "pool":
                eng = "dve"
            if eng == "act":
                act(out, in_, AF.Copy)
            else:
                Sx.op(eng, lambda e: e.tensor_copy(out, in_), reads=[in_], writes=[out])

        def recip(out, in_):
            Sx.op("dve", lambda e: e.reciprocal(out, in_), reads=[in_], writes=[out])

        def memset(eng, ap, v):
            Sx.op(eng, lambda e: e.memset(ap, v), writes=[ap])

        ident = SB("ident", [128, 128], BF16)
        bm = SB("bm", [128, 128], BF16)
        rg = SB("rg", [128, 2, 128], BF16)
        maskG = SB("maskG", [128, 512], BF16)
        maskA = SB("maskA", [128, 512], BF16)
        maskB = SB("maskB", [128, 512], BF16)
        mb3 = SB("mb3", [128, 2048], BF16)
        wout = SB("wout", [128, 8, 1024], BF16)
        convw = SB("convw", [128, 48], F32)
        gsm = SB("gsm", [128, 32], F32)
        stats = SB("stats", [128, 64], F32)
        ones16 = SB("ones16", [128, 2], BF16)
        for n, tl in (("ident", ident), ("bm", bm), ("maskG", maskG), ("maskA", maskA),
                      ("maskB", maskB), ("mb3", mb3)):
            Sx.dma(tl[:], cd[n])
        Sx.dma(convw[:], pd["convw"])
        Sx.dma(gsm[:, 0:8], pd["g_mix"])
        Sx.dma(gsm[:, 8:16], pd["g_ffn"])
        Sx.dma(gsm[:, 16:24], pd["g_out"])
        Sx.dma(gsm[:, 24:26], pd["g_qk"])
        memset("pool", ones16[:], 1.0)
        onesM = SB("onesM", [128, 128], BF16)
        memset("pool", onesM[:], 1.0)
        epsc = SB("epsc", [128, 2], F32)
        memset("pool", epsc[:], EPS)

        o = 0
        r0f = A32(o, 128); o += 512
        cosf = A32(o, S); o += 4 * S
        Sx.dma(r0f, cd["r0"])
        Sx.dma(cosf, cd["cosT"])
        cos16 = [A16(o, S), A16(o + 2 * S, S)]
        for j in range(2):
            ts("dve", rg[:, j, :], r0f, gsm[:, 24 + j:25 + j], None, ALU.mult, extra=[gsm[:, 24 + j:25 + j]])
            ts("dve", cos16[j], cosf, gsm[:, 24 + j:25 + j], None, ALU.mult, extra=[gsm[:, 24 + j:25 + j]])
            Sx.dma(rope_s[j], cos16[j])

        o = 20 * 1024
        wst = [A32(o, 3072), A32(o + 12288, 3072)]
        o += 24576
        wcv = [A16(o, 3072), A16(o + 6144, 3072)]
        o += 12288
        it = 0

        def wscale(i, out, in_, g):
            act(out, in_, AF.Copy, extra=[g], scale=g)

        for kc in range(8):
            b = it % 2; it += 1
            Sx.dma(wst[b], pd["w_in"][kc * 128:(kc + 1) * 128, :], q="act")
            wscale(it, wcv[b], wst[b], gsm[:, kc:kc + 1])
            Sx.dma(win_s[:, :, kc, :].rearrange("f p c -> p f c"),
                   wcv[b].rearrange("p (f c) -> p f c", c=128), q="pool")
        for kc in range(8):
            for hh in range(2):
                b = it % 2; it += 1
                Sx.dma(wst[b][:, 0:2048], pd["w_up"][kc * 128:(kc + 1) * 128, hh * 2048:(hh + 1) * 2048], q="act")
                wscale(it, wcv[b][:, 0:2048], wst[b][:, 0:2048], gsm[:, 8 + kc:9 + kc])
                Sx.dma(wup_s[hh * 16:(hh + 1) * 16, :, kc, :].rearrange("f p c -> p f c"),
                       wcv[b][:, 0:2048].rearrange("p (f c) -> p f c", c=128), q="pool")
        for fch in range(0, 32, 2):
            b = it % 2; it += 1
            Sx.dma(wst[b][:, 0:2048].rearrange("p (f c) -> p f c", c=1024),
                   pd["w_down"][fch * 128:(fch + 2) * 128, :].rearrange("(f p) c -> p f c", p=128), q="act")
            cp("act", wcv[b][:, 0:2048], wst[b][:, 0:2048])
            Sx.dma(wdn_s[fch:fch + 2, :, :].rearrange("f p c -> p f c"),
                   wcv[b][:, 0:2048].rearrange("p (f c) -> p f c", c=1024), q="pool")
        for kc in range(0, 8, 2):
            b = it % 2; it += 1
            Sx.dma(wst[b][:, 0:2048].rearrange("p (f c) -> p f c", c=1024),
                   pd["w_out"][kc * 128:(kc + 2) * 128, :].rearrange("(f p) c -> p f c", p=128), q="act")
            for j in range(2):
                wscale(0, wout[:, kc + j, :], wst[b][:, j * 1024:(j + 1) * 1024], gsm[:, 16 + kc + j:17 + kc + j])

        o = 0
        featT = A32(o, S); o += 8192
        w1 = A32(o, 64); o += 256
        w2 = A32(o, 64); o += 256
        fv = A32(o, 8); o += 32
        w3 = A32(o, 2048); o += 8192
        h1 = A32(o, S); o += 8192
        h2 = A32(o, S); o += 8192
        wtmp = A32(o, S); o += 8192
        bsd = A32(o, 2048); o += 8192
        b3b = A32(o, 2048); o += 8192
        skb = A32(o, 1024); o += 4096
        dcb = [A32(o + i * 2048, 512) for i in range(2)]; o += 4096
        ftmp = [A32(o + i * 2048, 512) for i in range(4)]; o += 8192
        hsd = A16(o, 4 * 16 * 512); o += 65536
        assert o <= ARENA_BYTES, o
        hsd4 = hsd.rearrange("p (a t c) -> p a t c", a=4, t=16)
        Sx.dma(featT[0:33, :], cd["featT"])
        Sx.dma(w1[0:33, :], pd["flt_w1"])
        Sx.dma(w2[0:64, :], pd["flt_w2"])
        Sx.dma(fv[0:64, 0:4], pd["flt_v"])
        Sx.dma(w3[0:64, :], pd["flt_w3"])
        Sx.dma(b3b, pd["flt_b3"].partition_broadcast(128)[:, 0, :])
        Sx.dma(skb, pd["hy_skip"].partition_broadcast(128)[:, 0, :])
        tt("dve", fv[0:64, 4:5], fv[0:64, 0:1], fv[0:64, 1:2], ALU.mult)
        tt("dve", fv[0:64, 5:6], fv[0:64, 2:3], fv[0:64, 3:4], ALU.mult)
        b3v = b3b.rearrange("p (o d c) -> p o d c", o=2, d=2)
        bsdv = bsd.rearrange("p (o d c) -> p o d c", o=2, d=2)
        for oo in range(2):
            tt("pool", bsdv[:, oo, 0, :], b3v[:, oo, 0, :], b3v[:, oo, 1, :], ALU.add)
            tt("pool", bsdv[:, oo, 1, :], b3v[:, oo, 0, :], b3v[:, oo, 1, :], ALU.subtract)
        PI = float(np.pi)

        def sin_layer(dst, src_w, src_k, rhs_t, fcol, fbcol):
            for tg in range(4):
                bk = banks[tg]
                Sx.mm(bk[0:64, :], src_w, rhs_t[:, tg * 512:(tg + 1) * 512])
                ts("dve", wtmp[0:64, tg * 512:(tg + 1) * 512], bk[0:64, :], fv[0:64, fcol:fcol + 1],
                   fv[0:64, fbcol:fbcol + 1], ALU.mult, ALU.add, extra=[fv[0:64, fcol:fcol + 1], fv[0:64, fbcol:fbcol + 1]])
            w_ = wtmp[0:64, :]
            m_ = dst[0:64, :]
            for _ in range(2):
                ts("dve", m_, w_, PI, 2 * PI, ALU.is_gt, ALU.mult)
                tt("dve", w_, w_, m_, ALU.subtract)
                ts("dve", m_, w_, -PI, 2 * PI, ALU.is_lt, ALU.mult)
                tt("dve", w_, w_, m_, ALU.add)
            act(m_, w_, AF.Sin)

        sin_layer(h1, w1[0:33, :], 33, featT[0:33, :], 1, 4)
        sin_layer(h2, w2[0:64, :], 64, h1[0:64, :], 3, 5)
        for t_ in range(16):
            for oo in range(2):
                pf, pbk = banks[4 + 2 * (t_ % 2)], banks[5 + 2 * (t_ % 2)]
                tcols = slice((t_ // 8) + 256 * (t_ % 8), (t_ // 8) + 256 * (t_ % 8) + 255, 2)
                Sx.mm(pf[:, :], h2[0:64, tcols], w3[0:64, (2 * oo) * 512:(2 * oo + 1) * 512])
                Sx.mm(pbk[:, :], h2[0:64, tcols], w3[0:64, (2 * oo + 1) * 512:(2 * oo + 2) * 512])
                cp("act", ftmp[0], pf[:, :])
                tt("dve", ftmp[1], pbk[:, :], ftmp[0], ALU.add)
                stt(ftmp[2], pbk[:, :], -1.0, ftmp[0], ALU.mult, ALU.add)
                tt("pool", ftmp[1], ftmp[1], bsdv[:, oo, 0, :], ALU.add)
                tt("pool", ftmp[2], ftmp[2], bsdv[:, oo, 1, :], ALU.add)
                dc = dcb[t_ % 2]
                if oo == 0:
                    Sx.dma(dc, cd["decay"][:, t_, :])
                tt("pool", hsd4[:, 2 * oo, t_, :], ftmp[1], dc, ALU.mult)
                tt("dve", hsd4[:, 2 * oo + 1, t_, :], ftmp[2], dc, ALU.mult)
        kout = [wtmp, h1]
        h2b = h2.bitcast(BF16)
        k16b = [h2b[:, 0:2048], h2b[:, 2048:4096]]
        cbk = [featT[:, 0:512], featT[:, 512:1024]]
        assert o <= 152 * 1024, o
        SLAB = 152 * 1024
        slabs = [A16(SLAB + i * 8192, 4096) for i in range(3)]
        si = 0
        for oo in range(2):
            for a in range(8):
                sl = slabs[si % 3]; si += 1
                slv = sl.rearrange("p (r q j k) -> p r q j k", r=2, q=2, j=8)
                Sx.dma(slv, cd["fwd"][a])
                bset = [banks[4 * (a % 2) + i] for i in range(4)]
                for par in range(2):
                    for cs in range(2):
                        pk = bset[2 * par + cs]
                        for j in range(8):
                            Sx.mm(pk[:, :], slv[:, par, cs, j, :], hsd4[:, 2 * oo + cs, 8 * par + j, :], start=(j == 0), stop=(j == 7))
                Ac, As_, Bc, Bs = bset
                ko = kout[a % 2]
                sk = skb[:, oo * 512:(oo + 1) * 512]
                cp("act", cbk[0], Bc[:, :])
                cp("act", cbk[1], Bs[:, :])
                tt("dve", ko[:, 0:512], Ac[:, :], cbk[0], ALU.add)
                tt("pool", ko[:, 0:512], ko[:, 0:512], sk, ALU.add)
                tt("dve", ko[:, 1024:1536], Ac[:, :], cbk[0], ALU.subtract)
                tt("pool", ko[:, 1024:1536], ko[:, 1024:1536], sk, ALU.add)
                tt("dve", ko[:, 512:1024], As_[:, :], cbk[1], ALU.add)
                stt(ko[:, 1536:2048], As_[:, :], -1.0, cbk[1], ALU.mult, ALU.add)
                k16 = k16b[a % 2]
                cp("act", k16, ko)
                Sx.dma(ksp_s[oo, a], k16.rearrange("p (j c) -> p j c", j=4), q="pool")

        R1 = 0
        R3 = 32 * 1024
        R5 = 48 * 1024
        RX = 64 * 1024
        R7 = 152 * 1024
        R8 = 176 * 1024
        xnT = A16(R1, 8 * S).rearrange("p (k t) -> p k t", k=8)
        Pbuf = A16(R1, 32 * 512).rearrange("p (j c) -> p j c", j=32)
        ffT = A16(RX + 55296, 32 * 512).rearrange("p (j c) -> p j c", j=32)
        vT = A16(R3, 4 * S).rearrange("p (k t) -> p k t", k=4)
        hyT = vT
        attnT = A16(R5, 4 * S).rearrange("p (k t) -> p k t", k=4)
        qT = A16(RX, 4 * S).rearrange("p (k t) -> p k t", k=4)
        kT = A16(RX + 16384, 4 * S).rearrange("p (k t) -> p k t", k=4)
        Vp = A16(RX + 32768, 2 * 53 * 128).rearrange("p (h t c) -> p h t c", h=2, t=53)
        TMP = RX + 32768 + 27136 + 512
        zx = A16(RX, 16 * 1536).rearrange("p (t c) -> p t c", t=16)
        UB = RX + 49152
        ubuf = A32(UB, 2050)
        c1 = A32(UB + 8448, 1024)
        c2 = A32(UB + 8448 + 4096, 1024)
        c16 = A16(UB + 8448 + 8192, 2048)
        TL = RX
        hbuf = A32(TL, 4 * 1024).rearrange("p (t c) -> p t c", t=4); TL += 16384
        hnT = A16(TL, 8 * 512).rearrange("p (k t) -> p k t", k=8); TL += 8192
        xt = [A32(TL + i * 4096, 1024) for i in range(2)]; TL += 8192
        yt = [A32(TL + i * 4096, 1024) for i in range(2)]; TL += 8192
        xn16 = [A16(TL + i * 2048, 1024) for i in range(2)]; TL += 4096
        rl = [A32(TL + i * 2048, 512) for i in range(2)]; TL += 4096
        sqj = A16(TL, 1024); TL += 2048
        wbuf = [A16(R7 + i * 2048, 1024) for i in range(4)]
        dbuf = [A16(R7 + 8192 + i * 2048, 1024) for i in range(4)]
        kbuf = [A32(R8, 1024)]
        wi = [0]
        di = [0]

        def vp_tiles():
            tl = []
            for i in range(17):
                s0, s1 = max(0, 128 * i - 64), min(S, 128 * i + 64)
                tl.append((s0, 1, s1 - s0))
            for rho in range(4):
                for i in range(5):
                    j0, j1 = max(0, 128 * i - 64), min(512, 128 * i + 64)
                    tl.append((4 * j0 + rho, 4, j1 - j0))
            for r in range(16):
                tl.append((r, 16, 128))
            return tl

        VPT = vp_tiles()

        def cols(start, step, n):
            return slice(start, start + step * (n - 1) + 1, step)

        xt1 = [A32(R7 + 16384 + i * 4096, 1024) for i in range(2)]
        xn1 = A16(R8, 1024)
        sqj1 = A16(R8 + 2048, 1024)

        def p1_front(s2, t_):
            xb = xt1[t_ % 2]
            Sx.dma(xb, x_d[s2, t_ * 128:(t_ + 1) * 128, :])
            ssq = stats[:, t_ % 2:t_ % 2 + 1]
            rs = stats[:, 2 + t_ % 2:3 + t_ % 2]
            act(sqj1, xb, AF.Square, accum_out=ssq)
            act(rs, ssq, AF.Ln, scale=1.0 / D, bias=epsc[:, 0:1], extra=[epsc[:, 0:1]])
            act(rs, rs, AF.Exp, scale=-0.5)
            ts("dve", xn1, xb, rs, None, ALU.mult, extra=[rs])

        def p1_back(s2, t_):
            for g in range(2):
                pbk = pb16(4 + g)
                for j in range(4):
                    kc = g * 4 + j
                    Sx.tr(pbk[:, j * 128:(j + 1) * 128], xn1[:, kc * 128:(kc + 1) * 128], ident[:])
                cp("act" if g == 0 else "dve", xnT[:, g * 4:(g + 1) * 4, t_ * 128:(t_ + 1) * 128],
                   pbk[:, 0:512].rearrange("p (j c) -> p j c", j=4))

        for s in range(NS):
            Sx.tag = f"s{s}p1"
            if s == 0:
                for t_ in range(16):
                    p1_front(0, t_)
                    p1_back(0, t_)

            Sx.tag = f"s{s}p2"
            bki = [0]

            def inproj_chunk(fc, consumer):
                wb = wbuf[wi[0] % 4]; wi[0] += 1
                wv = wb.rearrange("p (k c) -> p k c", k=8)
                Sx.dma(wv, win_s[fc])
                for tg in range(4):
                    bk = banks[bki[0] % 4]; bki[0] += 1
                    for kc in range(8):
                        Sx.mm(bk[:, :], wv[:, kc, :], xnT[:, kc, tg * 512:(tg + 1) * 512], start=(kc == 0), stop=(kc == 7))
                    nxt = consumer(fc, tg, bk)
                    for e_ in pend:
                        e_[0] -= 1
                    while pend and pend[0][0] <= 0:
                        pend.pop(0)[1]()
                    if nxt is not None:
                        pend.append([nxt[0], nxt[1]])

            pend = []

            def flush_pend():
                while pend:
                    pend.pop(0)[1]()

            QT = TMP
            cosq = A16(RX + 32768, S)
            cosk = A16(RX + 32768 + 2 * S, S)
            sinT = A16(RX + 32768 + 4 * S, S)
            Sx.dma(cosq, rope_s[0])
            Sx.dma(cosk, rope_s[1])
            Sx.dma(sinT, cd["sinT"])
            a_sb = [A32(QT + i * 2048, 512) for i in range(2)]
            sq16 = [A16(QT + 4096 + i * 1024, 512) for i in range(2)]
            a16 = [A16(QT + 6144 + i * 1024, 512) for i in range(2)]
            sd = [A32(QT + 8192 + i * 2048, 512) for i in range(2)]
            t1 = [A32(QT + 12288 + i * 2048, 512) for i in range(2)]
            t2 = [A32(QT + 16384 + i * 2048, 512) for i in range(2)]
            qi = [0]

            def qk_consumer(fc, tg, bk):
                i = qi[0] % 2; qi[0] += 1
                isk = fc >= 4
                dst = (kT if isk else qT)[:, fc % 4, tg * 512:(tg + 1) * 512]
                ctab = (cosk if isk else cosq)[:, tg * 512:(tg + 1) * 512]
                cp("act", a_sb[i], bk[:, :])
                act(sq16[i], bk[:, :], AF.Square)
                cp("dve", a16[i], a_sb[i])
                tt("dve", t1[i], a_sb[i], ctab, ALU.mult)
                pm, pr = banks[4 + (qi[0] % 2) * 2], banks[5 + (qi[0] % 2) * 2]

                def tail():
                    Sx.mm(pm[:, :], bm[:], sq16[i])
                    Sx.mm(pr[:, :], rg[:, 1 if isk else 0, :], a16[i])
                    act(t2[i], pm[:, :], AF.Ln, bias=epsc[:, 0:1], extra=[epsc[:, 0:1]])
                    act(sd[i], t2[i], AF.Exp, scale=-0.5)
                    tt("dve", t2[i], pr[:, :], sinT[:, tg * 512:(tg + 1) * 512], ALU.mult)
                    tt("dve", t1[i], t1[i], t2[i], ALU.add)
                    tt("dve", dst, t1[i], sd[i], ALU.mult)
                return (1, tail)

            def v_consumer(fc, tg, bk):
                cp("act", vT[:, fc - 8, tg * 512:(tg + 1) * 512], bk[:, :])

            for fc in range(8):
                inproj_chunk(fc, qk_consumer)
            for fc in range(8, 12):
                inproj_chunk(fc, v_consumer)
            flush_pend()

            Sx.tag = f"s{s}p3"
            AT = TMP
            PT = [A16(AT + i * 1024, 512) for i in range(6)]
            tot = [A32(AT + 6144 + i * 2048, 512) for i in range(2)]
            rden = [A32(AT + 10240 + i * 2048, 512) for i in range(2)]
            lnd = [A32(AT + 14336 + i * 2048, 512) for i in range(2)]
            memset("pool", Vp[:, :, :, 64:128], 1.0)
            items = []

            def add_vp_items(hp):
                for g0 in range(0, 53, 4):
                    g1 = min(53, g0 + 4)

                    def front(bk, g0=g0, g1=g1, hp=hp):
                        pbk = banks[bk][:, :].bitcast(BF16)
                        for ti in range(g0, g1):
                            st0, stp, n = VPT[ti]
                            Sx.tr(pbk[0:n, (ti - g0) * 128:(ti - g0 + 1) * 128], vT[:, hp, cols(st0, stp, n)], ident[:])

                    def back(bk, k, g0=g0, g1=g1):
                        pbk = banks[bk][:, :].bitcast(BF16)
                        full = all(VPT[ti][2] == 128 for ti in range(g0, g1))
                        if full:
                            cp("dve" if (g0 // 4) % 2 else "act", Vp[:, :, g0:g1, 0:64],
                               pbk[:, 0:(g1 - g0) * 128].rearrange("p (t h e) -> p h t e", h=2, e=64))
                        else:
                            for ti in range(g0, g1):
                                n = VPT[ti][2]
                                cp("dve" if ti % 2 else "act", Vp[0:n, :, ti, 0:64],
                                   pbk[0:n, (ti - g0) * 128:(ti - g0 + 1) * 128].rearrange("p (h e) -> p h e", h=2))
                    items.append((front, back))

            def add_batch(hp, hh, c, smm, mask, pvs, state, last):
                def front(bk, smm=smm):
                    sbk = banks[bk]
                    for (n, cs, l, r) in smm:
                        Sx.mm(sbk[0:n, cs], l, r)

                def back(bk, k, mask=mask, pvs=pvs, state=state, last=last, hp=hp, hh=hh, c=c):
                    sbk = banks[bk]
                    pt = PT[k % 6]
                    act(pt, sbk[:, :], AF.Exp, scale=0.125)
                    tt("pool", pt, pt, mask, ALU.mult)
                    accs = (banks[0], banks[1], banks[2])
                    for (co, nk, ncl, vti, ai, acs) in pvs:
                        Sx.mm(accs[ai][:, acs], Vp[0:nk, hh, vti, :], pt[0:nk, co:co + ncl], start=state[ai], stop=False)
                        state[ai] = False
                    if last:
                        i2 = (2 * hp + hh + c) % 2
                        tv = tot[i2]
                        cp("dve", tv, accs[0][:, :])
                        tt("dve", tv.rearrange("p (j r) -> p j r", r=4), tv.rearrange("p (j r) -> p j r", r=4),
                           accs[1][:, :].rearrange("p (r j) -> p j r", r=4), ALU.add)
                        tt("dve", tv.rearrange("p (j r) -> p j r", r=16), tv.rearrange("p (j r) -> p j r", r=16),
                           accs[2][:, :].rearrange("p (r j) -> p j r", r=16), ALU.add)

                        def fin(hp=hp, hh=hh, c=c, i2=i2):
                            r0_, r1_ = hh * 64, hh * 64 + 64
                            tv, rv, lv = tot[i2], rden[i2], lnd[i2]
                            act(lv[0:64, :], tv[64:128, :], AF.Ln)
                            act(rv[0:64, :], lv[0:64, :], AF.Exp, scale=-1.0)
                            tt("pool", attnT[r0_:r1_, hp, c * 512:(c + 1) * 512], tv[0:64, :], rv[0:64, :], ALU.mult)
                        deferred.append((k + 2, fin))
                items.append((front, back))

            for hp in range(4):
                add_vp_items(hp)
                for hh in range(2):
                    r0_, r1_ = hh * 64, hh * 64 + 64
                    qh = qT[r0_:r1_, hp, :]
                    kh = kT[r0_:r1_, hp, :]
                    for c in range(4):
                        state = [True, True, True]
                        for half in range(2):
                            smm, pvs = [], []
                            for qa in range(2):
                                a = 4 * c + 2 * half + qa
                                for kt_ in range(2):
                                    ti = a + kt_
                                    st0, stp, n = VPT[ti]
                                    co = (qa * 2 + kt_) * 128
                                    smm.append((n, slice(co, co + 128), kh[:, cols(st0, 1, n)], qh[:, a * 128:(a + 1) * 128]))
                                    pvs.append((co, n, 128, ti, 0, slice((2 * half + qa) * 128, (2 * half + qa + 1) * 128)))
                            add_batch(hp, hh, c, smm, (maskA if (c == 0 and half == 0) else maskG)[:], pvs, state, False)
                        for half in range(2):
                            smm, pvs = [], []
                            for qa in range(2):
                                rho = 2 * half + qa
                                for kt_ in range(2):
                                    ti = 17 + 5 * rho + c + kt_
                                    st0, stp, n = VPT[ti]
                                    co = (qa * 2 + kt_) * 128
                                    smm.append((n, slice(co, co + 128), kh[:, cols(st0, 4, n)], qh[:, cols(512 * c + rho, 4, 128)]))
                                    pvs.append((co, n, 128, ti, 1, slice(rho * 128, (rho + 1) * 128)))
                            add_batch(hp, hh, c, smm, (maskB if c == 0 else maskG)[:], pvs, state, False)
                        smm, pvs = [], []
                        for r in range(16):
                            smm.append((128, slice(r * 32, (r + 1) * 32), kh[:, cols(r, 16, 128)], qh[:, cols(512 * c + r, 16, 32)]))
                            pvs.append((r * 32, 128, 32, 37 + r, 2, slice(r * 32, (r + 1) * 32)))
                        add_batch(hp, hh, c, smm, mb3[:, c * 512:(c + 1) * 512], pvs, state, True)
            LA = 3
            deferred = []
            for i in range(len(items) + LA):
                if i < len(items):
                    items[i][0](3 + i % 5)
                if i - LA >= 0:
                    items[i - LA][1](3 + (i - LA) % 5, i - LA)
                    while deferred and deferred[0][0] <= i - LA:
                        deferred.pop(0)[1]()
            while deferred:
                deferred.pop(0)[1]()

            if dbg and "d_attn" in dbg_d and s == 0:
                for kc in range(4):
                    dtmp = A32(TMP, 2048)
                    cp("dve", dtmp, attnT[:, kc, :])
                    Sx.dma(dbg_d["d_attn"][kc * 128:(kc + 1) * 128, :], dtmp)

            Sx.tag = f"s{s}p4"
            memset("pool", ubuf[:, 0:1], 0.0)
            memset("pool", ubuf[:, 2049:2050], 0.0)
            tri = [0]

            ubufs = [ubuf, A32(RX + 71680, 2050)]
            c16s = [c16, A16(RX + 71680 + 8448, 2048)]
            memset("pool", ubufs[1][:, 0:1], 0.0)
            memset("pool", ubufs[1][:, 2049:2050], 0.0)

            def hy_consumer(fc, tg, bk):
                ch = fc - 12
                ub, cc = ubufs[ch % 2], c16s[ch % 2]
                cp("act", ub[:, 1 + tg * 512:1 + (tg + 1) * 512], bk[:, :])
                if tg != 3:
                    return None
                w = lambda k: convw[:, ch * 4 + k:ch * 4 + k + 1]
                for hf in range(2):
                    u0 = 1024 * hf
                    ts("dve", c1, ub[:, u0:u0 + 1024], w(0), w(3), ALU.mult, ALU.add, extra=[w(0), w(3)])
                    stt(c2, ub[:, u0 + 1:u0 + 1025], w(1), c1, ALU.mult, ALU.add, extra=[w(1)])
                    stt(cc[:, u0:u0 + 1024], ub[:, u0 + 2:u0 + 1026], w(2), c2, ALU.mult, ALU.add, extra=[w(2)])

                def tail():
                    for g in range(4):
                        pbk = pb16(4 + tri[0] % 4); tri[0] += 1
                        for j in range(4):
                            t_ = 4 * g + j
                            Sx.tr(pbk[:, j * 128:(j + 1) * 128], cc[:, cols((t_ // 8) + 256 * (t_ % 8), 2, 128)], ident[:])
                        cp("act" if g % 2 else "dve", zx[:, 4 * g:4 * g + 4, ch * 128:(ch + 1) * 128],
                           pbk[:, 0:512].rearrange("p (j c) -> p j c", j=4))
                return (4, tail)

            sqFa = A16(R3, 4 * 2048).rearrange("p (g k t) -> p g k t", g=4, k=4)
            for tg in range(4):
                tgs = slice(tg * 512, (tg + 1) * 512)
                tt("pool", sqFa[:, tg], attnT[:, :, tgs], attnT[:, :, tgs], ALU.mult)

            def attn_norm(tgl):
                lnb = A32(R8, 512)
                rab = A32(R8 + 2048, 512)
                for tg in tgl:
                    tgs = slice(tg * 512, (tg + 1) * 512)
                    sqF = sqFa[:, tg]
                    pbc = banks[6 + tg % 2]
                    for kc in range(4):
                        Sx.mm(pbc[:, :], onesM[:], sqF[:, kc, :], start=(kc == 0), stop=(kc == 3))
                    act(lnb, pbc[:, :], AF.Ln, scale=1.0 / 512, bias=epsc[:, 0:1], extra=[epsc[:, 0:1]])
                    act(rab, lnb, AF.Exp, scale=-0.5)
                    for kc in range(4):
                        tt("pool" if kc % 2 else "dve", attnT[:, kc, tgs], attnT[:, kc, tgs], rab, ALU.mult)

            for fc in range(12, 24):
                inproj_chunk(fc, hy_consumer)
                if fc == 12:
                    attn_norm((0, 1))
                if fc == 13:
                    attn_norm((2, 3))
            flush_pend()

            Sx.tag = f"s{s}p5"
            DT = RX + 49152
            cB = [[A32(DT + (2 * i + j) * 2048, 512) for j in range(2)] for i in range(2)]
            Zb = [[A16(DT + 8192 + (4 * i + j) * 1024, 512) for j in range(4)] for i in range(2)]
            mt = [A32(DT + 16384 + i * 2048, 512) for i in range(4)]
            Pr = [A32(DT + 24576 + i * 2048, 512) for i in range(4)]
            kb2s = [A16(DT + 32768 + i * 2048, 1024) for i in range(2)]
            kb1s = [kbuf[0].bitcast(BF16)[:, i * 1024:(i + 1) * 1024] for i in range(2)]
            z2b = [A16(DT + 36864 + i * 1024, 512) for i in range(4)]
            Pv = Pbuf.rearrange("p (e q a) c -> p e q a c", e=2, q=2)
            slab_i = [0]
            for oo in range(2):
                def mmA(a, oo=oo):
                    sl = slabs[slab_i[0] % 3]; slab_i[0] += 1
                    slv = sl.rearrange("p (r q j k) -> p r q j k", r=2, q=2, j=8)
                    Sx.dma(slv, cd["fwd"][a])
                    Sx.dma(kb1s[a % 2].rearrange("p (j c) -> p j c", j=2), ksp_s[oo, a, :, 0:2, :], q="pool")
                    Sx.dma(kb2s[a % 2].rearrange("p (j c) -> p j c", j=2), ksp_s[oo, a, :, 2:4, :], q="pool")
                    bset = [banks[4 * (a % 2) + i] for i in range(4)]
                    for par_ in range(2):
                        for cs in range(2):
                            pk = bset[2 * par_ + cs]
                            for j in range(8):
                                Sx.mm(pk[:, :], slv[:, par_, cs, j, :], zx[:, 8 * par_ + j, 0:512], start=(j == 0), stop=(j == 7))

                def st1(a):
                    bset = [banks[4 * (a % 2) + i] for i in range(4)]
                    Br, Bi = cB[a % 2]
                    Z = Zb[a % 2]
                    cp("act", Br, bset[2][:, :])
                    cp("act", Bi, bset[3][:, :])
                    tt("dve", Z[0], bset[0][:, :], Br, ALU.add)
                    tt("dve", Z[1], bset[1][:, :], Bi, ALU.add)
                    tt("dve", Z[2], bset[0][:, :], Br, ALU.subtract)
                    stt(Z[3], bset[1][:, :], -1.0, Bi, ALU.mult, ALU.add)

                def st2(a):
                    Z = Zb[a % 2]
                    for half, (kk_, zz) in enumerate(((kb1s[a % 2], Z[0:2]), (kb2s[a % 2], Z[2:4]))):
                        kre, kim = kk_[:, 0:512], kk_[:, 512:1024]
                        tt("dve", mt[0], zz[0], kre, ALU.mult)
                        tt("dve", mt[1], zz[1], kim, ALU.mult)
                        tt("dve", mt[2], zz[0], kim, ALU.mult)
                        tt("dve", mt[3], zz[1], kre, ALU.mult)
                        tt("dve", Pr[2 * half], mt[0], mt[1], ALU.subtract)
                        tt("dve", Pr[2 * half + 1], mt[2], mt[3], ALU.add)
                    tt("dve", Pv[:, 0, 0, a, :], Pr[0], Pr[2], ALU.add)
                    tt("dve", Pv[:, 0, 1, a, :], Pr[1], Pr[3], ALU.subtract)
                    tt("dve", Pv[:, 1, 0, a, :], Pr[0], Pr[2], ALU.subtract)
                    tt("dve", Pv[:, 1, 1, a, :], Pr[1], Pr[3], ALU.add)

                for a in range(8):
                    mmA(a)
                    st1(a)
                    if a >= 1:
                        st2(a - 1)
                st2(7)

                def mmI(b):
                    sl = slabs[slab_i[0] % 3]; slab_i[0] += 1
                    slv = sl[:, 0:2048].rearrange("p (q a t) -> p q a t", q=2, a=8)
                    Sx.dma(slv, cd["inv"][b])
                    py = banks[b % 4]
                    e_ = b // 8
                    order = [(q_, a) for a in range(8) for q_ in range(2)]
                    for n, (q_, a) in enumerate(order):
                        Sx.mm(py[:, :], slv[:, q_, a, :], Pv[:, e_, q_, a, :], start=(n == 0), stop=(n == 15))

                def part1(b, oo=oo):
                    py = banks[b % 4]
                    if oo == 0:
                        tt("dve", zx[:, b, 0:512], py[:, :], zx[:, b, 512:1024], ALU.mult)
                        return
                    zb = z2b[b % 4]
                    z2t = (mt + Pr)[b % 8]
                    tt("dve", z2t, py[:, :], zx[:, b, 1024:1536], ALU.mult)
                    ssq = stats[:, 16 + b % 4:17 + b % 4]
                    rsq = stats[:, 20 + b % 4:21 + b % 4]
                    act(zb, z2t, AF.Square, accum_out=ssq)
                    act(rsq, ssq, AF.Ln, scale=1.0 / 512, bias=epsc[:, 0:1], extra=[epsc[:, 0:1]])
                    act(rsq, rsq, AF.Exp, scale=-0.5)
                    ts("dve", zb, z2t, rsq, None, ALU.mult, extra=[rsq])

                def part2(b):
                    zb = z2b[b % 4]
                    pbk = pb16(4 + b % 2)
                    for j in range(4):
                        Sx.tr(pbk[:, j * 128:(j + 1) * 128], zb[:, j * 128:(j + 1) * 128], ident[:])
                    cp("act", hyT[:, :, cols((b // 8) + 256 * (b % 8), 2, 128)], pbk[:, 0:512].rearrange("p (j c) -> p j c", j=4))

                for b_ in range(16):
                    mmI(b_)
                    part1(b_)
                    if oo == 1 and b_ >= 2:
                        part2(b_ - 2)
                if oo == 1:
                    part2(14)
                    part2(15)

            if dbg and "d_hy" in dbg_d and s == 0:
                for kc in range(4):
                    dtmp = A32(DT, 2048)
                    cp("dve", dtmp, hyT[:, kc, :])
                    Sx.dma(dbg_d["d_hy"][kc * 128:(kc + 1) * 128, :], dtmp)

            Sx.tag = f"s{s}p6"
            for g in range(4):
                def oproj(j, g=g):
                    t_ = 4 * g + j
                    tsl = slice(t_ * 128, (t_ + 1) * 128)
                    Sx.dma(xt[t_ % 2], x_d[s, tsl, :])
                    for hf in range(2):
                        pa = banks[2 * (j % 2) + hf]
                        for kc in range(8):
                            src = attnT if kc < 4 else hyT
                            Sx.mm(pa[:, :], src[:, kc % 4, tsl], wout[:, kc, hf * 512:(hf + 1) * 512], start=(kc == 0), stop=(kc == 7))

                def post(j, g=g):
                    t_ = 4 * g + j
                    xb = xt[t_ % 2]
                    for hf in range(2):
                        pa = banks[2 * (j % 2) + hf]
                        tt("dve", hbuf[:, j, hf * 512:(hf + 1) * 512], pa[:, :], xb[:, hf * 512:(hf + 1) * 512], ALU.add)
                    ssq = stats[:, 34 + t_ % 2:35 + t_ % 2]
                    act(sqj, hbuf[:, j, :], AF.Square, accum_out=ssq)
                    rf = stats[:, 36 + t_ % 2:37 + t_ % 2]
                    act(rf, ssq, AF.Ln, scale=1.0 / D, bias=epsc[:, 0:1], extra=[epsc[:, 0:1]])
                    act(rf, rf, AF.Exp, scale=-0.5)
                    xb16 = xn16[t_ % 2]
                    ts("dve", xb16, hbuf[:, j, :], rf, None, ALU.mult, extra=[rf])
                    for gg in range(2):
                        pbk = pb16(4 + gg)
                        for jj in range(4):
                            kc = gg * 4 + jj
                            Sx.tr(pbk[:, jj * 128:(jj + 1) * 128], xb16[:, kc * 128:(kc + 1) * 128], ident[:])
                        cp("act" if gg == 0 else "dve", hnT[:, gg * 4:(gg + 1) * 4, j * 128:(j + 1) * 128],
                           pbk[:, 0:512].rearrange("p (j c) -> p j c", j=4))

                oproj(0)
                oproj(1)
                post(0)
                oproj(2)
                post(1)
                oproj(3)
                post(2)
                post(3)
                for fch in range(32):
                    wb = wbuf[wi[0] % 4]; wi[0] += 1
                    wv = wb.rearrange("p (k c) -> p k c", k=8)
                    Sx.dma(wv, wup_s[fch])
                    bk = banks[6 + fch % 2]
                    for kc in range(8):
                        Sx.mm(bk[:, :], wv[:, kc, :], hnT[:, kc, :], start=(kc == 0), stop=(kc == 7))
                    rb = rl[fch % 2]
                    act(rb, bk[:, :], AF.Relu)
                    tt("pool", ffT[:, fch, :], rb, rb, ALU.mult)
                    if s + 1 < NS and fch % 8 == 2:
                        p1_front(s + 1, 4 * g + fch // 8)
                    if s + 1 < NS and fch % 8 == 6:
                        p1_back(s + 1, 4 * g + fch // 8)
                for fch in range(32):
                    db = dbuf[di[0] % 4]; di[0] += 1
                    Sx.dma(db, wdn_s[fch])
                    for j in range(4):
                        for hf in range(2):
                            Sx.mm(banks[2 * j + hf][:, :], ffT[:, fch, j * 128:(j + 1) * 128], db[:, hf * 512:(hf + 1) * 512],
                                  start=(fch == 0), stop=(fch == 31))
                for j in range(4):
                    t_ = 4 * g + j
                    yb = yt[t_ % 2]
                    for hf in range(2):
                        tt("dve", yb[:, hf * 512:(hf + 1) * 512], banks[2 * j + hf][:, :], hbuf[:, j, hf * 512:(hf + 1) * 512], ALU.add)
                    Sx.dma(y_d[s, t_ * 128:(t_ + 1) * 128, :], yb, q="pool")

        Sx.emit()
        import os
        if os.environ.get("KDUMP"):
            import json
            json.dump({e: [Sx.ops[i]["tag"] for i in Sx.streams[e]] for e in ENGS}, open(os.environ["KDUMP"], "w"))
    return nc


_PROG = {}


def _get_prog(NS):
    if NS not in _PROG:
        _PROG[NS] = build_program(NS)
    return _PROG[NS]


def kernel(**inputs):
    xp = np.asarray(inputs["x_prompt"], dtype=np.float32)
    xs = np.asarray(inputs["x_sample"], dtype=np.float32)
    xall = np.concatenate([xp, xs], axis=0)
    nseq = xall.shape[0]
    NS = nseq // NCORES
    consts = make_consts()
    params = layout_params(inputs)
    nc = _get_prog(NS)
    in_maps = []
    for c in range(NCORES):
        m = {"x": np.ascontiguousarray(xall[c * NS:(c + 1) * NS])}
        m.update(consts)
        m.update(params)
        in_maps.append(m)
    res = run_bass_kernel_spmd(nc, in_maps, core_ids=list(range(NCORES)))
    yall = np.concatenate([np.asarray(r["y"]) for r in res.results], axis=0)
    nb = xp.shape[0]
    return (np.ascontiguousarray(yall[:nb]).astype(np.float32), np.ascontiguousarray(yall[nb:]).astype(np.float32))
```
